# Optimizing a Trainium2 kernel written in Bass

```python
import jax, jax.numpy as jnp
from jax import lax
import numpy as np

D_MODEL = 2048
BATCH = 4
SEQ = 2048
DEPTH = 4
DEC_BATCH = 8
DEC_SEQ = 4096
PAST_LEN = 128

EPS = 1e-6
BLOCK = 128
D_A = 1024
G_A = 4
DG_A = D_A // G_A
HQ_B = 8
HKV_B = 2
HD_B = 128
D_B = HQ_B * HD_B
DKV_B = HKV_B * HD_B
WINDOW = 128
ROPE_THETA = 10000.0
H_C = 4
DK_C = 512
DV_C = 1024
HDK_C = DK_C // H_C
HDV_C = DV_C // H_C
GATE_RANK = 16
GATE_TEMP = 16.0
CHUNK_C = 64
IN_SPLITS = (D_A, D_A, D_A,
             D_B, DKV_B, DKV_B, D_B,
             DK_C, DK_C, DV_C, DV_C,
             GATE_RANK, GATE_RANK,
             D_MODEL, D_MODEL, D_MODEL)
N_IN = sum(IN_SPLITS)

kernel_name = "hybrid_gmlp_swa_gla_encoder"


def rmsnorm(x, g):
    xf = x.astype(jnp.float32)
    y = xf * lax.rsqrt(jnp.mean(xf * xf, axis=-1, keepdims=True) + EPS) * g.astype(jnp.float32)
    return y.astype(x.dtype)


def rope(x):
    S, hd = x.shape[1], x.shape[-1]
    half = hd // 2
    inv = ROPE_THETA ** (-jnp.arange(half, dtype=jnp.float32) * 2.0 / hd)
    ang = jnp.arange(S, dtype=jnp.float32)[:, None] * inv[None, :]
    cos = jnp.cos(ang)[None, :, None, :].astype(x.dtype)
    sin = jnp.sin(ang)[None, :, None, :].astype(x.dtype)
    x1, x2 = x[..., :half], x[..., half:]
    return jnp.concatenate([x1 * cos - x2 * sin, x2 * cos + x1 * sin], axis=-1)


def spatial_gating(u, v, ln_g, ln_b, ws, bs):
    B, S, _ = v.shape
    n = S // BLOCK
    vf = v.astype(jnp.float32)
    mu = jnp.mean(vf, axis=-1, keepdims=True)
    var = jnp.mean((vf - mu) ** 2, axis=-1, keepdims=True)
    vn = ((vf - mu) * lax.rsqrt(var + EPS) * ln_g + ln_b).astype(v.dtype)
    vn = vn.reshape(B, n, BLOCK, G_A, DG_A)
    f = jnp.einsum('gij,bcjgd->bcigd', ws, vn) + bs.T[None, None, :, :, None]
    return u * f.reshape(B, S, D_A)


def window_attention(q, k, v, sink):
    B, S = q.shape[0], q.shape[1]
    n = S // BLOCK
    W = BLOCK
    G = HQ_B // HKV_B
    q = rope(q)
    k = rope(k)
    qb = q.reshape(B, n, W, HKV_B, G, HD_B)

    def band(t):
        tp = jnp.pad(t, ((0, 0), (W, W), (0, 0), (0, 0))).reshape(B, n + 2, W, HKV_B, HD_B)
        return jnp.concatenate([tp[:, :-2], tp[:, 1:-1], tp[:, 2:]], axis=2)

    kb, vb = band(k), band(v)
    s = jnp.einsum('bnqkgd,bnjkd->bnkgqj', qb, kb).astype(jnp.float32) * (HD_B ** -0.5)
    i = jnp.arange(W)[:, None]
    j = jnp.arange(3 * W)[None, :]
    kpos = jnp.arange(n)[:, None, None] * W - W + j[None]
    mask = (jnp.abs(j - W - i) <= WINDOW)[None] & (kpos >= 0) & (kpos < S)
    s = jnp.where(mask[None, :, None, None], s, -jnp.inf)
    sk = sink.astype(jnp.float32).reshape(HKV_B, G)[None, None, :, :, None, None]
    m = jnp.maximum(jnp.max(s, axis=-1, keepdims=True), sk)
    p = jnp.exp(s - m)
    den = jnp.sum(p, axis=-1, keepdims=True) + jnp.exp(sk - m)
    o = jnp.einsum('bnkgqj,bnjkd->bnqkgd', (p / den).astype(v.dtype), vb)
    return o.reshape(B, S, D_B)


def gla_chunked(q, k, v, g, strict):
    B, S, H, dk = q.shape
    dv = v.shape[-1]
    C = CHUNK_C
    n = S // C
    q = q.reshape(B, n, C, H, dk)
    k = k.reshape(B, n, C, H, dk)
    v = v.reshape(B, n, C, H, dv)
    g = g.reshape(B, n, C, H, dk)
    b = jnp.cumsum(g, axis=2)
    b_last = b[:, :, -1:]
    qe = q * jnp.exp(b)
    ke = k * jnp.exp(-b)
    A = jnp.einsum('bncha,bnmha->bnhcm', qe, ke)
    idx = jnp.arange(C)
    tri = (idx[:, None] > idx[None, :]) if strict else (idx[:, None] >= idx[None, :])
    A = jnp.where(tri, A, 0.0)
    o = jnp.einsum('bnhcm,bnmhv->bnchv', A, v)
    kd = k * jnp.exp(b_last - b)
    dS = jnp.einsum('bnmha,bnmhv->nbhav', kd, v)
    decay = jnp.transpose(jnp.exp(b_last[:, :, 0]), (1, 0, 2, 3))

    def step(state, inp):
        dec, ds = inp
        return dec[..., None] * state + ds, state

    _, s_before = lax.scan(step, jnp.zeros((B, H, dk, dv), jnp.float32), (decay, dS))
    o = o + jnp.einsum('bncha,nbhav->bnchv', qe, s_before)
    return o.reshape(B, S, H, dv)


def bidirectional_gla(q, k, v, lr_f, lr_b, wf, bf, wb, bb, norm_g):
    B, S, _ = q.shape
    f32 = jnp.float32
    qh = q.astype(f32).reshape(B, S, H_C, HDK_C) * (HDK_C ** -0.5)
    kh = k.astype(f32).reshape(B, S, H_C, HDK_C)
    vh = v.astype(f32).reshape(B, S, H_C, HDV_C)
    gf = (jax.nn.log_sigmoid(lr_f.astype(f32) @ wf.astype(f32) + bf.astype(f32)) / GATE_TEMP).reshape(B, S, H_C, HDK_C)
    gb = (jax.nn.log_sigmoid(lr_b.astype(f32) @ wb.astype(f32) + bb.astype(f32)) / GATE_TEMP).reshape(B, S, H_C, HDK_C)
    fwd = gla_chunked(qh, kh, vh, gf, False)
    flip = lambda t: jnp.flip(t, axis=1)
    bwd = flip(gla_chunked(flip(qh), flip(kh), flip(vh), flip(gb), True))
    o = fwd + bwd
    o = o * lax.rsqrt(jnp.mean(o * o, axis=-1, keepdims=True) + EPS) * norm_g.astype(f32)
    return o.reshape(B, S, DV_C).astype(q.dtype)


def hybrid_layer(x, norm_g, w_in, a_ln_g, a_ln_b, a_ws, a_bs, b_sink,
                 c_wf, c_bf, c_wb, c_bb, c_norm_g, w_pa, w_pb, w_pc, w_out):
    B, S, _ = x.shape
    h = rmsnorm(x, norm_g)
    offsets = np.cumsum(IN_SPLITS)[:-1].tolist()
    (ua, va, za, qb, kb, vb, zb, qc, kc, vc, zc, lrf, lrb,
     gate_a, gate_b, gate_c) = [h @ w for w in jnp.split(w_in, offsets, axis=1)]
    ya = spatial_gating(ua, va, a_ln_g, a_ln_b, a_ws, a_bs) * jax.nn.silu(za)
    yb = window_attention(qb.reshape(B, S, HQ_B, HD_B), kb.reshape(B, S, HKV_B, HD_B),
                          vb.reshape(B, S, HKV_B, HD_B), b_sink) * jax.nn.silu(zb)
    yc = bidirectional_gla(qc, kc, vc, lrf, lrb, c_wf, c_bf, c_wb, c_bb, c_norm_g) * jax.nn.silu(zc)
    merged = (jax.nn.sigmoid(gate_a) * (ya @ w_pa)
              + jax.nn.sigmoid(gate_b) * (yb @ w_pb)
              + jax.nn.sigmoid(gate_c) * (yc @ w_pc))
    return x + merged @ w_out


def trunk(x, norm_g, w_in, a_ln_g, a_ln_b, a_ws, a_bs, b_sink,
          c_wf, c_bf, c_wb, c_bb, c_norm_g, w_pa, w_pb, w_pc, w_out, final_g):
    for l in range(DEPTH):
        x = hybrid_layer(x, norm_g[l], w_in[l], a_ln_g[l], a_ln_b[l], a_ws[l], a_bs[l], b_sink[l],
                         c_wf[l], c_bf[l], c_wb[l], c_bb[l], c_norm_g[l],
                         w_pa[l], w_pb[l], w_pc[l], w_out[l])
    return rmsnorm(x, final_g)


def setup_inputs(seed: int = 0) -> dict:
    key = jax.random.key(seed)
    ks = jax.random.split(key, 20)
    f32 = jnp.float32
    nrm = lambda k, shape, s: jax.random.normal(k, shape, f32) * s
    return {
        "x_prompt": nrm(ks[0], (BATCH, SEQ, D_MODEL), 1.0),
        "x_sample": nrm(ks[1], (DEC_BATCH, DEC_SEQ, D_MODEL), 1.0),
        "norm_g": 1.0 + nrm(ks[2], (DEPTH, D_MODEL), 0.05),
        "w_in": nrm(ks[3], (DEPTH, D_MODEL, N_IN), D_MODEL ** -0.5),
        "a_ln_g": 1.0 + nrm(ks[4], (DEPTH, D_A), 0.05),
        "a_ln_b": nrm(ks[5], (DEPTH, D_A), 0.02),
        "a_ws": nrm(ks[6], (DEPTH, G_A, BLOCK, BLOCK), BLOCK ** -0.5),
        "a_bs": 1.0 + nrm(ks[7], (DEPTH, G_A, BLOCK), 0.1),
        "b_sink": nrm(ks[8], (DEPTH, HQ_B), 0.5),
        "c_wf": nrm(ks[9], (DEPTH, GATE_RANK, DK_C), GATE_RANK ** -0.5),
        "c_bf": nrm(ks[10], (DEPTH, DK_C), 0.1),
        "c_wb": nrm(ks[11], (DEPTH, GATE_RANK, DK_C), GATE_RANK ** -0.5),
        "c_bb": nrm(ks[12], (DEPTH, DK_C), 0.1),
        "c_norm_g": 1.0 + nrm(ks[13], (DEPTH, HDV_C), 0.05),
        "w_pa": nrm(ks[14], (DEPTH, D_A, D_MODEL), D_A ** -0.5),
        "w_pb": nrm(ks[15], (DEPTH, D_B, D_MODEL), D_B ** -0.5),
        "w_pc": nrm(ks[16], (DEPTH, DV_C, D_MODEL), DV_C ** -0.5),
        "w_out": nrm(ks[17], (DEPTH, D_MODEL, D_MODEL), D_MODEL ** -0.5),
        "final_g": 1.0 + nrm(ks[18], (D_MODEL,), 0.05),
    }


def reference(x_prompt, x_sample, norm_g, w_in, a_ln_g, a_ln_b, a_ws, a_bs, b_sink,
              c_wf, c_bf, c_wb, c_bb, c_norm_g, w_pa, w_pb, w_pc, w_out, final_g):
    y_prompt = trunk(x_prompt, norm_g, w_in, a_ln_g, a_ln_b, a_ws, a_bs, b_sink,
                     c_wf, c_bf, c_wb, c_bb, c_norm_g, w_pa, w_pb, w_pc, w_out, final_g)
    y_sample = trunk(x_sample, norm_g, w_in, a_ln_g, a_ln_b, a_ws, a_bs, b_sink,
                     c_wf, c_bf, c_wb, c_bb, c_norm_g, w_pa, w_pb, w_pc, w_out, final_g)
    return (y_prompt, y_sample)
```

```python
import numpy as np
from contextlib import ExitStack
import concourse.bass as bass
import concourse.mybir as mybir
from concourse.bass_utils import run_bass_kernel_spmd

F32 = mybir.dt.float32
BF16 = mybir.dt.bfloat16
AF = mybir.ActivationFunctionType
ALU = mybir.AluOpType
AX = mybir.AxisListType

D = 2048
EPS = 1e-6
NEG = -30000.0
ENGS = ("pe", "act", "dve", "pool", "sp")

OFF = dict(ua=0, va=1024, za=2048, qb=3072, kb=4096, vb=4352, zb=4608, qc=5632, kc=6144,
           vc=6656, zc=7680, lrf=8704, lrb=8720, ga=8736, gb=10784, gc=12832)
NIMG = 39
(VA0, VA1, UA0, UA1, ZA0, ZA1, PA0, GA0, GA1, PA1, GA2, GA3,
 QB0, QB1, KVB, ZB0, ZB1, PB0, GB0, GB1, PB1, GB2, GB3,
 QC, KC, VC0, VC1, ZC0, ZC1, PC0, GC0, GC1, PC1, GC2, GC3,
 WO0, WO1, WO2, WO3) = range(NIMG)
MAIN_ORDER = [VA0, VA1, UA0, UA1, ZA0, ZA1, PA0, GA0, GA1, PA1, GA2, GA3,
              QB0, QB1, KVB, ZB0, ZB1, PB0, GB0, GB1, PB1, GB2, GB3,
              VC0, VC1, QC, KC, ZC0, ZC1, PC0, GC0, GC1, PC1, GC2, GC3,
              WO0, WO1, WO2, WO3]
PRE_ORDER = [KC, VC0, VC1]


class Buf:
    __slots__ = ("name", "writers", "readers")

    def __init__(self, name=""):
        self.name = name
        self.writers = {}
        self.readers = {}


class Prog:
    NDMA = 12

    def __init__(self, nc):
        self.nc = nc
        self.q = {e: [] for e in ENGS}
        self.esem = {e: nc.alloc_semaphore(name=f"s_{e}") for e in ENGS}
        self.ecnt = {e: 0 for e in ENGS}
        self.dsem = {e: [nc.alloc_semaphore(name=f"d_{e}{i}") for i in range(self.NDMA)]
                     for e in ("sp", "pool")}
        self.dcnt = {e: [0] * self.NDMA for e in self.dsem}
        self.dnext = {e: 0 for e in self.dsem}
        self.waited = {e: {} for e in ENGS}
        self.semobj = {}
        for e in ENGS:
            self.semobj[("e", e)] = self.esem[e]
        for e in self.dsem:
            for i, s in enumerate(self.dsem[e]):
                self.semobj[("d", e, i)] = s
        self.ninst = 0

    def _emit_waits(self, eng, evs):
        w = self.waited[eng]
        for key, val in evs.items():
            if eng == "pe" and key == ("e", "pe"):
                continue
            if w.get(key, 0) >= val:
                continue
            w[key] = val
            sem = self.semobj[key]
            self.q[eng].append(lambda e, sem=sem, val=val: e.wait_ge(sem, val))

    @staticmethod
    def _merge(d, key, val):
        if d.get(key, 0) < val:
            d[key] = val

    @staticmethod
    def _flat(bufs):
        out = []
        for b in bufs:
            if isinstance(b, (list, tuple)):
                out.extend(Prog._flat(b))
            else:
                out.append(b)
        return out

    def _deps(self, reads, writes):
        evs = {}
        for b in reads:
            for k, v in b.writers.items():
                self._merge(evs, k, v)
        for b in writes:
            for k, v in b.writers.items():
                self._merge(evs, k, v)
            for k, v in b.readers.items():
                self._merge(evs, k, v)
        return evs

    def _commit(self, ev, reads, writes):
        key, val = ev
        for b in writes:
            b.writers = {key: val}
            b.readers = {}
        for b in reads:
            self._merge(b.readers, key, val)

    def op(self, eng, fns, reads=(), writes=()):
        reads, writes = self._flat(reads), self._flat(writes)
        if callable(fns):
            fns = [fns]
        self._emit_waits(eng, self._deps(reads, writes))
        self.ecnt[eng] += 1
        val = self.ecnt[eng]
        sem = self.esem[eng]
        for f in fns[:-1]:
            self.q[eng].append(f)
        last = fns[-1]
        self.q[eng].append(lambda e, last=last, sem=sem: last(e).then_inc(sem, 1))
        self.ninst += len(fns)
        self._commit((("e", eng), val), reads, writes)

    def dma(self, qeng, out, in_, reads=(), writes=()):
        reads, writes = self._flat(reads), self._flat(writes)
        i = self.dnext[qeng]
        self.dnext[qeng] = (i + 1) % self.NDMA
        key = ("d", qeng, i)
        evs = self._deps(reads, writes)
        if self.dcnt[qeng][i] > 0:
            self._merge(evs, key, self.dcnt[qeng][i])
        self._emit_waits(qeng, evs)
        self.dcnt[qeng][i] += 16
        val = self.dcnt[qeng][i]
        sem = self.dsem[qeng][i]
        self.q[qeng].append(
            lambda e, out=out, in_=in_, sem=sem: e.dma_start(out=out, in_=in_).then_inc(sem, 16))
        self.ninst += 1
        self._commit((key, val), reads, writes)

    def finish(self):
        evs = {}
        for e in self.dsem:
            for i in range(self.NDMA):
                if self.dcnt[e][i] > 0:
                    self._merge(evs, ("d", e, i), self.dcnt[e][i])
        for e in ENGS:
            if e != "sp" and self.ecnt[e] > 0:
                self._merge(evs, ("e", e), self.ecnt[e])
        self._emit_waits("sp", evs)

    def run_block(self):
        q = self.q
        with self.nc.Block() as block:
            @block.tensor
            def _(e):
                for f in q["pe"]:
                    f(e)

            @block.scalar
            def _(e):
                for f in q["act"]:
                    f(e)

            @block.vector
            def _(e):
                for f in q["dve"]:
                    f(e)

            @block.gpsimd
            def _(e):
                for f in q["pool"]:
                    f(e)

            @block.sync
            def _(e):
                for f in q["sp"]:
                    f(e)


def skewed(n, stages, offsets):
    for t in range(n + max(offsets)):
        for fn, off in zip(stages, offsets):
            i = t - off
            if 0 <= i < n:
                fn(i)


def mm(out, lhsT, rhs, start=True, stop=True):
    return lambda e: e.matmul(out, lhsT, rhs, start=start, stop=stop)


DBG = []


def build(seqs, depth, smax, dbg=False):
    NTOK = sum(seqs)
    NT = NTOK // 128
    nc = bass.Bass("TRN2", target_bir_lowering=False)
    es = ExitStack()

    def din(name, shape, dt=F32):
        return nc.dram_tensor(name, list(shape), dt, kind="ExternalInput").ap()

    def dint(name, shape, dt):
        return nc.dram_tensor(name, list(shape), dt, kind="Internal").ap()

    x_in = din("x", [NTOK, D])
    wsl = din("wsl", [depth * NIMG, 128, 8192])
    wlr_d = din("wlr", [depth, 128, 512])
    wg_d = din("wg", [depth * 2, 17, 512])
    wsT_d = din("wsT", [depth, 128, 512])
    ng_d = din("norm_gT", [depth, 128, 16])
    lng_d = din("a_ln_g", [depth, 1024])
    lnb_d = din("a_ln_b", [depth, 1024])
    bs_d = din("a_bs", [depth, 512])
    sink_d = din("b_sink", [depth, 8])
    cng_d = din("c_norm_gT", [depth, 128, 2])
    fg_d = din("final_g", [1, D])
    cbf_d = din("cbf", [128, 2432])
    cf_d = din("cf", [128, 516])
    ropeC_d = din("ropeC", [128, smax])
    ropeS_d = din("ropeS", [128, smax])
    y_out = nc.dram_tensor("y", [NTOK, D], F32, kind="ExternalOutput").ap()
    wBl = [dint(f"wB{l}", [NIMG, 128, 8192], BF16) for l in range(depth)]
    wB = [wBl[i // NIMG][i % NIMG] for i in range(depth * NIMG)]
    xa = dint("xa", [NTOK, D], F32)
    xb = dint("xb", [NTOK, D], F32)
    st_d = dint("states", [NT, 128, 1024], BF16)

    P = Prog(nc)
    dbg_n = [0]

    def dump(label, ap, bufs, ncols):
        if not dbg:
            return
        o = nc.dram_tensor(f"dbg_{label}", [128, ncols], F32, kind="ExternalOutput").ap()
        P.dma("pool", o, ap, reads=bufs, writes=[Buf()])
        DBG.append(label)

    def sb(name, shape, dt):
        return es.enter_context(nc.sbuf_tensor("s_" + name, list(shape), dt))

    NRING = 3
    ring = [sb(f"ring{i}", [128, 8192], BF16) for i in range(NRING)]
    ring_b = [Buf(f"ring{i}") for i in range(NRING)]
    hT = sb("hT", [128, 16, 512], BF16); hT_b = [Buf(f"hT{i}") for i in range(4)]
    hTh = sb("hTh", [128, 16, 128], BF16); hTh_b = Buf("hTh")
    xst = [sb("xst0", [128, D], F32)] * 2
    xst_b = [Buf("xst0")] * 2
    hst = sb("hst", [128, D], BF16); hst_b = Buf("hst")
    stat = sb("stat", [128, 192], F32)
    stat_b = [Buf(f"stat{i}") for i in range(16)]
    merged = sb("merged", [128, 16, 512], BF16)
    mrg_b = [Buf(f"mrg{i}") for i in range(16)]
    yT = sb("yT", [128, 8, 512], BF16)
    yT_b = [Buf(f"yT{i}") for i in range(4)]
    cbf = sb("cbf", [128, 2432], BF16); cbf_b = Buf("cbf")
    ident = cbf[:, 0:128]
    masks = [cbf[:, 128 + 384 * i:128 + 384 * (i + 1)] for i in range(3)]
    gmask = [cbf[:, 1280:1792], cbf[:, 1792:2304]]
    ones_bf = cbf[:, 2304:2432]
    cf = sb("cf", [128, 516], F32); cf_b = Buf("cf")
    M1 = [cf[:, 0:128], cf[:, 128:256]]
    M2 = [cf[:, 256:384], cf[:, 384:512]]
    negc = cf[:, 512:513]
    one_c = cf[:, 513:514]
    eps_c = cf[:, 514:515]
    gT = sb("gT", [128, 16], F32); G_b = Buf("G")
    lnG = sb("lnG", [128, 1024], F32)
    lnB = sb("lnB", [128, 1024], F32)
    bs_bc = sb("bs_bc", [128, 512], F32)
    sink_bc = sb("sink_bc", [128, 8], F32)
    nsink = sb("nsink", [128, 8], F32)
    cng = sb("cng", [128, 2], F32)
    wlr = sb("wlr", [128, 16, 32], BF16)
    wg = [sb(f"wg{i}", [17, 512], F32) for i in range(2)]
    wsT = sb("wsT", [128, 4, 128], BF16)
    lp_b = Buf("layer_params")
    ropeC = sb("ropeC", [128, 640], F32)
    ropeS = sb("ropeS", [128, 640], F32)
    rope_b = Buf("rope")
    tA = sb("tA", [128, 1024], F32); tA_b = Buf("tA")
    tB = sb("tB", [128, 1024], F32); tB_b = Buf("tB"); tBh_b = [tB_b, Buf("tB1")]
    tC = [sb(f"tC{i}", [128, 1024], BF16) for i in range(2)]
    tC_b = [Buf(f"tC{i}") for i in range(2)]
    uT = sb("uT", [128, 8, 512], BF16); uT_b = Buf("uT")
    szt = [sb(f"szt{i}", [128, 512], BF16) for i in range(2)]; szt_b = [Buf(f"szt{i}") for i in range(2)]
    xch = [sb(f"xch{i}", [128, 512], F32) for i in range(2)]; xch_b = [Buf(f"xch{i}") for i in range(2)]
    krT = sb("krT", [128, 2, 768], BF16); krT_b = [Buf(f"krT{i}") for i in range(6)]
    vtB = sb("vtB", [128, 6, 256], BF16); vtB_b = [Buf(f"vtB{i}") for i in range(6)]
    pbuf = [sb(f"p{i}", [128, 384], BF16) for i in range(3)]; pbuf_b = [Buf(f"p{i}") for i in range(3)]
    pT = [sb(f"pT{i}", [128, 384], BF16) for i in range(2)]; pT_b = [Buf(f"pT{i}") for i in range(2)]
    Dg = [sb(f"Dg{i}", [128, 128], BF16) for i in range(3)]; Dg_b = [Buf(f"Dg{i}") for i in range(3)]
    vtC = uT[:, :, :].rearrange("p a b -> p (a b)").rearrange("p (j c) -> p j c", c=1024); vtC_b = [uT_b] * 4
    lrT = [sb(f"lrT{i}", [17, 512], F32) for i in range(2)]; lrT_b = [Buf(f"lrT{i}") for i in range(2)]
    gE = [sb("gE", [128, 512], BF16)] * 2; gE_b = [Buf("gE")] * 2
    gEi = [sb("gEi", [128, 512], BF16)] * 2; gEi_b = [Buf("gEi")] * 2
    gEk = [sb("gEk", [128, 512], BF16)] * 2; gEk_b = [Buf("gEk")] * 2
    qe = [sb(f"qe{i}", [128, 512], BF16) for i in range(4)]; qe_b = [Buf(f"qe{i}") for i in range(4)]
    ke = [sb("ke", [128, 512], BF16)] * 2; ke_b = [Buf("ke")] * 2
    kd = [sb(f"kd{i}", [128, 512], BF16) for i in range(2)]; kd_b = [Buf(f"kd{i}") for i in range(2)]
    Am = [sb(f"Am{i}", [128, 512], BF16) for i in range(4)]; Am_b = [Buf(f"Am{i}") for i in range(4)]
    sq = sb("sq", [128, 1024], BF16); sq_b = Buf("sq")
    rstd = sb("rstd", [128, 512], F32); rstd_b = Buf("rstd")
    dec = sb("dec", [128, 12], F32); dec_b = Buf("dec"); decm_b = [Buf("dec0"), Buf("dec1")]
    Sf = sb("Sf", [128, 1024], F32); Sf_b = Buf("Sf")
    Sf16 = sb("Sf16", [128, 1024], BF16); Sf16_b = Buf("Sf16")
    Sb = Sf; Sb_b = Sf_b
    Sb16 = [sb(f"Sb16_{i}", [128, 1024], BF16) for i in range(2)]
    Sb16_b = [Buf(f"Sb16_{i}") for i in range(2)]
    psum = [es.enter_context(nc.psum_tensor(f"ps{i}", [128, 512], F32)) for i in range(8)]
    ps_b = [Buf(f"ps{i}") for i in range(8)]
    pctr = [0]

    reserved = set()

    def bank(reserve=False):
        while pctr[0] % 8 in reserved:
            pctr[0] += 1
        i = pctr[0] % 8
        pctr[0] += 1
        if reserve:
            reserved.add(i)
        return psum[i], ps_b[i]

    sctr = [0]

    def statcol(n=1):
        assert n <= 12
        sl = sctr[0] % 16
        sctr[0] += 1
        return stat[:, sl * 12:sl * 12 + n], stat_b[sl]

    X = [x_in, xa, xb]
    X_b = [[Buf() for _ in range(NT)] for _ in range(3)]
    Y_b = [Buf() for _ in range(NT)]
    st_b = [Buf() for _ in range(NT)]
    wB_b = [Buf() for _ in range(depth * NIMG)]

    P.dma("pool", cbf[:, :], cbf_d, writes=[cbf_b])
    P.dma("sp", cf[:, :], cf_d, writes=[cf_b])
    conv_order = PRE_ORDER + [i for i in MAIN_ORDER if i not in PRE_ORDER]

    def convert(l, lo, hi):
        for i in conv_order[lo:hi]:
            P.dma("pool", wB[l * NIMG + i], wsl[l * NIMG + i], writes=[wB_b[l * NIMG + i]])

    convert(0, 0, NIMG)
    for t in lrT:
        P.op("pool", lambda e, t=t: e.memset(t[:, :], 1.0), writes=lrT_b)

    sched = []
    for l in range(depth):
        sched += [l * NIMG + i for i in PRE_ORDER]
        nst = NTOK // 512
        for _ in range(nst):
            sched += [l * NIMG + i for i in MAIN_ORDER]
    wstate = dict(issued=0, used=0)

    def w_prefetch(upto):
        while wstate["issued"] < min(upto, len(sched)):
            n = wstate["issued"]
            r = n % NRING
            P.dma("sp", ring[r][:, :], wB[sched[n]], reads=[wB_b[sched[n]]], writes=[ring_b[r]])
            wstate["issued"] += 1

    def w_get(img, ncol=512, hold=0):
        n = wstate["used"]
        assert sched[n] == img, (n, sched[n], img)
        w_prefetch(n + NRING - hold)
        wstate["used"] += 1
        r = n % NRING
        return ring[r][:, :].rearrange("p (k n) -> p k n", n=ncol), ring_b[r]

    def load_layer_params(l):
        w = [lp_b]
        P.dma("sp", gT[:, :], ng_d[l], writes=[G_b])
        P.dma("sp", lnG[:, :], lng_d[l, :].partition_broadcast(128), writes=w)
        P.dma("sp", lnB[:, :], lnb_d[l, :].partition_broadcast(128), writes=w)
        P.dma("sp", bs_bc[:, :], bs_d[l, :].partition_broadcast(128), writes=w)
        P.dma("sp", sink_bc[:, :], sink_d[l, :].partition_broadcast(128), writes=w)
        P.dma("sp", cng[:, :], cng_d[l], writes=w)
        for i in range(2):
            P.dma("sp", wg[i][:, :], wg_d[l * 2 + i], writes=w)
        P.dma("pool", wlr[:, :, :], wlr_d[l].rearrange("p (k c) -> p k c", c=32), writes=w)
        P.dma("pool", wsT[:, :, :], wsT_d[l].rearrange("p (g i) -> p g i", i=128), writes=w)
        P.op("pool", lambda e: e.tensor_scalar(nsink[:, :], sink_bc[:, :], -1.0, None, ALU.mult), reads=w, writes=w)

    def rstd_from(ssum, scale, n):
        r, rb = statcol(n)
        P.op("act", lambda e: e.activation(r, ssum[0], AF.Ln, bias=eps_c, scale=scale), reads=[ssum[1], cf_b], writes=[rb])
        P.op("act", lambda e: e.activation(r, r, AF.Exp, scale=-0.5), reads=[rb], writes=[rb])
        return r, rb

    def make_hT(l, tile, dst, dst_b, c0):
        p = tile % 2
        xt, xtb = xst[p], xst_b[p]
        src = X[0] if l == 0 else X[1 + (l - 1) % 2]
        srcb = X_b[0] if l == 0 else X_b[1 + (l - 1) % 2]
        P.dma("sp", xt[:, :], src[tile * 128:(tile + 1) * 128, :], reads=[srcb[tile]], writes=[xtb])
        ss, ssb = statcol()
        P.op("act", lambda e: e.activation(hst[:, :], xt[:, :], AF.Square, accum_out=ss), reads=[xtb], writes=[hst_b, ssb])
        r, rb = rstd_from((ss, ssb), 1.0 / D, 1)
        P.op("dve", lambda e: e.tensor_scalar(hst[:, :], xt[:, :], r, None, ALU.mult), reads=[xtb, rb], writes=[hst_b])
        for q in range(4):
            ps, pb = bank()
            P.op("pe", [mm(ps[:, kk * 128:(kk + 1) * 128], hst[:, (4 * q + kk) * 128:(4 * q + kk + 1) * 128], ident)
                        for kk in range(4)], reads=[hst_b, cbf_b], writes=[pb])
            for kk in range(4):
                k = 4 * q + kk
                d2 = dst[:, k, c0:c0 + 128]
                s2 = ps[:, kk * 128:(kk + 1) * 128]
                if kk % 2 == 0:
                    P.op("act", lambda e, d2=d2, s2=s2, k=k: e.activation(d2, s2, AF.Copy, scale=gT[:, k:k + 1]), reads=[pb, G_b], writes=[dst_b])
                else:
                    P.op("dve", lambda e, d2=d2, s2=s2, k=k: e.tensor_scalar(d2, s2, gT[:, k:k + 1], None, ALU.mult), reads=[pb, G_b], writes=[dst_b])

    def fm_block(wt, wb_, blk, K, rhs, rhs_b, ncols=512, rhs_c0=0):
        ps, pb = bank()
        P.op("pe", [mm(ps[:, 0:ncols], wt[:, k, blk * 128:(blk + 1) * 128], rhs[:, k, rhs_c0:rhs_c0 + ncols], k == 0, k == K - 1)
                    for k in range(K)], reads=[wb_, rhs_b], writes=[pb])
        return ps, pb

    def tm_group(lhs, lhs_b, c0, wt, wb_, wc0, ncols, K=16, reserve=False):
        ps, pb = bank(reserve=reserve)
        P.op("pe", [mm(ps[:, 0:ncols], lhs[:, k, c0:c0 + 128], wt[:, k, wc0:wc0 + ncols], k == 0, k == K - 1)
                    for k in range(K)], reads=[wb_, lhs_b], writes=[pb])
        return ps, pb

    def gate_L(d, lr_ap, lr_buf, Lout, Lout_b):
        ps, pb = bank()
        P.op("pe", mm(ps[:, :], lr_ap, wg[d][:, :]), reads=[lr_buf, lp_b], writes=[pb])
        P.op("act", lambda e: e.activation(Lout, ps[:, :], AF.Exp, scale=-1.0), reads=[pb], writes=[Lout_b])
        P.op("act", lambda e: e.activation(Lout, Lout, AF.Ln, bias=one_c, scale=1.0), reads=[Lout_b, cf_b], writes=[Lout_b])

    seq_tiles = []
    t0 = 0
    for s in seqs:
        seq_tiles.append((t0, s // 128))
        t0 += s // 128

    for l in range(depth):
        load_layer_params(l)
        cur = X[0] if l == 0 else X[1 + (l - 1) % 2]
        nxt_i = 1 + l % 2
        last = (l == depth - 1)
        wkc, wkc_b = w_get(l * NIMG + KC)
        wv0, wv0_b = w_get(l * NIMG + VC0, hold=1)
        wv1, wv1_b = w_get(l * NIMG + VC1, hold=2)
        for (tb, ntile) in seq_tiles:
            P.op("pool", lambda e: e.memset(Sb[:, :], 0.0), writes=[Sb_b])
            cur16 = 0
            P.op("pool", lambda e, c=cur16: e.memset(Sb16[c][:, :], 0.0), writes=[Sb16_b[cur16]])
            order = list(reversed(range(ntile)))
            PS = [dict() for _ in range(ntile)]
            c16 = [0]

            def PP0(i):
                make_hT(l, tb + order[i], hT, hT_b[i % 4], (i % 4) * 128)

            def PP1(i):
                par = i % 2
                hb_, c0 = hT_b[i % 4], (i % 4) * 128
                PS[i]["k"] = tm_group(hT, hb_, c0, wkc, wkc_b, 0, 512, reserve=True)
                vt = tC[par]; vt_b = tC_b[par]
                for half, (wv, wvb) in enumerate(((wv0, wv0_b), (wv1, wv1_b))):
                    ps, pb = tm_group(hT, hb_, c0, wv, wvb, 0, 512)
                    P.op("act", lambda e, ps=ps, half=half, vt=vt: e.copy(vt[:, half * 512:(half + 1) * 512], ps[:, :]), reads=[pb], writes=[vt_b])
                ps, pb = bank()
                P.op("pe", [mm(ps[0:16, 0:128], wlr[:, k, 16:32], hT[:, k, c0:c0 + 128], k == 0, k == 15) for k in range(16)],
                     reads=[lp_b, hb_], writes=[pb])
                P.op("dve", lambda e, ps=ps: e.tensor_copy(lrT[par][0:16, 0:128], ps[0:16, 0:128]), reads=[pb], writes=[lrT_b[par]])
                gate_L(1, lrT[par][:, 0:128], lrT_b[par], tB[:, par * 512:(par + 1) * 512], tBh_b[par])

            def PP2(i):
                par = i % 2
                tile = tb + order[i]
                Lb = tB[:, par * 512:(par + 1) * 512]
                vt = tC[par]; vt_b = tC_b[par]
                ps, pb = bank()
                P.op("pe", mm(ps[:, :], M2[1], Lb), reads=[cf_b, tBh_b[par]], writes=[pb])
                P.op("act", lambda e, ps=ps: e.activation(gEk[1][:, :], ps[:, :], AF.Exp), reads=[pb], writes=[gEk_b[1]])
                ps, pb = bank()
                P.op("pe", [mm(ps[:, h:h + 1], Lb[:, h * 128:(h + 1) * 128], negc) for h in range(4)], reads=[cf_b, tBh_b[par]], writes=[pb])
                P.op("act", lambda e, ps=ps: e.activation(dec[:, 8:12], ps[:, 0:4], AF.Exp), reads=[pb], writes=[dec_b])
                psk, pbk = PS[i]["k"]
                P.op("dve", lambda e: e.tensor_tensor(kd[1][:, :], psk[:, :], gEk[1][:, :], ALU.mult), reads=[pbk, gEk_b[1]], writes=[kd_b[1]])
                reserved.discard(ps_b.index(pbk))
                cur16 = c16[0]
                P.dma("sp", st_d[tile], Sb16[cur16][:, :], reads=[Sb16_b[cur16]], writes=[st_b[tile]])
                for hp in range(2):
                    ps, pb = bank()
                    P.op("pe", [mm(ps[:, hh * 256:(hh + 1) * 256], kd[1][:, (2 * hp + hh) * 128:(2 * hp + hh + 1) * 128],
                                   vt[:, (2 * hp + hh) * 256:(2 * hp + hh + 1) * 256]) for hh in range(2)],
                         reads=[kd_b[1], vt_b], writes=[pb])
                    for hh in range(2):
                        h = 2 * hp + hh
                        P.op("dve", lambda e, ps=ps, h=h, hh=hh: e.scalar_tensor_tensor(
                            Sb[:, h * 256:(h + 1) * 256], Sb[:, h * 256:(h + 1) * 256], dec[:, 8 + h:9 + h],
                            ps[:, hh * 256:(hh + 1) * 256], ALU.mult, ALU.add), reads=[pb, dec_b, Sb_b], writes=[Sb_b])
                c16[0] ^= 1
                nx = c16[0]
                P.op("pool", lambda e: e.tensor_copy(Sb16[nx][:, :], Sb[:, :]), reads=[Sb_b], writes=[Sb16_b[nx]])

            skewed(ntile, [PP0, PP1, PP2], [0, 1, 2])

        stc = [0]
        pre0 = [False]
        for (tb, ntile) in seq_tiles:
            nst = ntile // 4
            P.op("pool", lambda e: e.memset(Sf[:, :], 0.0), writes=[Sf_b])
            P.op("pool", lambda e: e.memset(Sf16[:, :], 0.0), writes=[Sf16_b])
            for st in range(nst):
                t_first = tb + st * 4
                pos0 = st * 512
                has_next = st < nst - 1
                has_prev = st > 0
                if l + 1 < depth:
                    per = -(-NIMG // (NTOK // 512))
                    convert(l + 1, stc[0] * per, (stc[0] + 1) * per)
                    stc[0] += 1
                def stage0_items(st_, l=l, tb=tb, nst=nst):
                    tf = tb + st_ * 4
                    hn = st_ < nst - 1
                    items = [(lambda j=j: make_hT(l, tf + j, hT, hT_b[j], j * 128)) for j in range(4)]
                    if hn:
                        items.append(lambda: make_hT(l, tf + 4, hTh, hTh_b, 0))

                    def ropes():
                        nr = 640 if hn else 512
                        P.dma("sp", ropeC[:, 0:nr], ropeC_d[:, st_ * 512:st_ * 512 + nr], writes=[rope_b])
                        P.dma("sp", ropeS[:, 0:nr], ropeS_d[:, st_ * 512:st_ * 512 + nr], writes=[rope_b])
                    items.append(ropes)
                    return items

                if not pre0[0]:
                    for it in stage0_items(st):
                        it()
                pre0[0] = False
                D0 = (l == 0 and st == 0 and tb == 0)
                if D0:
                    dump("hT", hT[:, :, :].rearrange("p a b -> p (a b)"), hT_b, 8192)
                wva = [w_get(l * NIMG + VA0), w_get(l * NIMG + VA1, hold=1)]
                S1 = [None] * 4

                def A1(j):
                    pss = [tm_group(hT, hT_b, j * 128, wva[h][0], wva[h][1], 0, 512) for h in range(2)]
                    bst, bstb = statcol(12)
                    for h in range(2):
                        P.op("dve", lambda e, h=h, ps=pss[h][0], bst=bst: e.bn_stats(bst[:, h * 6:(h + 1) * 6], ps[:, :]),
                             reads=[pss[h][1]], writes=[bstb])
                    mv, mvb = statcol(2)
                    P.op("dve", lambda e, bst=bst, mv=mv: e.bn_aggr(mv, bst), reads=[bstb], writes=[mvb])
                    r, rb = rstd_from((mv[:, 1:2], mvb), 1.0, 1)
                    nm, nmb = statcol()
                    P.op("dve", lambda e, nm=nm, mv=mv, r=r: e.scalar_tensor_tensor(nm, mv[:, 0:1], -1.0, r, ALU.mult, ALU.mult),
                         reads=[mvb, rb], writes=[nmb])
                    for h in range(2):
                        P.op("act", lambda e, h=h, ps=pss[h][0], r=r, nm=nm: e.activation(
                            tA[:, h * 512:(h + 1) * 512], ps[:, :], AF.Identity, bias=nm, scale=r),
                            reads=[pss[h][1], rb, nmb], writes=[tA_b])
                    P.op("dve", lambda e: e.tensor_tensor(tA[:, :], tA[:, :], lnG[:, :], ALU.mult), reads=[tA_b, lp_b], writes=[tA_b])
                    vn = tC[j % 2]; vnb = tC_b[j % 2]
                    P.op("dve", lambda e, vn=vn: e.tensor_tensor(vn[:, :], tA[:, :], lnB[:, :], ALU.add), reads=[tA_b, lp_b], writes=[vnb])
                    S1[j] = (vn, vnb)

                def A2(j):
                    vn, vnb = S1[j]
                    for hb in range(2):
                        ps, pb = bank()
                        P.op("pe", [mm(ps[:, c * 128:(c + 1) * 128], vn[:, (4 * hb + c) * 128:(4 * hb + c + 1) * 128],
                                       wsT[:, (4 * hb + c) // 2, :]) for c in range(4)], reads=[vnb, lp_b], writes=[pb])
                        for gg in range(2):
                            g = hb * 2 + gg
                            P.op("dve", lambda e, ps=ps, gg=gg, g=g, hb=hb: e.tensor_tensor(
                                yT[:, 4 * hb + 2 * gg:4 * hb + 2 * gg + 2, j * 128:(j + 1) * 128],
                                ps[:, gg * 256:(gg + 1) * 256].rearrange("p (a b) -> p a b", b=128),
                                bs_bc[:, g * 128:(g + 1) * 128].unsqueeze(1).to_broadcast([128, 2, 128]), ALU.add),
                                reads=[pb, lp_b], writes=[yT_b[j]])

                skewed(4, [A1, A2], [0, 1])
                for half, img in enumerate((UA0, UA1)):
                    wt, wb_ = w_get(l * NIMG + img)
                    for bl in range(4):
                        ps, pb = fm_block(wt, wb_, bl, 16, hT, hT_b)
                        P.op("act", lambda e, ps=ps, b=half * 4 + bl: e.copy(uT[:, b, :], ps[:, :]), reads=[pb], writes=[uT_b])
                zc = 0
                for half, img in enumerate((ZA0, ZA1)):
                    wt, wb_ = w_get(l * NIMG + img)
                    for bl in range(4):
                        b = half * 4 + bl
                        ps, pb = fm_block(wt, wb_, bl, 16, hT, hT_b)
                        zz = zc % 2; zc += 1
                        P.op("act", lambda e, ps=ps, zz=zz: e.activation(szt[zz][:, :], ps[:, :], AF.Silu), reads=[pb], writes=[szt_b[zz]])
                        P.op("dve", lambda e, b=b, zz=zz: e.tensor_tensor(uT[:, b, :], uT[:, b, :], szt[zz][:, :], ALU.mult),
                             reads=[uT_b, szt_b[zz]], writes=[uT_b])
                        P.op("dve", lambda e, b=b: e.tensor_tensor(yT[:, b, :], yT[:, b, :], uT[:, b, :], ALU.mult),
                             reads=[uT_b] + yT_b, writes=yT_b)
                if D0:
                    dump("ya", yT[:, :, :].rearrange("p a b -> p (a b)"), yT_b, 4096)
                merge_term(P, l, 0, w_get, fm_block, yT, yT_b, hT, hT_b, merged, mrg_b, bank, (PA0, GA0, GA1, PA1, GA2, GA3), tB, tBh_b)

                if has_prev:
                    P.op("pool", lambda e: e.tensor_copy(krT[:, :, 0:128], krT[:, :, 512:640]), reads=[krT_b[4]], writes=[krT_b[0]])
                    P.op("pool", lambda e: e.tensor_copy(vtB[:, 0, :], vtB[:, 4, :]), reads=[vtB_b[4]], writes=[vtB_b[0]])
                else:
                    P.op("pool", lambda e: e.memset(krT[:, :, 0:128], 0.0), writes=[krT_b[0]])
                    P.op("pool", lambda e: e.memset(vtB[:, 0, :], 0.0), writes=[vtB_b[0]])
                if not has_next:
                    P.op("pool", lambda e: e.memset(krT[:, :, 640:768], 0.0), writes=[krT_b[5]])
                    P.op("pool", lambda e: e.memset(vtB[:, 5, :], 0.0), writes=[vtB_b[5]])

                rctr = [0]

                def rope_evac(ps, pb, ncols, rc0, dst, dst_bufs):
                    if rctr[0] % 2 == 0:
                        T, Tb = tA, [tA_b]
                    else:
                        T, Tb = tB, tBh_b
                    rctr[0] += 1
                    P.op("dve", lambda e: e.tensor_tensor(T[:, 0:ncols], ps[:, 0:ncols], ropeC[:, rc0:rc0 + ncols], ALU.mult),
                         reads=[pb, rope_b], writes=Tb)
                    P.op("dve", lambda e: e.tensor_tensor(T[0:64, 512:512 + ncols], ps[64:128, 0:ncols], ropeS[64:128, rc0:rc0 + ncols], ALU.mult),
                         reads=[pb, rope_b], writes=Tb)
                    P.op("dve", lambda e: e.tensor_tensor(T[64:128, 512:512 + ncols], ps[0:64, 0:ncols], ropeS[0:64, rc0:rc0 + ncols], ALU.mult),
                         reads=[pb, rope_b], writes=Tb)
                    P.op("dve", lambda e: e.tensor_tensor(dst, T[:, 0:ncols], T[:, 512:512 + ncols], ALU.add), reads=Tb, writes=dst_bufs)

                for half, img in enumerate((QB0, QB1)):
                    wt, wb_ = w_get(l * NIMG + img)
                    for bl in range(4):
                        ps, pb = fm_block(wt, wb_, bl, 16, hT, hT_b)
                        rope_evac(ps, pb, 512, 0, uT[:, half * 4 + bl, :], [uT_b])
                wt, wb_ = w_get(l * NIMG + KVB)
                for kv in range(2):
                    ps, pb = fm_block(wt, wb_, kv, 16, hT, hT_b)
                    rope_evac(ps, pb, 512, 0, krT[:, kv, 128:640], krT_b[1:5])
                    if has_next:
                        ps, pb = fm_block(wt, wb_, kv, 16, hTh, hTh_b, ncols=128)
                        rope_evac(ps, pb, 128, 512, krT[:, kv, 640:768], [krT_b[5]])
                for j in range(5 if has_next else 4):
                    src, srcb, c0 = (hT, hT_b, j * 128) if j < 4 else (hTh, hTh_b, 0)
                    ps, pb = tm_group(src, srcb, c0, wt, wb_, 256, 256)
                    P.op("act", lambda e, ps=ps, j=j: e.copy(vtB[:, j + 1, :], ps[:, 0:256]), reads=[pb], writes=[vtB_b[j + 1]])
                scale = 128.0 ** -0.5
                S = [dict() for _ in range(32)]

                def stA(i):
                    j, h = divmod(i, 8)
                    kv = h // 4
                    tl = st * 4 + j
                    mk = masks[1] if tl == 0 else (masks[2] if tl == ntile - 1 else masks[0])
                    ps, pb = bank()
                    S[i]["ps"], S[i]["pb"] = ps, pb
                    P.op("pe", [mm(ps[:, 0:384], ident, mk, True, False),
                                mm(ps[:, 0:384], uT[:, h, j * 128:(j + 1) * 128], krT[:, kv, j * 128:j * 128 + 384], False, True)],
                         reads=[cbf_b, uT_b] + krT_b[j:j + 3], writes=[pb])

                def stB(i):
                    j, h = divmod(i, 8)
                    ps, pb = S[i]["ps"], S[i]["pb"]
                    pp = i % 3
                    mx, mxb = statcol(4)
                    P.op("dve", lambda e: e.reduce_max(mx[:, 0:1], ps[:, 0:384], AX.X), reads=[pb], writes=[mxb])
                    P.op("dve", lambda e: e.tensor_scalar(mx[:, 1:2], mx[:, 0:1], -scale, nsink[:, h:h + 1], ALU.mult, ALU.min),
                         reads=[mxb, lp_b], writes=[mxb])
                    P.op("act", lambda e: e.activation(pbuf[pp][:, :], ps[:, 0:384], AF.Exp, bias=mx[:, 1:2], scale=scale,
                                                       accum_out=mx[:, 2:3]), reads=[pb, mxb], writes=[pbuf_b[pp], mxb])
                    P.op("act", lambda e: e.activation(mx[:, 3:4], mx[:, 1:2], AF.Exp, bias=sink_bc[:, h:h + 1], scale=1.0),
                         reads=[mxb, lp_b], writes=[mxb])
                    P.op("dve", lambda e: e.tensor_tensor(mx[:, 2:3], mx[:, 2:3], mx[:, 3:4], ALU.add), reads=[mxb], writes=[mxb])
                    P.op("dve", lambda e: e.reciprocal(mx[:, 0:1], mx[:, 2:3]), reads=[mxb], writes=[mxb])
                    P.op("act", lambda e: e.activation(Dg[pp][:, :], ident, AF.Copy, scale=mx[:, 0:1]),
                         reads=[mxb, cbf_b], writes=[Dg_b[pp]])

                def stC(i):
                    pp = i % 3
                    ps2, pb2 = bank()
                    S[i]["ps2"], S[i]["pb2"] = ps2, pb2
                    P.op("pe", [mm(ps2[:, c * 128:(c + 1) * 128], pbuf[pp][:, c * 128:(c + 1) * 128], Dg[pp][:, :]) for c in range(3)],
                         reads=[pbuf_b[pp], Dg_b[pp]], writes=[pb2])

                def stD(i):
                    ps2, pb2 = S[i]["ps2"], S[i]["pb2"]
                    P.op("act", lambda e: e.copy(pT[i % 2][:, :], ps2[:, 0:384]), reads=[pb2], writes=[pT_b[i % 2]])

                def stE(i):
                    j, h = divmod(i, 8)
                    kv = h // 4
                    ob, obb = bank()
                    P.op("pe", [mm(ob[:, 0:128], vtB[:, j + c, kv * 128:(kv + 1) * 128],
                                   pT[i % 2][:, c * 128:(c + 1) * 128], c == 0, c == 2) for c in range(3)],
                         reads=[pT_b[i % 2]] + vtB_b[j:j + 3], writes=[obb])
                    P.op("dve", lambda e: e.tensor_copy(yT[:, h, j * 128:(j + 1) * 128], ob[:, 0:128]), reads=[obb], writes=[yT_b[j]])

                skewed(32, [stA, stB, stC, stD, stE], [0, 0, 2, 2, 3])
                zc = 0
                for half, img in enumerate((ZB0, ZB1)):
                    wt, wb_ = w_get(l * NIMG + img)
                    for bl in range(4):
                        b = half * 4 + bl
                        ps, pb = fm_block(wt, wb_, bl, 16, hT, hT_b)
                        zz = zc % 2; zc += 1
                        P.op("act", lambda e, ps=ps, zz=zz: e.activation(szt[zz][:, :], ps[:, :], AF.Silu), reads=[pb], writes=[szt_b[zz]])
                        P.op("dve", lambda e, b=b, zz=zz: e.tensor_tensor(yT[:, b, :], yT[:, b, :], szt[zz][:, :], ALU.mult),
                             reads=yT_b + [szt_b[zz]], writes=yT_b)
                if D0:
                    dump("m0", merged[:, :, :].rearrange("p a b -> p (a b)"), mrg_b, 8192)
                    dump("yb", yT[:, :, :].rearrange("p a b -> p (a b)"), yT_b, 4096)
                merge_term(P, l, 1, w_get, fm_block, yT, yT_b, hT, hT_b, merged, mrg_b, bank, (PB0, GB0, GB1, PB1, GB2, GB3), tB, tBh_b)

                wv = [w_get(l * NIMG + VC0), w_get(l * NIMG + VC1, hold=1)]
                for j in range(4):
                    for half in range(2):
                        ps, pb = tm_group(hT, hT_b, j * 128, wv[half][0], wv[half][1], 0, 512)
                        P.op("act", lambda e, ps=ps, j=j, half=half: e.copy(vtC[:, j, half * 512:(half + 1) * 512], ps[:, :]),
                             reads=[pb], writes=[uT_b])
                for d in range(2):
                    ps, pb = bank()
                    P.op("pe", [mm(ps[0:16, :], wlr[:, k, d * 16:(d + 1) * 16], hT[:, k, :], k == 0, k == 15) for k in range(16)],
                         reads=[lp_b, hT_b], writes=[pb])
                    P.op("dve", lambda e, ps=ps, d=d: e.tensor_copy(lrT[d][0:16, :], ps[0:16, :]), reads=[pb], writes=[lrT_b[d]])
                wq, wq_b = w_get(l * NIMG + QC)
                wk, wk_b = w_get(l * NIMG + KC, hold=1)
                qscale = 128.0 ** -0.5
                def G1(j):
                    tile = t_first + j
                    par = j % 2
                    cs = slice(j * 128, (j + 1) * 128)
                    P.dma("sp", Sb16[par][:, :], st_d[tile], reads=[st_b[tile]], writes=[Sb16_b[par]])
                    for d in range(2):
                        gate_L(d, lrT[d][:, cs], lrT_b[d], tB[:, d * 512:(d + 1) * 512], tBh_b[d])
                    psq, pbq = bank()
                    P.op("pe", [mm(psq[:, h * 128:(h + 1) * 128], wq[:, k, h * 128:(h + 1) * 128], hT[:, k, cs], k == 0, k == 15)
                                for h in range(4) for k in range(16)], reads=[wq_b, hT_b], writes=[pbq])
                    psk, pbk = bank()
                    P.op("pe", [mm(psk[:, h * 128:(h + 1) * 128], wk[:, k, h * 128:(h + 1) * 128], hT[:, k, cs], k == 0, k == 15)
                                for h in range(4) for k in range(16)], reads=[wk_b, hT_b], writes=[pbk])
                    for d in range(2):
                        L = tB[:, d * 512:(d + 1) * 512]
                        qd, qdb = qe[2 * par + d], qe_b[2 * par + d]
                        ad, adb = Am[2 * par + d], Am_b[2 * par + d]
                        ps, pb = bank()
                        P.op("pe", [mm(ps[:, h * 128:(h + 1) * 128], L[:, h * 128:(h + 1) * 128], M1[d]) for h in range(4)],
                             reads=[tBh_b[d], cf_b], writes=[pb])
                        P.op("act", lambda e, ps=ps, d=d: e.activation(gE[d][:, :], ps[:, :], AF.Exp), reads=[pb], writes=[gE_b[d]])
                        P.op("act", lambda e, ps=ps, d=d: e.activation(gEi[d][:, :], ps[:, :], AF.Exp, scale=-1.0), reads=[pb], writes=[gEi_b[d]])
                        if d == 0:
                            P.op("act", lambda e, ps=ps: e.activation(dec[:, 4 * par:4 * par + 4], ps[:, :].rearrange("p (h c) -> p h c", c=128)[:, :, 127], AF.Exp),
                                 reads=[pb], writes=[decm_b[par]])
                        P.op("dve", lambda e, d=d, qd=qd: e.scalar_tensor_tensor(qd[:, :], psq[:, :], qscale, gE[d][:, :], ALU.mult, ALU.mult),
                             reads=[pbq, gE_b[d]], writes=[qdb])
                        P.op("dve", lambda e, d=d: e.tensor_tensor(ke[d][:, :], psk[:, :], gEi[d][:, :], ALU.mult),
                             reads=[pbk, gEi_b[d]], writes=[ke_b[d]])
                        ps, pb = bank()
                        P.op("pe", [mm(ps[:, h * 128:(h + 1) * 128], ke[d][:, h * 128:(h + 1) * 128], qd[:, h * 128:(h + 1) * 128]) for h in range(4)],
                             reads=[ke_b[d], qdb], writes=[pb])
                        P.op("dve", lambda e, ps=ps, d=d, ad=ad: e.tensor_tensor(ad[:, :], ps[:, :], gmask[d], ALU.mult), reads=[pb, cbf_b], writes=[adb])
                    ps, pb = bank()
                    P.op("pe", mm(ps[:, :], M2[0], tB[:, 0:512]), reads=[tBh_b[0], cf_b], writes=[pb])
                    P.op("act", lambda e, ps=ps: e.activation(gEk[0][:, :], ps[:, :], AF.Exp), reads=[pb], writes=[gEk_b[0]])
                    ps, pb = tm_group(hT, hT_b, j * 128, wk, wk_b, 0, 512)
                    P.op("dve", lambda e, ps=ps: e.tensor_tensor(kd[par][:, :], ps[:, :], gEk[0][:, :], ALU.mult), reads=[pb, gEk_b[0]], writes=[kd_b[par]])

                def G2(j):
                    par = j % 2
                    obanks = [bank(), bank()]
                    for hp in range(2):
                        ob, obb = obanks[hp]
                        fns = []
                        for hh in range(2):
                            h = 2 * hp + hh
                            for vb in range(2):
                                oc = ob[:, (hh * 2 + vb) * 128:(hh * 2 + vb + 1) * 128]
                                vsl = slice(h * 256 + vb * 128, h * 256 + (vb + 1) * 128)
                                hs = slice(h * 128, (h + 1) * 128)
                                fns.append(mm(oc, vtC[:, j, vsl], Am[2 * par][:, hs], True, False))
                                fns.append(mm(oc, vtC[:, j, vsl], Am[2 * par + 1][:, hs], False, False))
                                fns.append(mm(oc, Sf16[:, vsl], qe[2 * par][:, hs], False, False))
                                fns.append(mm(oc, Sb16[par][:, vsl], qe[2 * par + 1][:, hs], False, True))
                        P.op("pe", fns, reads=[uT_b, Am_b[2 * par], Am_b[2 * par + 1], Sf16_b, Sb16_b[par], qe_b[2 * par], qe_b[2 * par + 1]], writes=[obb])
                    for hp in range(2):
                        ps, pb = bank()
                        P.op("pe", [mm(ps[:, hh * 256:(hh + 1) * 256], kd[par][:, (2 * hp + hh) * 128:(2 * hp + hh + 1) * 128],
                                       vtC[:, j, (2 * hp + hh) * 256:(2 * hp + hh + 1) * 256]) for hh in range(2)],
                             reads=[kd_b[par], uT_b], writes=[pb])
                        for hh in range(2):
                            h = 2 * hp + hh
                            P.op("dve", lambda e, ps=ps, h=h, hh=hh: e.scalar_tensor_tensor(
                                Sf[:, h * 256:(h + 1) * 256], Sf[:, h * 256:(h + 1) * 256], dec[:, 4 * par + h:4 * par + h + 1],
                                ps[:, hh * 256:(hh + 1) * 256], ALU.mult, ALU.add), reads=[pb, decm_b[par], Sf_b], writes=[Sf_b])
                    P.op("pool", lambda e: e.tensor_copy(Sf16[:, :], Sf[:, :]), reads=[Sf_b], writes=[Sf16_b])
                    for hp in range(2):
                        ob, obb = obanks[hp]
                        P.op("act", lambda e, ob=ob, hp=hp: e.activation(sq[:, hp * 512:(hp + 1) * 512], ob[:, :], AF.Square), reads=[obb], writes=[sq_b])
                    ps, pb = bank()
                    P.op("pe", [mm(ps[:, h * 128:(h + 1) * 128], ones_bf, sq[:, (h * 2 + vb) * 128:(h * 2 + vb + 1) * 128], vb == 0, vb == 1)
                                for h in range(4) for vb in range(2)], reads=[sq_b, cbf_b], writes=[pb])
                    P.op("act", lambda e, ps=ps: e.activation(rstd[:, :], ps[:, :], AF.Ln, bias=eps_c, scale=1.0 / 256), reads=[pb, cf_b], writes=[rstd_b])
                    P.op("act", lambda e: e.activation(rstd[:, :], rstd[:, :], AF.Exp, scale=-0.5), reads=[rstd_b], writes=[rstd_b])
                    for hp in range(2):
                        ob, obb = obanks[hp]
                        for vb in range(2):
                            P.op("dve", lambda e, ob=ob, hp=hp, vb=vb, j=j: e.scalar_tensor_tensor(
                                yT[:, 4 * hp:4 * hp + 4, j * 128:(j + 1) * 128].rearrange("p (h v) c -> p h v c", v=2)[:, :, vb, :],
                                ob[:, :].rearrange("p (h v c) -> p h v c", v=2, c=128)[:, :, vb, :], cng[:, vb:vb + 1],
                                rstd[:, hp * 256:(hp + 1) * 256].rearrange("p (h c) -> p h c", c=128), ALU.mult, ALU.mult),
                                reads=[obb, rstd_b, lp_b], writes=[yT_b[j]])

                skewed(4, [G1, G2], [0, 1])
                zc = 0
                for half, img in enumerate((ZC0, ZC1)):
                    wt, wb_ = w_get(l * NIMG + img)
                    for bl in range(4):
                        b = half * 4 + bl
                        ps, pb = fm_block(wt, wb_, bl, 16, hT, hT_b)
                        zz = zc % 2; zc += 1
                        P.op("act", lambda e, ps=ps, zz=zz: e.activation(szt[zz][:, :], ps[:, :], AF.Silu), reads=[pb], writes=[szt_b[zz]])
                        P.op("dve", lambda e, b=b, zz=zz: e.tensor_tensor(yT[:, b, :], yT[:, b, :], szt[zz][:, :], ALU.mult),
                             reads=yT_b + [szt_b[zz]], writes=yT_b)
                if D0:
                    dump("yc", yT[:, :, :].rearrange("p a b -> p (a b)"), yT_b, 4096)
                merge_term(P, l, 2, w_get, fm_block, yT, yT_b, hT, hT_b, merged, mrg_b, bank, (PC0, GC0, GC1, PC1, GC2, GC3), tB, tBh_b)

                if D0:
                    dump("m2", merged[:, :, :].rearrange("p a b -> p (a b)"), mrg_b, 8192)
                srcb = X_b[0] if l == 0 else X_b[1 + (l - 1) % 2]
                xc = [0]
                s5 = []
                for c in range(4):
                    def getw(c=c):
                        s5w[0] = w_get(l * NIMG + WO0 + c)
                    for j in range(4):
                        def grp(c=c, j=j, getw=getw):
                            if j == 0:
                                getw()
                            wo, wo_b = s5w[0]
                            tile = t_first + j
                            rows = slice(tile * 128, (tile + 1) * 128)
                            cols = slice(c * 512, (c + 1) * 512)
                            xx = xc[0] % 2; xc[0] += 1
                            P.dma("sp", xch[xx][:, :], cur[rows, cols], reads=[srcb[tile]], writes=[xch_b[xx]])
                            ps, pb = bank()
                            P.op("pe", [mm(ps[:, :], merged[:, k, j * 128:(j + 1) * 128], wo[:, k, :], k == 0, k == 15) for k in range(16)],
                                 reads=[wo_b] + mrg_b, writes=[pb])
                            P.op("dve", lambda e: e.tensor_tensor(xch[xx][:, :], ps[:, :], xch[xx][:, :], ALU.add),
                                 reads=[pb, xch_b[xx]], writes=[xch_b[xx]])
                            P.dma("sp", X[nxt_i][rows, cols], xch[xx][:, :], reads=[xch_b[xx]], writes=[X_b[nxt_i][tile]])
                        s5.append(grp)
                s5w = [None]
                nxt0 = stage0_items(st + 1) if has_next else []
                gi = 0
                for it in nxt0:
                    for _ in range(3):
                        if gi < len(s5):
                            s5[gi](); gi += 1
                    it()
                while gi < len(s5):
                    s5[gi](); gi += 1
                pre0[0] = has_next

    fin = X[1 + (depth - 1) % 2]
    fin_b = X_b[1 + (depth - 1) % 2]
    P.dma("sp", tA[:, :], fg_d[0, 0:1024].partition_broadcast(128), writes=[tA_b])
    P.dma("sp", tB[:, :], fg_d[0, 1024:2048].partition_broadcast(128), writes=tBh_b)
    for tile in range(NT):
        p = tile % 2
        xt, xtb = xst[p], xst_b[p]
        rows = slice(tile * 128, (tile + 1) * 128)
        P.dma("sp", xt[:, :], fin[rows, :], reads=[fin_b[tile]], writes=[xtb])
        ss, ssb = statcol()
        P.op("act", lambda e, xt=xt, ss=ss: e.activation(hst[:, :], xt[:, :], AF.Square, accum_out=ss), reads=[xtb], writes=[hst_b, ssb])
        r, rb = rstd_from((ss, ssb), 1.0 / D, 1)
        P.op("dve", lambda e, xt=xt, r=r: e.scalar_tensor_tensor(xt[:, 0:1024], xt[:, 0:1024], r, tA[:, :], ALU.mult, ALU.mult),
             reads=[xtb, rb, tA_b], writes=[xtb])
        P.op("dve", lambda e, xt=xt, r=r: e.scalar_tensor_tensor(xt[:, 1024:2048], xt[:, 1024:2048], r, tB[:, :], ALU.mult, ALU.mult),
             reads=[xtb, rb] + tBh_b, writes=[xtb])
        P.dma("sp", y_out[rows, :], xt[:, :], reads=[xtb], writes=[Y_b[tile]])
    P.finish()
    P.run_block()
    es.close()
    return nc


def merge_term(P, l, which, w_get, fm_block, yT, yT_b, hT, hT_b, merged, mrg_b, bank, imgs, tB, tBh_b):
    PA, G0, G1, PB_, G2, G3 = imgs
    order = [(PA, (G0, G1)), (PB_, (G2, G3))]
    for half, (pimg, gimgs) in enumerate(order):
        wp3, wp_b = w_get(l * NIMG + pimg, 1024)
        for gi, gimg in enumerate(gimgs):
            wg3, wg_b = w_get(l * NIMG + gimg, hold=gi + 1)
            for bl in range(4):
                m = half * 8 + gi * 4 + bl
                psg, pgb = fm_block(wg3, wg_b, bl, 16, hT, hT_b)
                sg = tB[:, (m % 2) * 512:(m % 2 + 1) * 512]
                sgb = tBh_b[m % 2]
                P.op("act", lambda e, psg=psg, sg=sg: e.activation(sg, psg[:, :], AF.Sigmoid), reads=[pgb], writes=[sgb])
                psp, ppb = bank()
                P.op("pe", [mm(psp[:, :], wp3[:, k, (gi * 4 + bl) * 128:(gi * 4 + bl + 1) * 128], yT[:, k, :], k == 0, k == 7) for k in range(8)],
                     reads=[wp_b] + yT_b, writes=[ppb])
                if which == 0:
                    P.op("dve", lambda e, psp=psp, sg=sg, m=m: e.tensor_tensor(merged[:, m, :], psp[:, :], sg, ALU.mult),
                         reads=[ppb, sgb], writes=[mrg_b[m]])
                else:
                    P.op("dve", lambda e, psp=psp, sg=sg: e.tensor_tensor(sg, psp[:, :], sg, ALU.mult), reads=[ppb, sgb], writes=[sgb])
                    P.op("pool", lambda e, sg=sg, m=m: e.tensor_tensor(merged[:, m, :], merged[:, m, :], sg, ALU.add),
                         reads=[sgb, mrg_b[m]], writes=[mrg_b[m]])


def _img_in(w, c0, n=512):
    a = w[:, c0:c0 + n].reshape(16, 128, n).transpose(1, 0, 2)
    return a


def _pack_layer(w_in, w_pa, w_pb, w_pc, w_out):
    imgs = np.empty((NIMG, 128, 8192), np.float32)

    def put_in(i, c0):
        imgs[i] = _img_in(w_in, c0).reshape(128, 8192)

    def put_proj(i, wp, c0):
        imgs[i] = wp[:, c0:c0 + 1024].reshape(8, 128, 1024).transpose(1, 0, 2).reshape(128, 8192)

    put_in(VA0, OFF["va"]); put_in(VA1, OFF["va"] + 512)
    put_in(UA0, OFF["ua"]); put_in(UA1, OFF["ua"] + 512)
    put_in(ZA0, OFF["za"]); put_in(ZA1, OFF["za"] + 512)
    put_in(QB0, OFF["qb"]); put_in(QB1, OFF["qb"] + 512)
    put_in(KVB, OFF["kb"])
    put_in(ZB0, OFF["zb"]); put_in(ZB1, OFF["zb"] + 512)
    put_in(QC, OFF["qc"]); put_in(KC, OFF["kc"])
    put_in(VC0, OFF["vc"]); put_in(VC1, OFF["vc"] + 512)
    put_in(ZC0, OFF["zc"]); put_in(ZC1, OFF["zc"] + 512)
    for name, ids in (("ga", (GA0, GA1, GA2, GA3)), ("gb", (GB0, GB1, GB2, GB3)), ("gc", (GC0, GC1, GC2, GC3))):
        for i, img in enumerate(ids):
            put_in(img, OFF[name] + 512 * i)
    put_proj(PA0, w_pa, 0); put_proj(PA1, w_pa, 1024)
    put_proj(PB0, w_pb, 0); put_proj(PB1, w_pb, 1024)
    put_proj(PC0, w_pc, 0); put_proj(PC1, w_pc, 1024)
    for c in range(4):
        imgs[WO0 + c] = w_out[:, c * 512:(c + 1) * 512].reshape(16, 128, 512).transpose(1, 0, 2).reshape(128, 8192)
    return imgs


def _consts(smax):
    i = np.arange(128)[:, None]
    j = np.arange(384)[None, :]
    ok = (j >= i) & (j <= i + 256)
    m0 = np.where(ok, 0.0, NEG)
    m1 = np.where(ok & (j >= 128), 0.0, NEG)
    m2 = np.where(ok & (j < 256), 0.0, NEG)
    mm_, cc = np.arange(128)[:, None], np.arange(128)[None, :]
    gmf = np.tile((mm_ <= cc).astype(np.float32), (1, 4))
    gmb = np.tile((mm_ > cc).astype(np.float32), (1, 4))
    cbf = np.concatenate([np.eye(128), m0, m1, m2, gmf, gmb, np.ones((128, 128))], axis=1).astype(np.float32)
    s = -1.0 / 16.0
    M1f = (mm_ <= cc) * s
    M1b = (mm_ >= cc) * s
    M2f = (mm_ > cc) * s
    M2b = (mm_ < cc) * s
    cf = np.concatenate([M1f, M1b, M2f, M2b, np.full((128, 1), s), np.ones((128, 1)), np.full((128, 1), EPS),
                         np.zeros((128, 1))], axis=1).astype(np.float32)
    half = 64
    inv = (10000.0 ** (-np.arange(half, dtype=np.float32) * 2.0 / 128)).astype(np.float32)
    ang = np.arange(smax, dtype=np.float32)[None, :] * inv[:, None]
    cos = np.cos(ang).astype(np.float32)
    sin = np.sin(ang).astype(np.float32)
    ropeC = np.concatenate([cos, cos], axis=0)
    ropeS = np.concatenate([sin, -sin], axis=0)
    return cbf, cf, np.ascontiguousarray(ropeC), np.ascontiguousarray(ropeS)


def _run(xs_per_core, seqs, depth, norm_g, w_in, a_ln_g, a_ln_b, a_ws, a_bs, b_sink, c_wf, c_bf, c_wb, c_bb,
         c_norm_g, w_pa, w_pb, w_pc, w_out, final_g):
    smax = max(seqs)
    f = lambda a: np.ascontiguousarray(np.asarray(a, dtype=np.float32))
    w_in, w_pa, w_pb, w_pc, w_out = map(f, (w_in, w_pa, w_pb, w_pc, w_out))
    wsl = np.concatenate([_pack_layer(w_in[l], w_pa[l], w_pb[l], w_pc[l], w_out[l]) for l in range(depth)], axis=0)
    wlr = np.stack([_img_in(w_in[l], OFF["lrf"], 32).reshape(128, 512) for l in range(depth)])
    wg = np.stack([np.concatenate([f(w)[l], f(b)[l][None, :]], axis=0) for l in range(depth) for (w, b) in ((c_wf, c_bf), (c_wb, c_bb))])
    wsT = np.stack([f(a_ws)[l].transpose(2, 0, 1).reshape(128, 512) for l in range(depth)])
    cbf, cf, ropeC, ropeS = _consts(smax)
    common = dict(wsl=wsl, wlr=f(wlr), wg=f(wg), wsT=f(wsT), norm_gT=np.ascontiguousarray(f(norm_g)[:depth].reshape(depth, 16, 128).transpose(0, 2, 1)), a_ln_g=f(a_ln_g)[:depth],
                  a_ln_b=f(a_ln_b)[:depth], a_bs=f(a_bs)[:depth].reshape(depth, 512), b_sink=f(b_sink)[:depth],
                  c_norm_gT=np.ascontiguousarray(f(c_norm_g)[:depth].reshape(depth, 2, 128).transpose(0, 2, 1)), final_g=f(final_g).reshape(1, D), cbf=cbf, cf=cf, ropeC=ropeC, ropeS=ropeS)
    nc = build(seqs, depth, smax, dbg=_DBGFLAG[0])
    in_maps = [dict(common, x=f(x)) for x in xs_per_core]
    res = run_bass_kernel_spmd(nc, in_maps, core_ids=list(range(len(in_maps))))
    if _DBGFLAG[0]:
        _DBGOUT.append(res.results[0])
    return [r["y"] for r in res.results]


_DBGFLAG = [False]
_DBGOUT = []


def kernel(x_prompt, x_sample, norm_g, w_in, a_ln_g, a_ln_b, a_ws, a_bs, b_sink, c_wf, c_bf, c_wb, c_bb,
           c_norm_g, w_pa, w_pb, w_pc, w_out, final_g):
    x_prompt = np.asarray(x_prompt, dtype=np.float32)
    x_sample = np.asarray(x_sample, dtype=np.float32)
    SP, SS = x_prompt.shape[1], x_sample.shape[1]
    xs = [np.concatenate([x_sample[c], x_prompt[c % 4]], axis=0) for c in range(8)]
    ys = _run(xs, [SS, SP], 4, norm_g, w_in, a_ln_g, a_ln_b, a_ws, a_bs, b_sink, c_wf, c_bf, c_wb, c_bb,
              c_norm_g, w_pa, w_pb, w_pc, w_out, final_g)
    y_sample = np.stack([ys[c][:SS] for c in range(8)])
    y_prompt = np.stack([ys[c][SS:SS + SP] for c in range(4)])
    return (y_prompt, y_sample)
```

```python
import numpy as np
from contextlib import ExitStack
import concourse.bass as bass
import concourse.mybir as mybir
from concourse.bass_utils import run_bass_kernel_spmd

F32 = mybir.dt.float32
BF16 = mybir.dt.bfloat16
AF = mybir.ActivationFunctionType
ALU = mybir.AluOpType
AX = mybir.AxisListType

D = 2048
EPS = 1e-6
NEG = -30000.0
ENGS = ("pe", "act", "dve", "pool", "sp")

OFF = dict(ua=0, va=1024, za=2048, qb=3072, kb=4096, vb=4352, zb=4608, qc=5632, kc=6144,
           vc=6656, zc=7680, lrf=8704, lrb=8720, ga=8736, gb=10784, gc=12832)
NIMG = 39
(VA0, VA1, UA0, UA1, ZA0, ZA1, PA0, GA0, GA1, PA1, GA2, GA3,
 QB0, QB1, KVB, ZB0, ZB1, PB0, GB0, GB1, PB1, GB2, GB3,
 QC, KC, VC0, VC1, ZC0, ZC1, PC0, GC0, GC1, PC1, GC2, GC3,
 WO0, WO1, WO2, WO3) = range(NIMG)
MAIN_ORDER = [VA0, VA1, UA0, UA1, ZA0, ZA1, PA0, GA0, GA1, PA1, GA2, GA3,
              QB0, QB1, KVB, ZB0, ZB1, PB0, GB0, GB1, PB1, GB2, GB3,
              VC0, VC1, QC, KC, ZC0, ZC1, PC0, GC0, GC1, PC1, GC2, GC3,
              WO0, WO1, WO2, WO3]
PRE_ORDER = [KC, VC0, VC1]


class Buf:
    __slots__ = ("name", "writers", "readers")

    def __init__(self, name=""):
        self.name = name
        self.writers = {}
        self.readers = {}


class Prog:
    NDMA = 12

    def __init__(self, nc):
        self.nc = nc
        self.q = {e: [] for e in ENGS}
        self.esem = {e: nc.alloc_semaphore(name=f"s_{e}") for e in ENGS}
        self.ecnt = {e: 0 for e in ENGS}
        self.dsem = {e: [nc.alloc_semaphore(name=f"d_{e}{i}") for i in range(self.NDMA)]
                     for e in ("sp", "pool")}
        self.dcnt = {e: [0] * self.NDMA for e in self.dsem}
        self.dnext = {e: 0 for e in self.dsem}
        self.waited = {e: {} for e in ENGS}
        self.semobj = {}
        for e in ENGS:
            self.semobj[("e", e)] = self.esem[e]
        for e in self.dsem:
            for i, s in enumerate(self.dsem[e]):
                self.semobj[("d", e, i)] = s
        self.ninst = 0

    def _emit_waits(self, eng, evs):
        w = self.waited[eng]
        for key, val in evs.items():
            if eng == "pe" and key == ("e", "pe"):
                continue
            if w.get(key, 0) >= val:
                continue
            w[key] = val
            sem = self.semobj[key]
            self.q[eng].append(lambda e, sem=sem, val=val: e.wait_ge(sem, val))

    @staticmethod
    def _merge(d, key, val):
        if d.get(key, 0) < val:
            d[key] = val

    @staticmethod
    def _flat(bufs):
        out = []
        for b in bufs:
            if isinstance(b, (list, tuple)):
                out.extend(Prog._flat(b))
            else:
                out.append(b)
        return out

    def _deps(self, reads, writes):
        evs = {}
        for b in reads:
            for k, v in b.writers.items():
                self._merge(evs, k, v)
        for b in writes:
            for k, v in b.writers.items():
                self._merge(evs, k, v)
            for k, v in b.readers.items():
                self._merge(evs, k, v)
        return evs

    def _commit(self, ev, reads, writes):
        key, val = ev
        for b in writes:
            b.writers = {key: val}
            b.readers = {}
        for b in reads:
            self._merge(b.readers, key, val)

    def op(self, eng, fns, reads=(), writes=()):
        reads, writes = self._flat(reads), self._flat(writes)
        if callable(fns):
            fns = [fns]
        self._emit_waits(eng, self._deps(reads, writes))
        self.ecnt[eng] += 1
        val = self.ecnt[eng]
        sem = self.esem[eng]
        for f in fns[:-1]:
            self.q[eng].append(f)
        last = fns[-1]
        self.q[eng].append(lambda e, last=last, sem=sem: last(e).then_inc(sem, 1))
        self.ninst += len(fns)
        self._commit((("e", eng), val), reads, writes)

    def dma(self, qeng, out, in_, reads=(), writes=()):
        reads, writes = self._flat(reads), self._flat(writes)
        i = self.dnext[qeng]
        self.dnext[qeng] = (i + 1) % self.NDMA
        key = ("d", qeng, i)
        evs = self._deps(reads, writes)
        if self.dcnt[qeng][i] > 0:
            self._merge(evs, key, self.dcnt[qeng][i])
        self._emit_waits(qeng, evs)
        self.dcnt[qeng][i] += 16
        val = self.dcnt[qeng][i]
        sem = self.dsem[qeng][i]
        self.q[qeng].append(
            lambda e, out=out, in_=in_, sem=sem: e.dma_start(out=out, in_=in_).then_inc(sem, 16))
        self.ninst += 1
        self._commit((key, val), reads, writes)

    def finish(self):
        evs = {}
        for e in self.dsem:
            for i in range(self.NDMA):
                if self.dcnt[e][i] > 0:
                    self._merge(evs, ("d", e, i), self.dcnt[e][i])
        for e in ENGS:
            if e != "sp" and self.ecnt[e] > 0:
                self._merge(evs, ("e", e), self.ecnt[e])
        self._emit_waits("sp", evs)

    def run_block(self):
        q = self.q
        with self.nc.Block() as block:
            @block.tensor
            def _(e):
                for f in q["pe"]:
                    f(e)

            @block.scalar
            def _(e):
                for f in q["act"]:
                    f(e)

            @block.vector
            def _(e):
                for f in q["dve"]:
                    f(e)

            @block.gpsimd
            def _(e):
                for f in q["pool"]:
                    f(e)

            @block.sync
            def _(e):
                for f in q["sp"]:
                    f(e)


def skewed(n, stages, offsets):
    for t in range(n + max(offsets)):
        for fn, off in zip(stages, offsets):
            i = t - off
            if 0 <= i < n:
                fn(i)


def mm(out, lhsT, rhs, start=True, stop=True):
    return lambda e: e.matmul(out, lhsT, rhs, start=start, stop=stop)


DBG = []


def build(seqs, depth, smax, dbg=False):
    NTOK = sum(seqs)
    NT = NTOK // 128
    nc = bass.Bass("TRN2", target_bir_lowering=False)
    es = ExitStack()

    def din(name, shape, dt=F32):
        return nc.dram_tensor(name, list(shape), dt, kind="ExternalInput").ap()

    def dint(name, shape, dt):
        return nc.dram_tensor(name, list(shape), dt, kind="Internal").ap()

    x_in = din("x", [NTOK, D])
    wsl = din("wsl", [depth * NIMG, 128, 8192])
    wlr_d = din("wlr", [depth, 128, 512])
    wg_d = din("wg", [depth * 2, 17, 512])
    wsT_d = din("wsT", [depth, 128, 512])
    ng_d = din("norm_gT", [depth, 128, 16])
    lng_d = din("a_ln_g", [depth, 1024])
    lnb_d = din("a_ln_b", [depth, 1024])
    bs_d = din("a_bs", [depth, 512])
    sink_d = din("b_sink", [depth, 8])
    cng_d = din("c_norm_gT", [depth, 128, 2])
    fg_d = din("final_g", [1, D])
    cbf_d = din("cbf", [128, 2432])
    cf_d = din("cf", [128, 516])
    ropeC_d = din("ropeC", [128, smax])
    ropeS_d = din("ropeS", [128, smax])
    y_out = nc.dram_tensor("y", [NTOK, D], F32, kind="ExternalOutput").ap()
    wBl = [dint(f"wB{l}", [NIMG, 128, 8192], BF16) for l in range(depth)]
    wB = [wBl[i // NIMG][i % NIMG] for i in range(depth * NIMG)]
    xa = dint("xa", [NTOK, D], F32)
    xb = dint("xb", [NTOK, D], F32)
    st_d = dint("states", [NT, 128, 1024], BF16)

    P = Prog(nc)
    dbg_n = [0]

    def dump(label, ap, bufs, ncols):
        if not dbg:
            return
        o = nc.dram_tensor(f"dbg_{label}", [128, ncols], F32, kind="ExternalOutput").ap()
        P.dma("pool", o, ap, reads=bufs, writes=[Buf()])
        DBG.append(label)

    def sb(name, shape, dt):
        return es.enter_context(nc.sbuf_tensor("s_" + name, list(shape), dt))

    NRING = 3
    ring = [sb(f"ring{i}", [128, 8192], BF16) for i in range(NRING)]
    ring_b = [Buf(f"ring{i}") for i in range(NRING)]
    hT = sb("hT", [128, 16, 512], BF16); hT_b = [Buf(f"hT{i}") for i in range(4)]
    hTh = sb("hTh", [128, 16, 128], BF16); hTh_b = Buf("hTh")
    xst = [sb("xst0", [128, D], F32)] * 2
    xst_b = [Buf("xst0")] * 2
    hst = sb("hst", [128, D], BF16); hst_b = Buf("hst")
    stat = sb("stat", [128, 192], F32)
    stat_b = [Buf(f"stat{i}") for i in range(16)]
    merged = sb("merged", [128, 16, 512], BF16)
    mrg_b = [Buf(f"mrg{i}") for i in range(16)]
    yT = sb("yT", [128, 8, 512], BF16)
    yT_b = [Buf(f"yT{i}") for i in range(4)]
    cbf = sb("cbf", [128, 2432], BF16); cbf_b = Buf("cbf")
    ident = cbf[:, 0:128]
    masks = [cbf[:, 128 + 384 * i:128 + 384 * (i + 1)] for i in range(3)]
    gmask = [cbf[:, 1280:1792], cbf[:, 1792:2304]]
    ones_bf = cbf[:, 2304:2432]
    cf = sb("cf", [128, 516], F32); cf_b = Buf("cf")
    M1 = [cf[:, 0:128], cf[:, 128:256]]
    M2 = [cf[:, 256:384], cf[:, 384:512]]
    negc = cf[:, 512:513]
    one_c = cf[:, 513:514]
    eps_c = cf[:, 514:515]
    gT = sb("gT", [128, 16], F32); G_b = Buf("G")
    lnG = sb("lnG", [128, 1024], F32)
    lnB = sb("lnB", [128, 1024], F32)
    bs_bc = sb("bs_bc", [128, 512], F32)
    sink_bc = sb("sink_bc", [128, 8], F32)
    nsink = sb("nsink", [128, 8], F32)
    cng = sb("cng", [128, 2], F32)
    wlr = sb("wlr", [128, 16, 32], BF16)
    wg = [sb(f"wg{i}", [17, 512], F32) for i in range(2)]
    wsT = sb("wsT", [128, 4, 128], BF16)
    lp_b = Buf("layer_params")
    ropeC = sb("ropeC", [128, 640], F32)
    ropeS = sb("ropeS", [128, 640], F32)
    rope_b = Buf("rope")
    tA = sb("tA", [128, 1024], F32); tA_b = Buf("tA")
    tB = sb("tB", [128, 1024], F32); tB_b = Buf("tB"); tBh_b = [tB_b, Buf("tB1")]
    tC = [sb(f"tC{i}", [128, 1024], BF16) for i in range(2)]
    tC_b = [Buf(f"tC{i}") for i in range(2)]
    uT = sb("uT", [128, 8, 512], BF16); uT_b = Buf("uT")
    szt = [sb(f"szt{i}", [128, 512], BF16) for i in range(2)]; szt_b = [Buf(f"szt{i}") for i in range(2)]
    xch = [sb(f"xch{i}", [128, 512], F32) for i in range(2)]; xch_b = [Buf(f"xch{i}") for i in range(2)]
    krT = sb("krT", [128, 2, 768], BF16); krT_b = [Buf(f"krT{i}") for i in range(6)]
    vtB = sb("vtB", [128, 6, 256], BF16); vtB_b = [Buf(f"vtB{i}") for i in range(6)]
    pbuf = [sb(f"p{i}", [128, 384], BF16) for i in range(3)]; pbuf_b = [Buf(f"p{i}") for i in range(3)]
    pT = [sb(f"pT{i}", [128, 384], BF16) for i in range(2)]; pT_b = [Buf(f"pT{i}") for i in range(2)]
    Dg = [sb(f"Dg{i}", [128, 128], BF16) for i in range(3)]; Dg_b = [Buf(f"Dg{i}") for i in range(3)]
    vtC = uT[:, :, :].rearrange("p a b -> p (a b)").rearrange("p (j c) -> p j c", c=1024); vtC_b = [uT_b] * 4
    lrT = [sb(f"lrT{i}", [17, 512], F32) for i in range(2)]; lrT_b = [Buf(f"lrT{i}") for i in range(2)]
    gE = [sb("gE", [128, 512], BF16)] * 2; gE_b = [Buf("gE")] * 2
    gEi = [sb("gEi", [128, 512], BF16)] * 2; gEi_b = [Buf("gEi")] * 2
    gEk = [sb("gEk", [128, 512], BF16)] * 2; gEk_b = [Buf("gEk")] * 2
    qe = [sb(f"qe{i}", [128, 512], BF16) for i in range(4)]; qe_b = [Buf(f"qe{i}") for i in range(4)]
    ke = [sb("ke", [128, 512], BF16)] * 2; ke_b = [Buf("ke")] * 2
    kd = [sb(f"kd{i}", [128, 512], BF16) for i in range(2)]; kd_b = [Buf(f"kd{i}") for i in range(2)]
    Am = [sb(f"Am{i}", [128, 512], BF16) for i in range(4)]; Am_b = [Buf(f"Am{i}") for i in range(4)]
    sq = sb("sq", [128, 1024], BF16); sq_b = Buf("sq")
    rstd = sb("rstd", [128, 512], F32); rstd_b = Buf("rstd")
    dec = sb("dec", [128, 12], F32); dec_b = Buf("dec"); decm_b = [Buf("dec0"), Buf("dec1")]
    Sf = sb("Sf", [128, 1024], F32); Sf_b = Buf("Sf")
    Sf16 = sb("Sf16", [128, 1024], BF16); Sf16_b = Buf("Sf16")
    Sb = Sf; Sb_b = Sf_b
    Sb16 = [sb(f"Sb16_{i}", [128, 1024], BF16) for i in range(2)]
    Sb16_b = [Buf(f"Sb16_{i}") for i in range(2)]
    psum = [es.enter_context(nc.psum_tensor(f"ps{i}", [128, 512], F32)) for i in range(8)]
    ps_b = [Buf(f"ps{i}") for i in range(8)]
    pctr = [0]

    reserved = set()

    def bank(reserve=False):
        while pctr[0] % 8 in reserved:
            pctr[0] += 1
        i = pctr[0] % 8
        pctr[0] += 1
        if reserve:
            reserved.add(i)
        return psum[i], ps_b[i]

    sctr = [0]

    def statcol(n=1):
        assert n <= 12
        sl = sctr[0] % 16
        sctr[0] += 1
        return stat[:, sl * 12:sl * 12 + n], stat_b[sl]

    X = [x_in, xa, xb]
    X_b = [[Buf() for _ in range(NT)] for _ in range(3)]
    Y_b = [Buf() for _ in range(NT)]
    st_b = [Buf() for _ in range(NT)]
    wB_b = [Buf() for _ in range(depth * NIMG)]

    P.dma("pool", cbf[:, :], cbf_d, writes=[cbf_b])
    P.dma("sp", cf[:, :], cf_d, writes=[cf_b])
    conv_order = PRE_ORDER + [i for i in MAIN_ORDER if i not in PRE_ORDER]

    def convert(l, lo, hi):
        for i in conv_order[lo:hi]:
            P.dma("pool", wB[l * NIMG + i], wsl[l * NIMG + i], writes=[wB_b[l * NIMG + i]])

    convert(0, 0, NIMG)
    for t in lrT:
        P.op("pool", lambda e, t=t: e.memset(t[:, :], 1.0), writes=lrT_b)

    sched = []
    for l in range(depth):
        sched += [l * NIMG + i for i in PRE_ORDER]
        nst = NTOK // 512
        for _ in range(nst):
            sched += [l * NIMG + i for i in MAIN_ORDER]
    wstate = dict(issued=0, used=0)

    def w_prefetch(upto):
        while wstate["issued"] < min(upto, len(sched)):
            n = wstate["issued"]
            r = n % NRING
            P.dma("sp", ring[r][:, :], wB[sched[n]], reads=[wB_b[sched[n]]], writes=[ring_b[r]])
            wstate["issued"] += 1

    def w_get(img, ncol=512, hold=0):
        n = wstate["used"]
        assert sched[n] == img, (n, sched[n], img)
        w_prefetch(n + NRING - hold)
        wstate["used"] += 1
        r = n % NRING
        return ring[r][:, :].rearrange("p (k n) -> p k n", n=ncol), ring_b[r]

    def load_layer_params(l):
        w = [lp_b]
        P.dma("sp", gT[:, :], ng_d[l], writes=[G_b])
        P.dma("sp", lnG[:, :], lng_d[l, :].partition_broadcast(128), writes=w)
        P.dma("sp", lnB[:, :], lnb_d[l, :].partition_broadcast(128), writes=w)
        P.dma("sp", bs_bc[:, :], bs_d[l, :].partition_broadcast(128), writes=w)
        P.dma("sp", sink_bc[:, :], sink_d[l, :].partition_broadcast(128), writes=w)
        P.dma("sp", cng[:, :], cng_d[l], writes=w)
        for i in range(2):
            P.dma("sp", wg[i][:, :], wg_d[l * 2 + i], writes=w)
        P.dma("pool", wlr[:, :, :], wlr_d[l].rearrange("p (k c) -> p k c", c=32), writes=w)
        P.dma("pool", wsT[:, :, :], wsT_d[l].rearrange("p (g i) -> p g i", i=128), writes=w)
        P.op("pool", lambda e: e.tensor_scalar(nsink[:, :], sink_bc[:, :], -1.0, None, ALU.mult), reads=w, writes=w)

    def rstd_from(ssum, scale, n):
        r, rb = statcol(n)
        P.op("act", lambda e: e.activation(r, ssum[0], AF.Ln, bias=eps_c, scale=scale), reads=[ssum[1], cf_b], writes=[rb])
        P.op("act", lambda e: e.activation(r, r, AF.Exp, scale=-0.5), reads=[rb], writes=[rb])
        return r, rb

    def make_hT_A(l, tile):
        xt, xtb = xst[0], xst_b[0]
        src = X[0] if l == 0 else X[1 + (l - 1) % 2]
        srcb = X_b[0] if l == 0 else X_b[1 + (l - 1) % 2]
        P.dma("sp", xt[:, :], src[tile * 128:(tile + 1) * 128, :], reads=[srcb[tile]], writes=[xtb])
        ss, ssb = statcol()
        P.op("act", lambda e: e.activation(hst[:, :], xt[:, :], AF.Square, accum_out=ss), reads=[xtb], writes=[hst_b, ssb])
        r, rb = rstd_from((ss, ssb), 1.0 / D, 1)
        P.op("dve", lambda e: e.tensor_scalar(hst[:, :], xt[:, :], r, None, ALU.mult), reads=[xtb, rb], writes=[hst_b])

    def make_hT_B(dst, dst_b, c0):
        for q in range(4):
            ps, pb = bank()
            P.op("pe", [mm(ps[:, kk * 128:(kk + 1) * 128], hst[:, (4 * q + kk) * 128:(4 * q + kk + 1) * 128], ident)
                        for kk in range(4)], reads=[hst_b, cbf_b], writes=[pb])
            for kk in range(4):
                k = 4 * q + kk
                d2 = dst[:, k, c0:c0 + 128]
                s2 = ps[:, kk * 128:(kk + 1) * 128]
                if kk % 2 == 0:
                    P.op("act", lambda e, d2=d2, s2=s2, k=k: e.activation(d2, s2, AF.Copy, scale=gT[:, k:k + 1]), reads=[pb, G_b], writes=[dst_b])
                else:
                    P.op("dve", lambda e, d2=d2, s2=s2, k=k: e.tensor_scalar(d2, s2, gT[:, k:k + 1], None, ALU.mult), reads=[pb, G_b], writes=[dst_b])

    def make_hT(l, tile, dst, dst_b, c0):
        make_hT_A(l, tile)
        make_hT_B(dst, dst_b, c0)

    def fm_block(wt, wb_, blk, K, rhs, rhs_b, ncols=512, rhs_c0=0):
        ps, pb = bank()
        P.op("pe", [mm(ps[:, 0:ncols], wt[:, k, blk * 128:(blk + 1) * 128], rhs[:, k, rhs_c0:rhs_c0 + ncols], k == 0, k == K - 1)
                    for k in range(K)], reads=[wb_, rhs_b], writes=[pb])
        return ps, pb

    def tm_group(lhs, lhs_b, c0, wt, wb_, wc0, ncols, K=16, reserve=False):
        ps, pb = bank(reserve=reserve)
        P.op("pe", [mm(ps[:, 0:ncols], lhs[:, k, c0:c0 + 128], wt[:, k, wc0:wc0 + ncols], k == 0, k == K - 1)
                    for k in range(K)], reads=[wb_, lhs_b], writes=[pb])
        return ps, pb

    def gate_L(d, lr_ap, lr_buf, Lout, Lout_b):
        ps, pb = bank()
        P.op("pe", mm(ps[:, :], lr_ap, wg[d][:, :]), reads=[lr_buf, lp_b], writes=[pb])
        P.op("act", lambda e: e.activation(Lout, ps[:, :], AF.Exp, scale=-1.0), reads=[pb], writes=[Lout_b])
        P.op("act", lambda e: e.activation(Lout, Lout, AF.Ln, bias=one_c, scale=1.0), reads=[Lout_b, cf_b], writes=[Lout_b])

    seq_tiles = []
    t0 = 0
    for s in seqs:
        seq_tiles.append((t0, s // 128))
        t0 += s // 128

    for l in range(depth):
        load_layer_params(l)
        cur = X[0] if l == 0 else X[1 + (l - 1) % 2]
        nxt_i = 1 + l % 2
        last = (l == depth - 1)
        wkc, wkc_b = w_get(l * NIMG + KC)
        wv0, wv0_b = w_get(l * NIMG + VC0, hold=1)
        wv1, wv1_b = w_get(l * NIMG + VC1, hold=2)
        for (tb, ntile) in seq_tiles:
            P.op("pool", lambda e: e.memset(Sb[:, :], 0.0), writes=[Sb_b])
            cur16 = 0
            P.op("pool", lambda e, c=cur16: e.memset(Sb16[c][:, :], 0.0), writes=[Sb16_b[cur16]])
            order = list(reversed(range(ntile)))
            PS = [dict() for _ in range(ntile)]
            c16 = [0]

            def PP0(i):
                make_hT(l, tb + order[i], hT, hT_b[i % 4], (i % 4) * 128)

            def PP1(i):
                par = i % 2
                hb_, c0 = hT_b[i % 4], (i % 4) * 128
                PS[i]["k"] = tm_group(hT, hb_, c0, wkc, wkc_b, 0, 512, reserve=True)
                vt = tC[par]; vt_b = tC_b[par]
                for half, (wv, wvb) in enumerate(((wv0, wv0_b), (wv1, wv1_b))):
                    ps, pb = tm_group(hT, hb_, c0, wv, wvb, 0, 512)
                    P.op("act", lambda e, ps=ps, half=half, vt=vt: e.copy(vt[:, half * 512:(half + 1) * 512], ps[:, :]), reads=[pb], writes=[vt_b])
                ps, pb = bank()
                P.op("pe", [mm(ps[0:16, 0:128], wlr[:, k, 16:32], hT[:, k, c0:c0 + 128], k == 0, k == 15) for k in range(16)],
                     reads=[lp_b, hb_], writes=[pb])
                P.op("dve", lambda e, ps=ps: e.tensor_copy(lrT[par][0:16, 0:128], ps[0:16, 0:128]), reads=[pb], writes=[lrT_b[par]])
                gate_L(1, lrT[par][:, 0:128], lrT_b[par], tB[:, par * 512:(par + 1) * 512], tBh_b[par])

            def PP2(i):
                par = i % 2
                tile = tb + order[i]
                Lb = tB[:, par * 512:(par + 1) * 512]
                vt = tC[par]; vt_b = tC_b[par]
                ps, pb = bank()
                P.op("pe", mm(ps[:, :], M2[1], Lb), reads=[cf_b, tBh_b[par]], writes=[pb])
                P.op("act", lambda e, ps=ps: e.activation(gEk[1][:, :], ps[:, :], AF.Exp), reads=[pb], writes=[gEk_b[1]])
                ps, pb = bank()
                P.op("pe", [mm(ps[:, h:h + 1], Lb[:, h * 128:(h + 1) * 128], negc) for h in range(4)], reads=[cf_b, tBh_b[par]], writes=[pb])
                P.op("act", lambda e, ps=ps: e.activation(dec[:, 8:12], ps[:, 0:4], AF.Exp), reads=[pb], writes=[dec_b])
                psk, pbk = PS[i]["k"]
                P.op("dve", lambda e: e.tensor_tensor(kd[1][:, :], psk[:, :], gEk[1][:, :], ALU.mult), reads=[pbk, gEk_b[1]], writes=[kd_b[1]])
                reserved.discard(ps_b.index(pbk))
                cur16 = c16[0]
                P.dma("pool", st_d[tile], Sb16[cur16][:, :], reads=[Sb16_b[cur16]], writes=[st_b[tile]])
                for hp in range(2):
                    ps, pb = bank()
                    P.op("pe", [mm(ps[:, hh * 256:(hh + 1) * 256], kd[1][:, (2 * hp + hh) * 128:(2 * hp + hh + 1) * 128],
                                   vt[:, (2 * hp + hh) * 256:(2 * hp + hh + 1) * 256]) for hh in range(2)],
                         reads=[kd_b[1], vt_b], writes=[pb])
                    for hh in range(2):
                        h = 2 * hp + hh
                        P.op("dve", lambda e, ps=ps, h=h, hh=hh: e.scalar_tensor_tensor(
                            Sb[:, h * 256:(h + 1) * 256], Sb[:, h * 256:(h + 1) * 256], dec[:, 8 + h:9 + h],
                            ps[:, hh * 256:(hh + 1) * 256], ALU.mult, ALU.add), reads=[pb, dec_b, Sb_b], writes=[Sb_b])
                c16[0] ^= 1
                nx = c16[0]
                P.op("pool", lambda e: e.tensor_copy(Sb16[nx][:, :], Sb[:, :]), reads=[Sb_b], writes=[Sb16_b[nx]])

            skewed(ntile, [PP0, PP1, PP2], [0, 1, 2])

        stc = [0]
        pre0 = [False]
        for (tb, ntile) in seq_tiles:
            nst = ntile // 4
            P.op("pool", lambda e: e.memset(Sf[:, :], 0.0), writes=[Sf_b])
            P.op("pool", lambda e: e.memset(Sf16[:, :], 0.0), writes=[Sf16_b])
            for st in range(nst):
                t_first = tb + st * 4
                pos0 = st * 512
                has_next = st < nst - 1
                has_prev = st > 0
                if l + 1 < depth:
                    per = -(-NIMG // (NTOK // 512))
                    convert(l + 1, stc[0] * per, (stc[0] + 1) * per)
                    stc[0] += 1
                def stage0_items(st_, l=l, tb=tb, nst=nst):
                    tf = tb + st_ * 4
                    hn = st_ < nst - 1
                    items = [((lambda j=j: make_hT_A(l, tf + j)), (lambda j=j: make_hT_B(hT, hT_b[j], j * 128))) for j in range(4)]
                    if hn:
                        items.append(((lambda: make_hT_A(l, tf + 4)), (lambda: make_hT_B(hTh, hTh_b, 0))))

                    def ropes():
                        nr = 640 if hn else 512
                        P.dma("sp", ropeC[:, 0:nr], ropeC_d[:, st_ * 512:st_ * 512 + nr], writes=[rope_b])
                        P.dma("sp", ropeS[:, 0:nr], ropeS_d[:, st_ * 512:st_ * 512 + nr], writes=[rope_b])
                    items.append((ropes, lambda: None))
                    return items

                if not pre0[0]:
                    for ia, ib in stage0_items(st):
                        ia(); ib()
                pre0[0] = False
                D0 = (l == 0 and st == 0 and tb == 0)
                if D0:
                    dump("hT", hT[:, :, :].rearrange("p a b -> p (a b)"), hT_b, 8192)
                wva = [w_get(l * NIMG + VA0), w_get(l * NIMG + VA1, hold=1)]
                S1 = [None] * 4

                def A1(j):
                    pss = [tm_group(hT, hT_b, j * 128, wva[h][0], wva[h][1], 0, 512) for h in range(2)]
                    bst, bstb = statcol(12)
                    for h in range(2):
                        P.op("dve", lambda e, h=h, ps=pss[h][0], bst=bst: e.bn_stats(bst[:, h * 6:(h + 1) * 6], ps[:, :]),
                             reads=[pss[h][1]], writes=[bstb])
                    mv, mvb = statcol(2)
                    P.op("dve", lambda e, bst=bst, mv=mv: e.bn_aggr(mv, bst), reads=[bstb], writes=[mvb])
                    r, rb = rstd_from((mv[:, 1:2], mvb), 1.0, 1)
                    nm, nmb = statcol()
                    P.op("dve", lambda e, nm=nm, mv=mv, r=r: e.scalar_tensor_tensor(nm, mv[:, 0:1], -1.0, r, ALU.mult, ALU.mult),
                         reads=[mvb, rb], writes=[nmb])
                    for h in range(2):
                        P.op("act", lambda e, h=h, ps=pss[h][0], r=r, nm=nm: e.activation(
                            tA[:, h * 512:(h + 1) * 512], ps[:, :], AF.Identity, bias=nm, scale=r),
                            reads=[pss[h][1], rb, nmb], writes=[tA_b])
                    P.op("dve", lambda e: e.tensor_tensor(tA[:, :], tA[:, :], lnG[:, :], ALU.mult), reads=[tA_b, lp_b], writes=[tA_b])
                    vn = tC[j % 2]; vnb = tC_b[j % 2]
                    P.op("dve", lambda e, vn=vn: e.tensor_tensor(vn[:, :], tA[:, :], lnB[:, :], ALU.add), reads=[tA_b, lp_b], writes=[vnb])
                    S1[j] = (vn, vnb)

                def A2(j):
                    vn, vnb = S1[j]
                    for hb in range(2):
                        ps, pb = bank()
                        P.op("pe", [mm(ps[:, c * 128:(c + 1) * 128], vn[:, (4 * hb + c) * 128:(4 * hb + c + 1) * 128],
                                       wsT[:, (4 * hb + c) // 2, :]) for c in range(4)], reads=[vnb, lp_b], writes=[pb])
                        for gg in range(2):
                            g = hb * 2 + gg
                            P.op("dve", lambda e, ps=ps, gg=gg, g=g, hb=hb: e.tensor_tensor(
                                yT[:, 4 * hb + 2 * gg:4 * hb + 2 * gg + 2, j * 128:(j + 1) * 128],
                                ps[:, gg * 256:(gg + 1) * 256].rearrange("p (a b) -> p a b", b=128),
                                bs_bc[:, g * 128:(g + 1) * 128].unsqueeze(1).to_broadcast([128, 2, 128]), ALU.add),
                                reads=[pb, lp_b], writes=[yT_b[j]])

                skewed(4, [A1, A2], [0, 1])
                for half, img in enumerate((UA0, UA1)):
                    wt, wb_ = w_get(l * NIMG + img)
                    for bl in range(4):
                        ps, pb = fm_block(wt, wb_, bl, 16, hT, hT_b)
                        P.op("act", lambda e, ps=ps, b=half * 4 + bl: e.copy(uT[:, b, :], ps[:, :]), reads=[pb], writes=[uT_b])
                zc = 0
                for half, img in enumerate((ZA0, ZA1)):
                    wt, wb_ = w_get(l * NIMG + img)
                    for bl in range(4):
                        b = half * 4 + bl
                        ps, pb = fm_block(wt, wb_, bl, 16, hT, hT_b)
                        zz = zc % 2; zc += 1
                        P.op("act", lambda e, ps=ps, zz=zz: e.activation(szt[zz][:, :], ps[:, :], AF.Silu), reads=[pb], writes=[szt_b[zz]])
                        P.op("dve", lambda e, b=b, zz=zz: e.tensor_tensor(uT[:, b, :], uT[:, b, :], szt[zz][:, :], ALU.mult),
                             reads=[uT_b, szt_b[zz]], writes=[uT_b])
                        P.op("dve", lambda e, b=b: e.tensor_tensor(yT[:, b, :], yT[:, b, :], uT[:, b, :], ALU.mult),
                             reads=[uT_b] + yT_b, writes=yT_b)
                if D0:
                    dump("ya", yT[:, :, :].rearrange("p a b -> p (a b)"), yT_b, 4096)
                merge_term(P, l, 0, w_get, fm_block, yT, yT_b, hT, hT_b, merged, mrg_b, bank, (PA0, GA0, GA1, PA1, GA2, GA3), tB, tBh_b)

                if has_prev:
                    P.op("pool", lambda e: e.tensor_copy(krT[:, :, 0:128], krT[:, :, 512:640]), reads=[krT_b[4]], writes=[krT_b[0]])
                    P.op("pool", lambda e: e.tensor_copy(vtB[:, 0, :], vtB[:, 4, :]), reads=[vtB_b[4]], writes=[vtB_b[0]])
                else:
                    P.op("pool", lambda e: e.memset(krT[:, :, 0:128], 0.0), writes=[krT_b[0]])
                    P.op("pool", lambda e: e.memset(vtB[:, 0, :], 0.0), writes=[vtB_b[0]])
                if not has_next:
                    P.op("pool", lambda e: e.memset(krT[:, :, 640:768], 0.0), writes=[krT_b[5]])
                    P.op("pool", lambda e: e.memset(vtB[:, 5, :], 0.0), writes=[vtB_b[5]])

                rctr = [0]

                def rope_evac(ps, pb, ncols, rc0, dst, dst_bufs):
                    if rctr[0] % 2 == 0:
                        T, Tb = tA, [tA_b]
                    else:
                        T, Tb = tB, tBh_b
                    rctr[0] += 1
                    P.op("dve", lambda e: e.tensor_tensor(T[:, 0:ncols], ps[:, 0:ncols], ropeC[:, rc0:rc0 + ncols], ALU.mult),
                         reads=[pb, rope_b], writes=Tb)
                    P.op("dve", lambda e: e.tensor_tensor(T[0:64, 512:512 + ncols], ps[64:128, 0:ncols], ropeS[64:128, rc0:rc0 + ncols], ALU.mult),
                         reads=[pb, rope_b], writes=Tb)
                    P.op("dve", lambda e: e.tensor_tensor(T[64:128, 512:512 + ncols], ps[0:64, 0:ncols], ropeS[0:64, rc0:rc0 + ncols], ALU.mult),
                         reads=[pb, rope_b], writes=Tb)
                    P.op("dve", lambda e: e.tensor_tensor(dst, T[:, 0:ncols], T[:, 512:512 + ncols], ALU.add), reads=Tb, writes=dst_bufs)

                for half, img in enumerate((QB0, QB1)):
                    wt, wb_ = w_get(l * NIMG + img)
                    for bl in range(4):
                        ps, pb = fm_block(wt, wb_, bl, 16, hT, hT_b)
                        rope_evac(ps, pb, 512, 0, uT[:, half * 4 + bl, :], [uT_b])
                wt, wb_ = w_get(l * NIMG + KVB)
                for kv in range(2):
                    ps, pb = fm_block(wt, wb_, kv, 16, hT, hT_b)
                    rope_evac(ps, pb, 512, 0, krT[:, kv, 128:640], krT_b[1:5])
                    if has_next:
                        ps, pb = fm_block(wt, wb_, kv, 16, hTh, hTh_b, ncols=128)
                        rope_evac(ps, pb, 128, 512, krT[:, kv, 640:768], [krT_b[5]])
                for j in range(5 if has_next else 4):
                    src, srcb, c0 = (hT, hT_b, j * 128) if j < 4 else (hTh, hTh_b, 0)
                    ps, pb = tm_group(src, srcb, c0, wt, wb_, 256, 256)
                    P.op("act", lambda e, ps=ps, j=j: e.copy(vtB[:, j + 1, :], ps[:, 0:256]), reads=[pb], writes=[vtB_b[j + 1]])
                scale = 128.0 ** -0.5
                S = [dict() for _ in range(32)]

                def stA(i):
                    j, h = divmod(i, 8)
                    kv = h // 4
                    tl = st * 4 + j
                    mk = masks[1] if tl == 0 else (masks[2] if tl == ntile - 1 else masks[0])
                    ps, pb = bank()
                    S[i]["ps"], S[i]["pb"] = ps, pb
                    P.op("pe", [mm(ps[:, 0:384], ident, mk, True, False),
                                mm(ps[:, 0:384], uT[:, h, j * 128:(j + 1) * 128], krT[:, kv, j * 128:j * 128 + 384], False, True)],
                         reads=[cbf_b, uT_b] + krT_b[j:j + 3], writes=[pb])

                def stB(i):
                    j, h = divmod(i, 8)
                    ps, pb = S[i]["ps"], S[i]["pb"]
                    pp = i % 3
                    mx, mxb = statcol(4)
                    P.op("dve", lambda e: e.reduce_max(mx[:, 0:1], ps[:, 0:384], AX.X), reads=[pb], writes=[mxb])
                    P.op("dve", lambda e: e.tensor_scalar(mx[:, 1:2], mx[:, 0:1], -scale, nsink[:, h:h + 1], ALU.mult, ALU.min),
                         reads=[mxb, lp_b], writes=[mxb])
                    P.op("act", lambda e: e.activation(pbuf[pp][:, :], ps[:, 0:384], AF.Exp, bias=mx[:, 1:2], scale=scale,
                                                       accum_out=mx[:, 2:3]), reads=[pb, mxb], writes=[pbuf_b[pp], mxb])
                    P.op("act", lambda e: e.activation(mx[:, 3:4], mx[:, 1:2], AF.Exp, bias=sink_bc[:, h:h + 1], scale=1.0),
                         reads=[mxb, lp_b], writes=[mxb])
                    P.op("dve", lambda e: e.tensor_tensor(mx[:, 2:3], mx[:, 2:3], mx[:, 3:4], ALU.add), reads=[mxb], writes=[mxb])
                    P.op("dve", lambda e: e.reciprocal(mx[:, 0:1], mx[:, 2:3]), reads=[mxb], writes=[mxb])
                    P.op("pool", lambda e: e.tensor_scalar(Dg[pp][:, :], ident, mx[:, 0:1], None, ALU.mult),
                         reads=[mxb, cbf_b], writes=[Dg_b[pp]])

                def stC(i):
                    pp = i % 3
                    ps2, pb2 = bank()
                    S[i]["ps2"], S[i]["pb2"] = ps2, pb2
                    P.op("pe", [mm(ps2[:, c * 128:(c + 1) * 128], pbuf[pp][:, c * 128:(c + 1) * 128], Dg[pp][:, :]) for c in range(3)],
                         reads=[pbuf_b[pp], Dg_b[pp]], writes=[pb2])

                def stD(i):
                    ps2, pb2 = S[i]["ps2"], S[i]["pb2"]
                    if i % 2:
                        P.op("act", lambda e: e.copy(pT[i % 2][:, :], ps2[:, 0:384]), reads=[pb2], writes=[pT_b[i % 2]])
                    else:
                        P.op("dve", lambda e: e.tensor_copy(pT[i % 2][:, :], ps2[:, 0:384]), reads=[pb2], writes=[pT_b[i % 2]])

                def stE(i):
                    j, h = divmod(i, 8)
                    kv = h // 4
                    ob, obb = bank()
                    P.op("pe", [mm(ob[:, 0:128], vtB[:, j + c, kv * 128:(kv + 1) * 128],
                                   pT[i % 2][:, c * 128:(c + 1) * 128], c == 0, c == 2) for c in range(3)],
                         reads=[pT_b[i % 2]] + vtB_b[j:j + 3], writes=[obb])
                    P.op("dve", lambda e: e.tensor_copy(yT[:, h, j * 128:(j + 1) * 128], ob[:, 0:128]), reads=[obb], writes=[yT_b[j]])

                skewed(32, [stA, stB, stC, stD, stE], [0, 0, 2, 2, 3])
                zc = 0
                for half, img in enumerate((ZB0, ZB1)):
                    wt, wb_ = w_get(l * NIMG + img)
                    for bl in range(4):
                        b = half * 4 + bl
                        ps, pb = fm_block(wt, wb_, bl, 16, hT, hT_b)
                        zz = zc % 2; zc += 1
                        P.op("act", lambda e, ps=ps, zz=zz: e.activation(szt[zz][:, :], ps[:, :], AF.Silu), reads=[pb], writes=[szt_b[zz]])
                        P.op("dve", lambda e, b=b, zz=zz: e.tensor_tensor(yT[:, b, :], yT[:, b, :], szt[zz][:, :], ALU.mult),
                             reads=yT_b + [szt_b[zz]], writes=yT_b)
                if D0:
                    dump("m0", merged[:, :, :].rearrange("p a b -> p (a b)"), mrg_b, 8192)
                    dump("yb", yT[:, :, :].rearrange("p a b -> p (a b)"), yT_b, 4096)
                merge_term(P, l, 1, w_get, fm_block, yT, yT_b, hT, hT_b, merged, mrg_b, bank, (PB0, GB0, GB1, PB1, GB2, GB3), tB, tBh_b)

                wv = [w_get(l * NIMG + VC0), w_get(l * NIMG + VC1, hold=1)]
                for j in range(4):
                    for half in range(2):
                        ps, pb = tm_group(hT, hT_b, j * 128, wv[half][0], wv[half][1], 0, 512)
                        P.op("act", lambda e, ps=ps, j=j, half=half: e.copy(vtC[:, j, half * 512:(half + 1) * 512], ps[:, :]),
                             reads=[pb], writes=[uT_b])
                for d in range(2):
                    ps, pb = bank()
                    P.op("pe", [mm(ps[0:16, :], wlr[:, k, d * 16:(d + 1) * 16], hT[:, k, :], k == 0, k == 15) for k in range(16)],
                         reads=[lp_b, hT_b], writes=[pb])
                    P.op("dve", lambda e, ps=ps, d=d: e.tensor_copy(lrT[d][0:16, :], ps[0:16, :]), reads=[pb], writes=[lrT_b[d]])
                wq, wq_b = w_get(l * NIMG + QC)
                wk, wk_b = w_get(l * NIMG + KC, hold=1)
                qscale = 128.0 ** -0.5
                def G1(j):
                    tile = t_first + j
                    par = j % 2
                    cs = slice(j * 128, (j + 1) * 128)
                    P.dma("sp", Sb16[par][:, :], st_d[tile], reads=[st_b[tile]], writes=[Sb16_b[par]])
                    for d in range(2):
                        gate_L(d, lrT[d][:, cs], lrT_b[d], tB[:, d * 512:(d + 1) * 512], tBh_b[d])
                    psq, pbq = bank()
                    P.op("pe", [mm(psq[:, h * 128:(h + 1) * 128], wq[:, k, h * 128:(h + 1) * 128], hT[:, k, cs], k == 0, k == 15)
                                for h in range(4) for k in range(16)], reads=[wq_b, hT_b], writes=[pbq])
                    psk, pbk = bank()
                    P.op("pe", [mm(psk[:, h * 128:(h + 1) * 128], wk[:, k, h * 128:(h + 1) * 128], hT[:, k, cs], k == 0, k == 15)
                                for h in range(4) for k in range(16)], reads=[wk_b, hT_b], writes=[pbk])
                    for d in range(2):
                        L = tB[:, d * 512:(d + 1) * 512]
                        qd, qdb = qe[2 * par + d], qe_b[2 * par + d]
                        ad, adb = Am[2 * par + d], Am_b[2 * par + d]
                        ps, pb = bank()
                        P.op("pe", [mm(ps[:, h * 128:(h + 1) * 128], L[:, h * 128:(h + 1) * 128], M1[d]) for h in range(4)],
                             reads=[tBh_b[d], cf_b], writes=[pb])
                        P.op("act", lambda e, ps=ps, d=d: e.activation(gE[d][:, :], ps[:, :], AF.Exp), reads=[pb], writes=[gE_b[d]])
                        P.op("act", lambda e, ps=ps, d=d: e.activation(gEi[d][:, :], ps[:, :], AF.Exp, scale=-1.0), reads=[pb], writes=[gEi_b[d]])
                        if d == 0:
                            P.op("act", lambda e, ps=ps: e.activation(dec[:, 4 * par:4 * par + 4], ps[:, :].rearrange("p (h c) -> p h c", c=128)[:, :, 127], AF.Exp),
                                 reads=[pb], writes=[decm_b[par]])
                        P.op("dve", lambda e, d=d, qd=qd: e.scalar_tensor_tensor(qd[:, :], psq[:, :], qscale, gE[d][:, :], ALU.mult, ALU.mult),
                             reads=[pbq, gE_b[d]], writes=[qdb])
                        P.op("dve", lambda e, d=d: e.tensor_tensor(ke[d][:, :], psk[:, :], gEi[d][:, :], ALU.mult),
                             reads=[pbk, gEi_b[d]], writes=[ke_b[d]])
                        ps, pb = bank()
                        P.op("pe", [mm(ps[:, h * 128:(h + 1) * 128], ke[d][:, h * 128:(h + 1) * 128], qd[:, h * 128:(h + 1) * 128]) for h in range(4)],
                             reads=[ke_b[d], qdb], writes=[pb])
                        P.op("dve", lambda e, ps=ps, d=d, ad=ad: e.tensor_tensor(ad[:, :], ps[:, :], gmask[d], ALU.mult), reads=[pb, cbf_b], writes=[adb])
                    ps, pb = bank()
                    P.op("pe", mm(ps[:, :], M2[0], tB[:, 0:512]), reads=[tBh_b[0], cf_b], writes=[pb])
                    P.op("act", lambda e, ps=ps: e.activation(gEk[0][:, :], ps[:, :], AF.Exp), reads=[pb], writes=[gEk_b[0]])
                    ps, pb = tm_group(hT, hT_b, j * 128, wk, wk_b, 0, 512)
                    P.op("dve", lambda e, ps=ps: e.tensor_tensor(kd[par][:, :], ps[:, :], gEk[0][:, :], ALU.mult), reads=[pb, gEk_b[0]], writes=[kd_b[par]])

                def G2(j):
                    par = j % 2
                    obanks = [bank(), bank()]
                    for hp in range(2):
                        ob, obb = obanks[hp]
                        fns = []
                        for hh in range(2):
                            h = 2 * hp + hh
                            for vb in range(2):
                                oc = ob[:, (hh * 2 + vb) * 128:(hh * 2 + vb + 1) * 128]
                                vsl = slice(h * 256 + vb * 128, h * 256 + (vb + 1) * 128)
                                hs = slice(h * 128, (h + 1) * 128)
                                fns.append(mm(oc, vtC[:, j, vsl], Am[2 * par][:, hs], True, False))
                                fns.append(mm(oc, vtC[:, j, vsl], Am[2 * par + 1][:, hs], False, False))
                                fns.append(mm(oc, Sf16[:, vsl], qe[2 * par][:, hs], False, False))
                                fns.append(mm(oc, Sb16[par][:, vsl], qe[2 * par + 1][:, hs], False, True))
                        P.op("pe", fns, reads=[uT_b, Am_b[2 * par], Am_b[2 * par + 1], Sf16_b, Sb16_b[par], qe_b[2 * par], qe_b[2 * par + 1]], writes=[obb])
                    for hp in range(2):
                        ps, pb = bank()
                        P.op("pe", [mm(ps[:, hh * 256:(hh + 1) * 256], kd[par][:, (2 * hp + hh) * 128:(2 * hp + hh + 1) * 128],
                                       vtC[:, j, (2 * hp + hh) * 256:(2 * hp + hh + 1) * 256]) for hh in range(2)],
                             reads=[kd_b[par], uT_b], writes=[pb])
                        for hh in range(2):
                            h = 2 * hp + hh
                            P.op("dve", lambda e, ps=ps, h=h, hh=hh: e.scalar_tensor_tensor(
                                Sf[:, h * 256:(h + 1) * 256], Sf[:, h * 256:(h + 1) * 256], dec[:, 4 * par + h:4 * par + h + 1],
                                ps[:, hh * 256:(hh + 1) * 256], ALU.mult, ALU.add), reads=[pb, decm_b[par], Sf_b], writes=[Sf_b])
                    P.op("pool", lambda e: e.tensor_copy(Sf16[:, :], Sf[:, :]), reads=[Sf_b], writes=[Sf16_b])
                    for hp in range(2):
                        ob, obb = obanks[hp]
                        P.op("act", lambda e, ob=ob, hp=hp: e.activation(sq[:, hp * 512:(hp + 1) * 512], ob[:, :], AF.Square), reads=[obb], writes=[sq_b])
                    ps, pb = bank()
                    P.op("pe", [mm(ps[:, h * 128:(h + 1) * 128], ones_bf, sq[:, (h * 2 + vb) * 128:(h * 2 + vb + 1) * 128], vb == 0, vb == 1)
                                for h in range(4) for vb in range(2)], reads=[sq_b, cbf_b], writes=[pb])
                    P.op("act", lambda e, ps=ps: e.activation(rstd[:, :], ps[:, :], AF.Ln, bias=eps_c, scale=1.0 / 256), reads=[pb, cf_b], writes=[rstd_b])
                    P.op("act", lambda e: e.activation(rstd[:, :], rstd[:, :], AF.Exp, scale=-0.5), reads=[rstd_b], writes=[rstd_b])
                    for hp in range(2):
                        ob, obb = obanks[hp]
                        for vb in range(2):
                            P.op("dve", lambda e, ob=ob, hp=hp, vb=vb, j=j: e.scalar_tensor_tensor(
                                yT[:, 4 * hp:4 * hp + 4, j * 128:(j + 1) * 128].rearrange("p (h v) c -> p h v c", v=2)[:, :, vb, :],
                                ob[:, :].rearrange("p (h v c) -> p h v c", v=2, c=128)[:, :, vb, :], cng[:, vb:vb + 1],
                                rstd[:, hp * 256:(hp + 1) * 256].rearrange("p (h c) -> p h c", c=128), ALU.mult, ALU.mult),
                                reads=[obb, rstd_b, lp_b], writes=[yT_b[j]])

                skewed(4, [G1, G2], [0, 1])
                zc = 0
                for half, img in enumerate((ZC0, ZC1)):
                    wt, wb_ = w_get(l * NIMG + img)
                    for bl in range(4):
                        b = half * 4 + bl
                        ps, pb = fm_block(wt, wb_, bl, 16, hT, hT_b)
                        zz = zc % 2; zc += 1
                        P.op("act", lambda e, ps=ps, zz=zz: e.activation(szt[zz][:, :], ps[:, :], AF.Silu), reads=[pb], writes=[szt_b[zz]])
                        P.op("dve", lambda e, b=b, zz=zz: e.tensor_tensor(yT[:, b, :], yT[:, b, :], szt[zz][:, :], ALU.mult),
                             reads=yT_b + [szt_b[zz]], writes=yT_b)
                if D0:
                    dump("yc", yT[:, :, :].rearrange("p a b -> p (a b)"), yT_b, 4096)
                merge_term(P, l, 2, w_get, fm_block, yT, yT_b, hT, hT_b, merged, mrg_b, bank, (PC0, GC0, GC1, PC1, GC2, GC3), tB, tBh_b)

                if D0:
                    dump("m2", merged[:, :, :].rearrange("p a b -> p (a b)"), mrg_b, 8192)
                srcb = X_b[0] if l == 0 else X_b[1 + (l - 1) % 2]
                xc = [0]
                s5 = []
                for c in range(4):
                    def getw(c=c):
                        s5w[0] = w_get(l * NIMG + WO0 + c)
                    for j in range(4):
                        def grp(c=c, j=j, getw=getw):
                            if j == 0:
                                getw()
                            wo, wo_b = s5w[0]
                            tile = t_first + j
                            rows = slice(tile * 128, (tile + 1) * 128)
                            cols = slice(c * 512, (c + 1) * 512)
                            xx = xc[0] % 2; xc[0] += 1
                            P.dma("sp", xch[xx][:, :], cur[rows, cols], reads=[srcb[tile]], writes=[xch_b[xx]])
                            ps, pb = bank()
                            P.op("pe", [mm(ps[:, :], merged[:, k, j * 128:(j + 1) * 128], wo[:, k, :], k == 0, k == 15) for k in range(16)],
                                 reads=[wo_b] + mrg_b, writes=[pb])
                            P.op("dve", lambda e: e.tensor_tensor(xch[xx][:, :], ps[:, :], xch[xx][:, :], ALU.add),
                                 reads=[pb, xch_b[xx]], writes=[xch_b[xx]])
                            P.dma("pool", X[nxt_i][rows, cols], xch[xx][:, :], reads=[xch_b[xx]], writes=[X_b[nxt_i][tile]])
                        s5.append(grp)
                s5w = [None]
                nxt0 = stage0_items(st + 1) if has_next else []
                gi = 0
                if nxt0:
                    nxt0[0][0]()
                for ii, (ia, ib) in enumerate(nxt0):
                    for _ in range(3):
                        if gi < len(s5):
                            s5[gi](); gi += 1
                    ib()
                    if ii + 1 < len(nxt0):
                        nxt0[ii + 1][0]()
                while gi < len(s5):
                    s5[gi](); gi += 1
                pre0[0] = has_next

    fin = X[1 + (depth - 1) % 2]
    fin_b = X_b[1 + (depth - 1) % 2]
    P.dma("sp", tA[:, :], fg_d[0, 0:1024].partition_broadcast(128), writes=[tA_b])
    P.dma("sp", tB[:, :], fg_d[0, 1024:2048].partition_broadcast(128), writes=tBh_b)
    for tile in range(NT):
        p = tile % 2
        xt, xtb = xst[p], xst_b[p]
        rows = slice(tile * 128, (tile + 1) * 128)
        P.dma("sp", xt[:, :], fin[rows, :], reads=[fin_b[tile]], writes=[xtb])
        ss, ssb = statcol()
        P.op("act", lambda e, xt=xt, ss=ss: e.activation(hst[:, :], xt[:, :], AF.Square, accum_out=ss), reads=[xtb], writes=[hst_b, ssb])
        r, rb = rstd_from((ss, ssb), 1.0 / D, 1)
        P.op("dve", lambda e, xt=xt, r=r: e.scalar_tensor_tensor(xt[:, 0:1024], xt[:, 0:1024], r, tA[:, :], ALU.mult, ALU.mult),
             reads=[xtb, rb, tA_b], writes=[xtb])
        P.op("dve", lambda e, xt=xt, r=r: e.scalar_tensor_tensor(xt[:, 1024:2048], xt[:, 1024:2048], r, tB[:, :], ALU.mult, ALU.mult),
             reads=[xtb, rb] + tBh_b, writes=[xtb])
        P.dma("pool", y_out[rows, :], xt[:, :], reads=[xtb], writes=[Y_b[tile]])
    P.finish()
    P.run_block()
    es.close()
    return nc


def merge_term(P, l, which, w_get, fm_block, yT, yT_b, hT, hT_b, merged, mrg_b, bank, imgs, tB, tBh_b):
    PA, G0, G1, PB_, G2, G3 = imgs
    order = [(PA, (G0, G1)), (PB_, (G2, G3))]
    for half, (pimg, gimgs) in enumerate(order):
        wp3, wp_b = w_get(l * NIMG + pimg, 1024)
        for gi, gimg in enumerate(gimgs):
            wg3, wg_b = w_get(l * NIMG + gimg, hold=gi + 1)
            for bl in range(4):
                m = half * 8 + gi * 4 + bl
                psg, pgb = fm_block(wg3, wg_b, bl, 16, hT, hT_b)
                sg = tB[:, (m % 2) * 512:(m % 2 + 1) * 512]
                sgb = tBh_b[m % 2]
                P.op("act", lambda e, psg=psg, sg=sg: e.activation(sg, psg[:, :], AF.Sigmoid), reads=[pgb], writes=[sgb])
                psp, ppb = bank()
                P.op("pe", [mm(psp[:, :], wp3[:, k, (gi * 4 + bl) * 128:(gi * 4 + bl + 1) * 128], yT[:, k, :], k == 0, k == 7) for k in range(8)],
                     reads=[wp_b] + yT_b, writes=[ppb])
                if which == 0:
                    P.op("dve", lambda e, psp=psp, sg=sg, m=m: e.tensor_tensor(merged[:, m, :], psp[:, :], sg, ALU.mult),
                         reads=[ppb, sgb], writes=[mrg_b[m]])
                else:
                    P.op("dve", lambda e, psp=psp, sg=sg: e.tensor_tensor(sg, psp[:, :], sg, ALU.mult), reads=[ppb, sgb], writes=[sgb])
                    P.op("pool", lambda e, sg=sg, m=m: e.tensor_tensor(merged[:, m, :], merged[:, m, :], sg, ALU.add),
                         reads=[sgb, mrg_b[m]], writes=[mrg_b[m]])


def _img_in(w, c0, n=512):
    a = w[:, c0:c0 + n].reshape(16, 128, n).transpose(1, 0, 2)
    return a


def _pack_layer(w_in, w_pa, w_pb, w_pc, w_out):
    imgs = np.empty((NIMG, 128, 8192), np.float32)

    def put_in(i, c0):
        imgs[i] = _img_in(w_in, c0).reshape(128, 8192)

    def put_proj(i, wp, c0):
        imgs[i] = wp[:, c0:c0 + 1024].reshape(8, 128, 1024).transpose(1, 0, 2).reshape(128, 8192)

    put_in(VA0, OFF["va"]); put_in(VA1, OFF["va"] + 512)
    put_in(UA0, OFF["ua"]); put_in(UA1, OFF["ua"] + 512)
    put_in(ZA0, OFF["za"]); put_in(ZA1, OFF["za"] + 512)
    put_in(QB0, OFF["qb"]); put_in(QB1, OFF["qb"] + 512)
    put_in(KVB, OFF["kb"])
    put_in(ZB0, OFF["zb"]); put_in(ZB1, OFF["zb"] + 512)
    put_in(QC, OFF["qc"]); put_in(KC, OFF["kc"])
    put_in(VC0, OFF["vc"]); put_in(VC1, OFF["vc"] + 512)
    put_in(ZC0, OFF["zc"]); put_in(ZC1, OFF["zc"] + 512)
    for name, ids in (("ga", (GA0, GA1, GA2, GA3)), ("gb", (GB0, GB1, GB2, GB3)), ("gc", (GC0, GC1, GC2, GC3))):
        for i, img in enumerate(ids):
            put_in(img, OFF[name] + 512 * i)
    put_proj(PA0, w_pa, 0); put_proj(PA1, w_pa, 1024)
    put_proj(PB0, w_pb, 0); put_proj(PB1, w_pb, 1024)
    put_proj(PC0, w_pc, 0); put_proj(PC1, w_pc, 1024)
    for c in range(4):
        imgs[WO0 + c] = w_out[:, c * 512:(c + 1) * 512].reshape(16, 128, 512).transpose(1, 0, 2).reshape(128, 8192)
    return imgs


def _consts(smax):
    i = np.arange(128)[:, None]
    j = np.arange(384)[None, :]
    ok = (j >= i) & (j <= i + 256)
    m0 = np.where(ok, 0.0, NEG)
    m1 = np.where(ok & (j >= 128), 0.0, NEG)
    m2 = np.where(ok & (j < 256), 0.0, NEG)
    mm_, cc = np.arange(128)[:, None], np.arange(128)[None, :]
    gmf = np.tile((mm_ <= cc).astype(np.float32), (1, 4))
    gmb = np.tile((mm_ > cc).astype(np.float32), (1, 4))
    cbf = np.concatenate([np.eye(128), m0, m1, m2, gmf, gmb, np.ones((128, 128))], axis=1).astype(np.float32)
    s = -1.0 / 16.0
    M1f = (mm_ <= cc) * s
    M1b = (mm_ >= cc) * s
    M2f = (mm_ > cc) * s
    M2b = (mm_ < cc) * s
    cf = np.concatenate([M1f, M1b, M2f, M2b, np.full((128, 1), s), np.ones((128, 1)), np.full((128, 1), EPS),
                         np.zeros((128, 1))], axis=1).astype(np.float32)
    half = 64
    inv = (10000.0 ** (-np.arange(half, dtype=np.float32) * 2.0 / 128)).astype(np.float32)
    ang = np.arange(smax, dtype=np.float32)[None, :] * inv[:, None]
    cos = np.cos(ang).astype(np.float32)
    sin = np.sin(ang).astype(np.float32)
    ropeC = np.concatenate([cos, cos], axis=0)
    ropeS = np.concatenate([sin, -sin], axis=0)
    return cbf, cf, np.ascontiguousarray(ropeC), np.ascontiguousarray(ropeS)


def _run(xs_per_core, seqs, depth, norm_g, w_in, a_ln_g, a_ln_b, a_ws, a_bs, b_sink, c_wf, c_bf, c_wb, c_bb,
         c_norm_g, w_pa, w_pb, w_pc, w_out, final_g):
    smax = max(seqs)
    f = lambda a: np.ascontiguousarray(np.asarray(a, dtype=np.float32))
    w_in, w_pa, w_pb, w_pc, w_out = map(f, (w_in, w_pa, w_pb, w_pc, w_out))
    wsl = np.concatenate([_pack_layer(w_in[l], w_pa[l], w_pb[l], w_pc[l], w_out[l]) for l in range(depth)], axis=0)
    wlr = np.stack([_img_in(w_in[l], OFF["lrf"], 32).reshape(128, 512) for l in range(depth)])
    wg = np.stack([np.concatenate([f(w)[l], f(b)[l][None, :]], axis=0) for l in range(depth) for (w, b) in ((c_wf, c_bf), (c_wb, c_bb))])
    wsT = np.stack([f(a_ws)[l].transpose(2, 0, 1).reshape(128, 512) for l in range(depth)])
    cbf, cf, ropeC, ropeS = _consts(smax)
    common = dict(wsl=wsl, wlr=f(wlr), wg=f(wg), wsT=f(wsT), norm_gT=np.ascontiguousarray(f(norm_g)[:depth].reshape(depth, 16, 128).transpose(0, 2, 1)), a_ln_g=f(a_ln_g)[:depth],
                  a_ln_b=f(a_ln_b)[:depth], a_bs=f(a_bs)[:depth].reshape(depth, 512), b_sink=f(b_sink)[:depth],
                  c_norm_gT=np.ascontiguousarray(f(c_norm_g)[:depth].reshape(depth, 2, 128).transpose(0, 2, 1)), final_g=f(final_g).reshape(1, D), cbf=cbf, cf=cf, ropeC=ropeC, ropeS=ropeS)
    nc = build(seqs, depth, smax, dbg=_DBGFLAG[0])
    in_maps = [dict(common, x=f(x)) for x in xs_per_core]
    res = run_bass_kernel_spmd(nc, in_maps, core_ids=list(range(len(in_maps))))
    if _DBGFLAG[0]:
        _DBGOUT.append(res.results[0])
    return [r["y"] for r in res.results]


_DBGFLAG = [False]
_DBGOUT = []


def kernel(x_prompt, x_sample, norm_g, w_in, a_ln_g, a_ln_b, a_ws, a_bs, b_sink, c_wf, c_bf, c_wb, c_bb,
           c_norm_g, w_pa, w_pb, w_pc, w_out, final_g):
    x_prompt = np.asarray(x_prompt, dtype=np.float32)
    x_sample = np.asarray(x_sample, dtype=np.float32)
    SP, SS = x_prompt.shape[1], x_sample.shape[1]
    xs = [np.concatenate([x_sample[c], x_prompt[c % 4]], axis=0) for c in range(8)]
    ys = _run(xs, [SS, SP], 4, norm_g, w_in, a_ln_g, a_ln_b, a_ws, a_bs, b_sink, c_wf, c_bf, c_wb, c_bb,
              c_norm_g, w_pa, w_pb, w_pc, w_out, final_g)
    y_sample = np.stack([ys[c][:SS] for c in range(8)])
    y_prompt = np.stack([ys[c][SS:SS + SP] for c in range(4)])
    return (y_prompt, y_sample)
```

```python
import numpy as np
from contextlib import ExitStack
import concourse.bass as bass
import concourse.mybir as mybir
from concourse.bass_utils import run_bass_kernel_spmd

F32 = mybir.dt.float32
BF16 = mybir.dt.bfloat16
AF = mybir.ActivationFunctionType
ALU = mybir.AluOpType
AX = mybir.AxisListType

D = 2048
EPS = 1e-6
NEG = -30000.0
ENGS = ("pe", "act", "dve", "pool", "sp")

OFF = dict(ua=0, va=1024, za=2048, qb=3072, kb=4096, vb=4352, zb=4608, qc=5632, kc=6144,
           vc=6656, zc=7680, lrf=8704, lrb=8720, ga=8736, gb=10784, gc=12832)
NIMG = 39
(VA0, VA1, UA0, UA1, ZA0, ZA1, PA0, GA0, GA1, PA1, GA2, GA3,
 QB0, QB1, KVB, ZB0, ZB1, PB0, GB0, GB1, PB1, GB2, GB3,
 QC, KC, VC0, VC1, ZC0, ZC1, PC0, GC0, GC1, PC1, GC2, GC3,
 WO0, WO1, WO2, WO3) = range(NIMG)
MAIN_ORDER = [VA0, VA1, UA0, UA1, ZA0, ZA1, PA0, GA0, GA1, PA1, GA2, GA3,
              QB0, QB1, KVB, ZB0, ZB1, PB0, GB0, GB1, PB1, GB2, GB3,
              VC0, VC1, QC, KC, ZC0, ZC1, PC0, GC0, GC1, PC1, GC2, GC3,
              WO0, WO1, WO2, WO3]
PRE_ORDER = [KC, VC0, VC1]


class Buf:
    __slots__ = ("name", "writers", "readers")

    def __init__(self, name=""):
        self.name = name
        self.writers = {}
        self.readers = {}


class Prog:
    NDMA = 12

    def __init__(self, nc):
        self.nc = nc
        self.q = {e: [] for e in ENGS}
        self.esem = {e: nc.alloc_semaphore(name=f"s_{e}") for e in ENGS}
        self.ecnt = {e: 0 for e in ENGS}
        self.dsem = {e: [nc.alloc_semaphore(name=f"d_{e}{i}") for i in range(self.NDMA)]
                     for e in ("sp", "pool")}
        self.dcnt = {e: [0] * self.NDMA for e in self.dsem}
        self.dnext = {e: 0 for e in self.dsem}
        self.waited = {e: {} for e in ENGS}
        self.semobj = {}
        for e in ENGS:
            self.semobj[("e", e)] = self.esem[e]
        for e in self.dsem:
            for i, s in enumerate(self.dsem[e]):
                self.semobj[("d", e, i)] = s
        self.ninst = 0

    def _emit_waits(self, eng, evs):
        w = self.waited[eng]
        for key, val in evs.items():
            if eng == "pe" and key == ("e", "pe"):
                continue
            if w.get(key, 0) >= val:
                continue
            w[key] = val
            sem = self.semobj[key]
            self.q[eng].append(lambda e, sem=sem, val=val: e.wait_ge(sem, val))

    @staticmethod
    def _merge(d, key, val):
        if d.get(key, 0) < val:
            d[key] = val

    @staticmethod
    def _flat(bufs):
        out = []
        for b in bufs:
            if isinstance(b, (list, tuple)):
                out.extend(Prog._flat(b))
            else:
                out.append(b)
        return out

    def _deps(self, reads, writes):
        evs = {}
        for b in reads:
            for k, v in b.writers.items():
                self._merge(evs, k, v)
        for b in writes:
            for k, v in b.writers.items():
                self._merge(evs, k, v)
            for k, v in b.readers.items():
                self._merge(evs, k, v)
        return evs

    def _commit(self, ev, reads, writes):
        key, val = ev
        for b in writes:
            b.writers = {key: val}
            b.readers = {}
        for b in reads:
            self._merge(b.readers, key, val)

    def op(self, eng, fns, reads=(), writes=()):
        reads, writes = self._flat(reads), self._flat(writes)
        if callable(fns):
            fns = [fns]
        self._emit_waits(eng, self._deps(reads, writes))
        self.ecnt[eng] += 1
        val = self.ecnt[eng]
        sem = self.esem[eng]
        for f in fns[:-1]:
            self.q[eng].append(f)
        last = fns[-1]
        self.q[eng].append(lambda e, last=last, sem=sem: last(e).then_inc(sem, 1))
        self.ninst += len(fns)
        self._commit((("e", eng), val), reads, writes)

    def dma(self, qeng, out, in_, reads=(), writes=()):
        reads, writes = self._flat(reads), self._flat(writes)
        i = self.dnext[qeng]
        self.dnext[qeng] = (i + 1) % self.NDMA
        key = ("d", qeng, i)
        evs = self._deps(reads, writes)
        if self.dcnt[qeng][i] > 0:
            self._merge(evs, key, self.dcnt[qeng][i])
        self._emit_waits(qeng, evs)
        self.dcnt[qeng][i] += 16
        val = self.dcnt[qeng][i]
        sem = self.dsem[qeng][i]
        self.q[qeng].append(
            lambda e, out=out, in_=in_, sem=sem: e.dma_start(out=out, in_=in_).then_inc(sem, 16))
        self.ninst += 1
        self._commit((key, val), reads, writes)

    def finish(self):
        evs = {}
        for e in self.dsem:
            for i in range(self.NDMA):
                if self.dcnt[e][i] > 0:
                    self._merge(evs, ("d", e, i), self.dcnt[e][i])
        for e in ENGS:
            if e != "sp" and self.ecnt[e] > 0:
                self._merge(evs, ("e", e), self.ecnt[e])
        self._emit_waits("sp", evs)

    def run_block(self):
        q = self.q
        with self.nc.Block() as block:
            @block.tensor
            def _(e):
                for f in q["pe"]:
                    f(e)

            @block.scalar
            def _(e):
                for f in q["act"]:
                    f(e)

            @block.vector
            def _(e):
                for f in q["dve"]:
                    f(e)

            @block.gpsimd
            def _(e):
                for f in q["pool"]:
                    f(e)

            @block.sync
            def _(e):
                for f in q["sp"]:
                    f(e)


def skewed(n, stages, offsets):
    for t in range(n + max(offsets)):
        for fn, off in zip(stages, offsets):
            i = t - off
            if 0 <= i < n:
                fn(i)


def mm(out, lhsT, rhs, start=True, stop=True):
    return lambda e: e.matmul(out, lhsT, rhs, start=start, stop=stop)


DBG = []


def build(seqs, depth, smax, dbg=False):
    NTOK = sum(seqs)
    NT = NTOK // 128
    nc = bass.Bass("TRN2", target_bir_lowering=False)
    es = ExitStack()

    def din(name, shape, dt=F32):
        return nc.dram_tensor(name, list(shape), dt, kind="ExternalInput").ap()

    def dint(name, shape, dt):
        return nc.dram_tensor(name, list(shape), dt, kind="Internal").ap()

    x_in = din("x", [NTOK, D])
    wsl = din("wsl", [depth * NIMG, 128, 8192])
    wlr_d = din("wlr", [depth, 128, 512])
    wg_d = din("wg", [depth * 2, 17, 512])
    wsT_d = din("wsT", [depth, 128, 512])
    ng_d = din("norm_gT", [depth, 128, 16])
    lng_d = din("a_ln_g", [depth, 1024])
    lnb_d = din("a_ln_b", [depth, 1024])
    bs_d = din("a_bs", [depth, 512])
    sink_d = din("b_sink", [depth, 8])
    cng_d = din("c_norm_gT", [depth, 128, 2])
    fg_d = din("final_g", [1, D])
    cbf_d = din("cbf", [128, 2432])
    cf_d = din("cf", [128, 516])
    ropeC_d = din("ropeC", [128, smax])
    ropeS_d = din("ropeS", [128, smax])
    y_out = nc.dram_tensor("y", [NTOK, D], F32, kind="ExternalOutput").ap()
    wBl = [dint(f"wB{l}", [NIMG, 128, 8192], BF16) for l in range(depth)]
    wB = [wBl[i // NIMG][i % NIMG] for i in range(depth * NIMG)]
    xa = dint("xa", [NTOK, D], F32)
    xb = dint("xb", [NTOK, D], F32)
    st_d = dint("states", [NT, 128, 1024], BF16)

    P = Prog(nc)
    dbg_n = [0]

    def dump(label, ap, bufs, ncols):
        if not dbg:
            return
        o = nc.dram_tensor(f"dbg_{label}", [128, ncols], F32, kind="ExternalOutput").ap()
        P.dma("pool", o, ap, reads=bufs, writes=[Buf()])
        DBG.append(label)

    def sb(name, shape, dt):
        return es.enter_context(nc.sbuf_tensor("s_" + name, list(shape), dt))

    NRING = 3
    ring = [sb(f"ring{i}", [128, 8192], BF16) for i in range(NRING)]
    ring_b = [Buf(f"ring{i}") for i in range(NRING)]
    hT = sb("hT", [128, 16, 512], BF16); hT_b = [Buf(f"hT{i}") for i in range(4)]
    hTh = sb("hTh", [128, 16, 128], BF16); hTh_b = Buf("hTh")
    xst = [sb("xst0", [128, D], F32)] * 2
    xst_b = [Buf("xst0")] * 2
    hst = sb("hst", [128, D], BF16); hst_b = Buf("hst")
    stat = sb("stat", [128, 192], F32)
    stat_b = [Buf(f"stat{i}") for i in range(16)]
    merged = sb("merged", [128, 16, 512], BF16)
    mrg_b = [Buf(f"mrg{i}") for i in range(16)]
    yT = sb("yT", [128, 8, 512], BF16)
    yT_b = [Buf(f"yT{i}") for i in range(4)]
    cbf = sb("cbf", [128, 2432], BF16); cbf_b = Buf("cbf")
    ident = cbf[:, 0:128]
    masks = [cbf[:, 128 + 384 * i:128 + 384 * (i + 1)] for i in range(3)]
    gmask = [cbf[:, 1280:1792], cbf[:, 1792:2304]]
    ones_bf = cbf[:, 2304:2432]
    cf = sb("cf", [128, 516], F32); cf_b = Buf("cf")
    M1 = [cf[:, 0:128], cf[:, 128:256]]
    M2 = [cf[:, 256:384], cf[:, 384:512]]
    negc = cf[:, 512:513]
    one_c = cf[:, 513:514]
    eps_c = cf[:, 514:515]
    gT = sb("gT", [128, 16], F32); G_b = Buf("G")
    lnG = sb("lnG", [128, 1024], F32)
    lnB = sb("lnB", [128, 1024], F32)
    bs_bc = sb("bs_bc", [128, 512], F32)
    sink_bc = sb("sink_bc", [128, 8], F32)
    nsink = sb("nsink", [128, 8], F32)
    cng = sb("cng", [128, 2], F32)
    wlr = sb("wlr", [128, 16, 32], BF16)
    wg = [sb(f"wg{i}", [17, 512], F32) for i in range(2)]
    wsT = sb("wsT", [128, 4, 128], BF16)
    lp_b = Buf("layer_params")
    ropeC = sb("ropeC", [128, 640], F32)
    ropeS = sb("ropeS", [128, 640], F32)
    rope_b = Buf("rope")
    tA = sb("tA", [128, 1024], F32); tA_b = Buf("tA")
    tB = sb("tB", [128, 1024], F32); tB_b = Buf("tB"); tBh_b = [tB_b, Buf("tB1")]
    tC = [sb(f"tC{i}", [128, 1024], BF16) for i in range(2)]
    tC_b = [Buf(f"tC{i}") for i in range(2)]
    uT = sb("uT", [128, 8, 512], BF16); uT_b = Buf("uT")
    szt = [sb(f"szt{i}", [128, 512], BF16) for i in range(2)]; szt_b = [Buf(f"szt{i}") for i in range(2)]
    xch = [sb(f"xch{i}", [128, 512], F32) for i in range(2)]; xch_b = [Buf(f"xch{i}") for i in range(2)]
    krT = sb("krT", [128, 2, 768], BF16); krT_b = [Buf(f"krT{i}") for i in range(6)]
    vtB = sb("vtB", [128, 6, 256], BF16); vtB_b = [Buf(f"vtB{i}") for i in range(6)]
    pbuf = [sb(f"p{i}", [128, 384], BF16) for i in range(3)]; pbuf_b = [Buf(f"p{i}") for i in range(3)]
    pT = [sb(f"pT{i}", [128, 384], BF16) for i in range(2)]; pT_b = [Buf(f"pT{i}") for i in range(2)]
    Dg = [sb(f"Dg{i}", [128, 128], BF16) for i in range(3)]; Dg_b = [Buf(f"Dg{i}") for i in range(3)]
    vtC = uT[:, :, :].rearrange("p a b -> p (a b)").rearrange("p (j c) -> p j c", c=1024); vtC_b = [uT_b] * 4
    lrT = [sb(f"lrT{i}", [17, 512], F32) for i in range(2)]; lrT_b = [Buf(f"lrT{i}") for i in range(2)]
    gE = [sb("gE", [128, 512], BF16)] * 2; gE_b = [Buf("gE")] * 2
    gEi = [sb("gEi", [128, 512], BF16)] * 2; gEi_b = [Buf("gEi")] * 2
    gEk = [sb("gEk", [128, 512], BF16)] * 2; gEk_b = [Buf("gEk")] * 2
    qe = [sb(f"qe{i}", [128, 512], BF16) for i in range(4)]; qe_b = [Buf(f"qe{i}") for i in range(4)]
    ke = [sb("ke", [128, 512], BF16)] * 2; ke_b = [Buf("ke")] * 2
    kd = [sb(f"kd{i}", [128, 512], BF16) for i in range(2)]; kd_b = [Buf(f"kd{i}") for i in range(2)]
    Am = [sb(f"Am{i}", [128, 512], BF16) for i in range(4)]; Am_b = [Buf(f"Am{i}") for i in range(4)]
    sq = sb("sq", [128, 1024], BF16); sq_b = Buf("sq")
    rstd = sb("rstd", [128, 512], F32); rstd_b = Buf("rstd")
    dec = sb("dec", [128, 12], F32); dec_b = Buf("dec"); decm_b = [Buf("dec0"), Buf("dec1")]
    Sf = sb("Sf", [128, 1024], F32); Sf_b = Buf("Sf")
    Sf16 = sb("Sf16", [128, 1024], BF16); Sf16_b = Buf("Sf16")
    Sb = Sf; Sb_b = Sf_b
    Sb16 = [sb(f"Sb16_{i}", [128, 1024], BF16) for i in range(2)]
    Sb16_b = [Buf(f"Sb16_{i}") for i in range(2)]
    psum = [es.enter_context(nc.psum_tensor(f"ps{i}", [128, 512], F32)) for i in range(8)]
    ps_b = [Buf(f"ps{i}") for i in range(8)]
    pctr = [0]

    reserved = set()

    def bank(reserve=False):
        while pctr[0] % 8 in reserved:
            pctr[0] += 1
        i = pctr[0] % 8
        pctr[0] += 1
        if reserve:
            reserved.add(i)
        return psum[i], ps_b[i]

    sctr = [0]

    def statcol(n=1):
        assert n <= 12
        sl = sctr[0] % 16
        sctr[0] += 1
        return stat[:, sl * 12:sl * 12 + n], stat_b[sl]

    X = [x_in, xa, xb]
    X_b = [[Buf() for _ in range(NT)] for _ in range(3)]
    Y_b = [Buf() for _ in range(NT)]
    st_b = [Buf() for _ in range(NT)]
    wB_b = [Buf() for _ in range(depth * NIMG)]

    P.dma("pool", cbf[:, :], cbf_d, writes=[cbf_b])
    P.dma("sp", cf[:, :], cf_d, writes=[cf_b])
    conv_order = PRE_ORDER + [i for i in MAIN_ORDER if i not in PRE_ORDER]

    def convert(l, lo, hi):
        for i in conv_order[lo:hi]:
            P.dma("pool", wB[l * NIMG + i], wsl[l * NIMG + i], writes=[wB_b[l * NIMG + i]])

    convert(0, 0, NIMG)
    for t in lrT:
        P.op("pool", lambda e, t=t: e.memset(t[:, :], 1.0), writes=lrT_b)

    sched = []
    for l in range(depth):
        sched += [l * NIMG + i for i in PRE_ORDER]
        nst = NTOK // 512
        for _ in range(nst):
            sched += [l * NIMG + i for i in MAIN_ORDER]
    wstate = dict(issued=0, used=0)

    def w_prefetch(upto):
        while wstate["issued"] < min(upto, len(sched)):
            n = wstate["issued"]
            r = n % NRING
            P.dma("sp", ring[r][:, :], wB[sched[n]], reads=[wB_b[sched[n]]], writes=[ring_b[r]])
            wstate["issued"] += 1

    def w_get(img, ncol=512, hold=0):
        n = wstate["used"]
        assert sched[n] == img, (n, sched[n], img)
        w_prefetch(n + NRING - hold)
        wstate["used"] += 1
        r = n % NRING
        return ring[r][:, :].rearrange("p (k n) -> p k n", n=ncol), ring_b[r]

    def load_layer_params(l):
        w = [lp_b]
        P.dma("sp", gT[:, :], ng_d[l], writes=[G_b])
        P.dma("sp", lnG[:, :], lng_d[l, :].partition_broadcast(128), writes=w)
        P.dma("sp", lnB[:, :], lnb_d[l, :].partition_broadcast(128), writes=w)
        P.dma("sp", bs_bc[:, :], bs_d[l, :].partition_broadcast(128), writes=w)
        P.dma("sp", sink_bc[:, :], sink_d[l, :].partition_broadcast(128), writes=w)
        P.dma("sp", cng[:, :], cng_d[l], writes=w)
        for i in range(2):
            P.dma("sp", wg[i][:, :], wg_d[l * 2 + i], writes=w)
        P.dma("pool", wlr[:, :, :], wlr_d[l].rearrange("p (k c) -> p k c", c=32), writes=w)
        P.dma("pool", wsT[:, :, :], wsT_d[l].rearrange("p (g i) -> p g i", i=128), writes=w)
        P.op("pool", lambda e: e.tensor_scalar(nsink[:, :], sink_bc[:, :], -1.0, None, ALU.mult), reads=w, writes=w)

    def rstd_from(ssum, scale, n):
        r, rb = statcol(n)
        P.op("act", lambda e: e.activation(r, ssum[0], AF.Ln, bias=eps_c, scale=scale), reads=[ssum[1], cf_b], writes=[rb])
        P.op("act", lambda e: e.activation(r, r, AF.Exp, scale=-0.5), reads=[rb], writes=[rb])
        return r, rb

    def make_hT_A(l, tile):
        xt, xtb = xst[0], xst_b[0]
        src = X[0] if l == 0 else X[1 + (l - 1) % 2]
        srcb = X_b[0] if l == 0 else X_b[1 + (l - 1) % 2]
        P.dma("sp", xt[:, :], src[tile * 128:(tile + 1) * 128, :], reads=[srcb[tile]], writes=[xtb])
        ss, ssb = statcol()
        P.op("act", lambda e: e.activation(hst[:, :], xt[:, :], AF.Square, accum_out=ss), reads=[xtb], writes=[hst_b, ssb])
        r, rb = rstd_from((ss, ssb), 1.0 / D, 1)
        P.op("dve", lambda e: e.tensor_scalar(hst[:, :], xt[:, :], r, None, ALU.mult), reads=[xtb, rb], writes=[hst_b])

    def make_hT_B(dst, dst_b, c0):
        for q in range(4):
            ps, pb = bank()
            P.op("pe", [mm(ps[:, kk * 128:(kk + 1) * 128], hst[:, (4 * q + kk) * 128:(4 * q + kk + 1) * 128], ident)
                        for kk in range(4)], reads=[hst_b, cbf_b], writes=[pb])
            for kk in range(4):
                k = 4 * q + kk
                d2 = dst[:, k, c0:c0 + 128]
                s2 = ps[:, kk * 128:(kk + 1) * 128]
                if kk % 2 == 0:
                    P.op("act", lambda e, d2=d2, s2=s2, k=k: e.activation(d2, s2, AF.Copy, scale=gT[:, k:k + 1]), reads=[pb, G_b], writes=[dst_b])
                else:
                    P.op("dve", lambda e, d2=d2, s2=s2, k=k: e.tensor_scalar(d2, s2, gT[:, k:k + 1], None, ALU.mult), reads=[pb, G_b], writes=[dst_b])

    def make_hT(l, tile, dst, dst_b, c0):
        make_hT_A(l, tile)
        make_hT_B(dst, dst_b, c0)

    def fm_block(wt, wb_, blk, K, rhs, rhs_b, ncols=512, rhs_c0=0):
        ps, pb = bank()
        P.op("pe", [mm(ps[:, 0:ncols], wt[:, k, blk * 128:(blk + 1) * 128], rhs[:, k, rhs_c0:rhs_c0 + ncols], k == 0, k == K - 1)
                    for k in range(K)], reads=[wb_, rhs_b], writes=[pb])
        return ps, pb

    def tm_group(lhs, lhs_b, c0, wt, wb_, wc0, ncols, K=16, reserve=False):
        ps, pb = bank(reserve=reserve)
        P.op("pe", [mm(ps[:, 0:ncols], lhs[:, k, c0:c0 + 128], wt[:, k, wc0:wc0 + ncols], k == 0, k == K - 1)
                    for k in range(K)], reads=[wb_, lhs_b], writes=[pb])
        return ps, pb

    def gate_L(d, lr_ap, lr_buf, Lout, Lout_b):
        ps, pb = bank()
        P.op("pe", mm(ps[:, :], lr_ap, wg[d][:, :]), reads=[lr_buf, lp_b], writes=[pb])
        P.op("act", lambda e: e.activation(Lout, ps[:, :], AF.Exp, scale=-1.0), reads=[pb], writes=[Lout_b])
        P.op("act", lambda e: e.activation(Lout, Lout, AF.Ln, bias=one_c, scale=1.0), reads=[Lout_b, cf_b], writes=[Lout_b])

    seq_tiles = []
    t0 = 0
    for s in seqs:
        seq_tiles.append((t0, s // 128))
        t0 += s // 128

    for l in range(depth):
        load_layer_params(l)
        cur = X[0] if l == 0 else X[1 + (l - 1) % 2]
        nxt_i = 1 + l % 2
        last = (l == depth - 1)
        wkc, wkc_b = w_get(l * NIMG + KC)
        wv0, wv0_b = w_get(l * NIMG + VC0, hold=1)
        wv1, wv1_b = w_get(l * NIMG + VC1, hold=2)
        for (tb, ntile) in seq_tiles:
            P.op("pool", lambda e: e.memset(Sb[:, :], 0.0), writes=[Sb_b])
            cur16 = 0
            P.op("pool", lambda e, c=cur16: e.memset(Sb16[c][:, :], 0.0), writes=[Sb16_b[cur16]])
            order = list(reversed(range(ntile)))
            PS = [dict() for _ in range(ntile)]
            c16 = [0]

            def PP0(i):
                make_hT(l, tb + order[i], hT, hT_b[i % 4], (i % 4) * 128)

            def PP1(i):
                par = i % 2
                hb_, c0 = hT_b[i % 4], (i % 4) * 128
                PS[i]["k"] = tm_group(hT, hb_, c0, wkc, wkc_b, 0, 512, reserve=True)
                vt = tC[par]; vt_b = tC_b[par]
                for half, (wv, wvb) in enumerate(((wv0, wv0_b), (wv1, wv1_b))):
                    ps, pb = tm_group(hT, hb_, c0, wv, wvb, 0, 512)
                    P.op("act", lambda e, ps=ps, half=half, vt=vt: e.copy(vt[:, half * 512:(half + 1) * 512], ps[:, :]), reads=[pb], writes=[vt_b])
                ps, pb = bank()
                P.op("pe", [mm(ps[0:16, 0:128], wlr[:, k, 16:32], hT[:, k, c0:c0 + 128], k == 0, k == 15) for k in range(16)],
                     reads=[lp_b, hb_], writes=[pb])
                P.op("dve", lambda e, ps=ps: e.tensor_copy(lrT[par][0:16, 0:128], ps[0:16, 0:128]), reads=[pb], writes=[lrT_b[par]])
                gate_L(1, lrT[par][:, 0:128], lrT_b[par], tB[:, par * 512:(par + 1) * 512], tBh_b[par])

            def PP2(i):
                par = i % 2
                tile = tb + order[i]
                Lb = tB[:, par * 512:(par + 1) * 512]
                vt = tC[par]; vt_b = tC_b[par]
                ps, pb = bank()
                P.op("pe", mm(ps[:, :], M2[1], Lb), reads=[cf_b, tBh_b[par]], writes=[pb])
                P.op("act", lambda e, ps=ps: e.activation(gEk[1][:, :], ps[:, :], AF.Exp), reads=[pb], writes=[gEk_b[1]])
                ps, pb = bank()
                P.op("pe", [mm(ps[:, h:h + 1], Lb[:, h * 128:(h + 1) * 128], negc) for h in range(4)], reads=[cf_b, tBh_b[par]], writes=[pb])
                P.op("act", lambda e, ps=ps: e.activation(dec[:, 8:12], ps[:, 0:4], AF.Exp), reads=[pb], writes=[dec_b])
                psk, pbk = PS[i]["k"]
                P.op("dve", lambda e: e.tensor_tensor(kd[1][:, :], psk[:, :], gEk[1][:, :], ALU.mult), reads=[pbk, gEk_b[1]], writes=[kd_b[1]])
                reserved.discard(ps_b.index(pbk))
                cur16 = c16[0]
                P.dma("pool", st_d[tile], Sb16[cur16][:, :], reads=[Sb16_b[cur16]], writes=[st_b[tile]])
                for hp in range(2):
                    ps, pb = bank()
                    P.op("pe", [mm(ps[:, hh * 256:(hh + 1) * 256], kd[1][:, (2 * hp + hh) * 128:(2 * hp + hh + 1) * 128],
                                   vt[:, (2 * hp + hh) * 256:(2 * hp + hh + 1) * 256]) for hh in range(2)],
                         reads=[kd_b[1], vt_b], writes=[pb])
                    for hh in range(2):
                        h = 2 * hp + hh
                        P.op("dve", lambda e, ps=ps, h=h, hh=hh: e.scalar_tensor_tensor(
                            Sb[:, h * 256:(h + 1) * 256], Sb[:, h * 256:(h + 1) * 256], dec[:, 8 + h:9 + h],
                            ps[:, hh * 256:(hh + 1) * 256], ALU.mult, ALU.add), reads=[pb, dec_b, Sb_b], writes=[Sb_b])
                c16[0] ^= 1
                nx = c16[0]
                P.op("pool", lambda e: e.tensor_copy(Sb16[nx][:, :], Sb[:, :]), reads=[Sb_b], writes=[Sb16_b[nx]])

            skewed(ntile, [PP0, PP1, PP2], [0, 1, 2])

        stc = [0]
        pre0 = [False]
        for (tb, ntile) in seq_tiles:
            nst = ntile // 4
            P.op("pool", lambda e: e.memset(Sf[:, :], 0.0), writes=[Sf_b])
            P.op("pool", lambda e: e.memset(Sf16[:, :], 0.0), writes=[Sf16_b])
            for st in range(nst):
                t_first = tb + st * 4
                pos0 = st * 512
                has_next = st < nst - 1
                has_prev = st > 0
                if l + 1 < depth:
                    per = -(-NIMG // (NTOK // 512))
                    convert(l + 1, stc[0] * per, (stc[0] + 1) * per)
                    stc[0] += 1
                def stage0_items(st_, l=l, tb=tb, nst=nst):
                    tf = tb + st_ * 4
                    hn = st_ < nst - 1
                    items = [((lambda j=j: make_hT_A(l, tf + j)), (lambda j=j: make_hT_B(hT, hT_b[j], j * 128))) for j in range(4)]
                    if hn:
                        items.append(((lambda: make_hT_A(l, tf + 4)), (lambda: make_hT_B(hTh, hTh_b, 0))))

                    def ropes():
                        nr = 640 if hn else 512
                        P.dma("sp", ropeC[:, 0:nr], ropeC_d[:, st_ * 512:st_ * 512 + nr], writes=[rope_b])
                        P.dma("sp", ropeS[:, 0:nr], ropeS_d[:, st_ * 512:st_ * 512 + nr], writes=[rope_b])
                    items.append((ropes, lambda: None))
                    return items

                if not pre0[0]:
                    for ia, ib in stage0_items(st):
                        ia(); ib()
                pre0[0] = False
                D0 = (l == 0 and st == 0 and tb == 0)
                if D0:
                    dump("hT", hT[:, :, :].rearrange("p a b -> p (a b)"), hT_b, 8192)
                wva = [w_get(l * NIMG + VA0), w_get(l * NIMG + VA1, hold=1)]
                S1 = [None] * 4

                def A1(j):
                    pss = [tm_group(hT, hT_b, j * 128, wva[h][0], wva[h][1], 0, 512) for h in range(2)]
                    bst, bstb = statcol(12)
                    for h in range(2):
                        P.op("dve", lambda e, h=h, ps=pss[h][0], bst=bst: e.bn_stats(bst[:, h * 6:(h + 1) * 6], ps[:, :]),
                             reads=[pss[h][1]], writes=[bstb])
                    mv, mvb = statcol(2)
                    P.op("dve", lambda e, bst=bst, mv=mv: e.bn_aggr(mv, bst), reads=[bstb], writes=[mvb])
                    r, rb = rstd_from((mv[:, 1:2], mvb), 1.0, 1)
                    nm, nmb = statcol()
                    P.op("dve", lambda e, nm=nm, mv=mv, r=r: e.scalar_tensor_tensor(nm, mv[:, 0:1], -1.0, r, ALU.mult, ALU.mult),
                         reads=[mvb, rb], writes=[nmb])
                    for h in range(2):
                        P.op("act", lambda e, h=h, ps=pss[h][0], r=r, nm=nm: e.activation(
                            tA[:, h * 512:(h + 1) * 512], ps[:, :], AF.Identity, bias=nm, scale=r),
                            reads=[pss[h][1], rb, nmb], writes=[tA_b])
                    P.op("dve", lambda e: e.tensor_tensor(tA[:, :], tA[:, :], lnG[:, :], ALU.mult), reads=[tA_b, lp_b], writes=[tA_b])
                    vn = tC[j % 2]; vnb = tC_b[j % 2]
                    P.op("dve", lambda e, vn=vn: e.tensor_tensor(vn[:, :], tA[:, :], lnB[:, :], ALU.add), reads=[tA_b, lp_b], writes=[vnb])
                    S1[j] = (vn, vnb)

                def A2(j):
                    vn, vnb = S1[j]
                    for hb in range(2):
                        ps, pb = bank()
                        P.op("pe", [mm(ps[:, c * 128:(c + 1) * 128], vn[:, (4 * hb + c) * 128:(4 * hb + c + 1) * 128],
                                       wsT[:, (4 * hb + c) // 2, :]) for c in range(4)], reads=[vnb, lp_b], writes=[pb])
                        for gg in range(2):
                            g = hb * 2 + gg
                            P.op("dve", lambda e, ps=ps, gg=gg, g=g, hb=hb: e.tensor_tensor(
                                yT[:, 4 * hb + 2 * gg:4 * hb + 2 * gg + 2, j * 128:(j + 1) * 128],
                                ps[:, gg * 256:(gg + 1) * 256].rearrange("p (a b) -> p a b", b=128),
                                bs_bc[:, g * 128:(g + 1) * 128].unsqueeze(1).to_broadcast([128, 2, 128]), ALU.add),
                                reads=[pb, lp_b], writes=[yT_b[j]])

                skewed(4, [A1, A2], [0, 1])
                for half, img in enumerate((UA0, UA1)):
                    wt, wb_ = w_get(l * NIMG + img)
                    for bl in range(4):
                        ps, pb = fm_block(wt, wb_, bl, 16, hT, hT_b)
                        P.op("act", lambda e, ps=ps, b=half * 4 + bl: e.copy(uT[:, b, :], ps[:, :]), reads=[pb], writes=[uT_b])
                zc = 0
                for half, img in enumerate((ZA0, ZA1)):
                    wt, wb_ = w_get(l * NIMG + img)
                    for bl in range(4):
                        b = half * 4 + bl
                        ps, pb = fm_block(wt, wb_, bl, 16, hT, hT_b)
                        zz = zc % 2; zc += 1
                        P.op("act", lambda e, ps=ps, zz=zz: e.activation(szt[zz][:, :], ps[:, :], AF.Silu), reads=[pb], writes=[szt_b[zz]])
                        P.op("dve", lambda e, b=b, zz=zz: e.tensor_tensor(uT[:, b, :], uT[:, b, :], szt[zz][:, :], ALU.mult),
                             reads=[uT_b, szt_b[zz]], writes=[uT_b])
                        P.op("dve", lambda e, b=b: e.tensor_tensor(yT[:, b, :], yT[:, b, :], uT[:, b, :], ALU.mult),
                             reads=[uT_b] + yT_b, writes=yT_b)
                if D0:
                    dump("ya", yT[:, :, :].rearrange("p a b -> p (a b)"), yT_b, 4096)
                merge_term(P, l, 0, w_get, fm_block, yT, yT_b, hT, hT_b, merged, mrg_b, bank, (PA0, GA0, GA1, PA1, GA2, GA3), tB, tBh_b)

                if has_prev:
                    P.op("pool", lambda e: e.tensor_copy(krT[:, :, 0:128], krT[:, :, 512:640]), reads=[krT_b[4]], writes=[krT_b[0]])
                    P.op("pool", lambda e: e.tensor_copy(vtB[:, 0, :], vtB[:, 4, :]), reads=[vtB_b[4]], writes=[vtB_b[0]])
                else:
                    P.op("pool", lambda e: e.memset(krT[:, :, 0:128], 0.0), writes=[krT_b[0]])
                    P.op("pool", lambda e: e.memset(vtB[:, 0, :], 0.0), writes=[vtB_b[0]])
                if not has_next:
                    P.op("pool", lambda e: e.memset(krT[:, :, 640:768], 0.0), writes=[krT_b[5]])
                    P.op("pool", lambda e: e.memset(vtB[:, 5, :], 0.0), writes=[vtB_b[5]])

                rctr = [0]

                def rope_evac(ps, pb, ncols, rc0, dst, dst_bufs):
                    if rctr[0] % 2 == 0:
                        T, Tb = tA, [tA_b]
                    else:
                        T, Tb = tB, tBh_b
                    rctr[0] += 1
                    P.op("dve", lambda e: e.tensor_tensor(T[:, 0:ncols], ps[:, 0:ncols], ropeC[:, rc0:rc0 + ncols], ALU.mult),
                         reads=[pb, rope_b], writes=Tb)
                    P.op("dve", lambda e: e.tensor_tensor(T[0:64, 512:512 + ncols], ps[64:128, 0:ncols], ropeS[64:128, rc0:rc0 + ncols], ALU.mult),
                         reads=[pb, rope_b], writes=Tb)
                    P.op("dve", lambda e: e.tensor_tensor(T[64:128, 512:512 + ncols], ps[0:64, 0:ncols], ropeS[0:64, rc0:rc0 + ncols], ALU.mult),
                         reads=[pb, rope_b], writes=Tb)
                    P.op("dve", lambda e: e.tensor_tensor(dst, T[:, 0:ncols], T[:, 512:512 + ncols], ALU.add), reads=Tb, writes=dst_bufs)

                for half, img in enumerate((QB0, QB1)):
                    wt, wb_ = w_get(l * NIMG + img)
                    for bl in range(4):
                        ps, pb = fm_block(wt, wb_, bl, 16, hT, hT_b)
                        rope_evac(ps, pb, 512, 0, uT[:, half * 4 + bl, :], [uT_b])
                wt, wb_ = w_get(l * NIMG + KVB)
                for kv in range(2):
                    ps, pb = fm_block(wt, wb_, kv, 16, hT, hT_b)
                    rope_evac(ps, pb, 512, 0, krT[:, kv, 128:640], krT_b[1:5])
                    if has_next:
                        ps, pb = fm_block(wt, wb_, kv, 16, hTh, hTh_b, ncols=128)
                        rope_evac(ps, pb, 128, 512, krT[:, kv, 640:768], [krT_b[5]])
                for j in range(5 if has_next else 4):
                    src, srcb, c0 = (hT, hT_b, j * 128) if j < 4 else (hTh, hTh_b, 0)
                    ps, pb = tm_group(src, srcb, c0, wt, wb_, 256, 256)
                    P.op("act", lambda e, ps=ps, j=j: e.copy(vtB[:, j + 1, :], ps[:, 0:256]), reads=[pb], writes=[vtB_b[j + 1]])
                scale = 128.0 ** -0.5
                S = [dict() for _ in range(32)]

                def stA(i):
                    j, h = divmod(i, 8)
                    kv = h // 4
                    tl = st * 4 + j
                    mk = masks[1] if tl == 0 else (masks[2] if tl == ntile - 1 else masks[0])
                    ps, pb = bank()
                    S[i]["ps"], S[i]["pb"] = ps, pb
                    P.op("pe", [mm(ps[:, 0:384], ident, mk, True, False),
                                mm(ps[:, 0:384], uT[:, h, j * 128:(j + 1) * 128], krT[:, kv, j * 128:j * 128 + 384], False, True)],
                         reads=[cbf_b, uT_b] + krT_b[j:j + 3], writes=[pb])

                def stB(i):
                    j, h = divmod(i, 8)
                    ps, pb = S[i]["ps"], S[i]["pb"]
                    pp = i % 3
                    mx, mxb = statcol(4)
                    P.op("dve", lambda e: e.reduce_max(mx[:, 0:1], ps[:, 0:384], AX.X), reads=[pb], writes=[mxb])
                    P.op("dve", lambda e: e.tensor_scalar(mx[:, 1:2], mx[:, 0:1], -scale, nsink[:, h:h + 1], ALU.mult, ALU.min),
                         reads=[mxb, lp_b], writes=[mxb])
                    P.op("act", lambda e: e.activation(pbuf[pp][:, :], ps[:, 0:384], AF.Exp, bias=mx[:, 1:2], scale=scale,
                                                       accum_out=mx[:, 2:3]), reads=[pb, mxb], writes=[pbuf_b[pp], mxb])
                    P.op("act", lambda e: e.activation(mx[:, 3:4], mx[:, 1:2], AF.Exp, bias=sink_bc[:, h:h + 1], scale=1.0),
                         reads=[mxb, lp_b], writes=[mxb])
                    P.op("dve", lambda e: e.tensor_tensor(mx[:, 2:3], mx[:, 2:3], mx[:, 3:4], ALU.add), reads=[mxb], writes=[mxb])
                    P.op("dve", lambda e: e.reciprocal(mx[:, 0:1], mx[:, 2:3]), reads=[mxb], writes=[mxb])
                    if i % 2:
                        P.op("pool", lambda e: e.tensor_scalar(Dg[pp][:, :], ident, mx[:, 0:1], None, ALU.mult),
                             reads=[mxb, cbf_b], writes=[Dg_b[pp]])
                    else:
                        P.op("act", lambda e: e.activation(Dg[pp][:, :], ident, AF.Copy, scale=mx[:, 0:1]),
                             reads=[mxb, cbf_b], writes=[Dg_b[pp]])

                def stC(i):
                    pp = i % 3
                    ps2, pb2 = bank()
                    S[i]["ps2"], S[i]["pb2"] = ps2, pb2
                    P.op("pe", [mm(ps2[:, c * 128:(c + 1) * 128], pbuf[pp][:, c * 128:(c + 1) * 128], Dg[pp][:, :]) for c in range(3)],
                         reads=[pbuf_b[pp], Dg_b[pp]], writes=[pb2])

                def stD(i):
                    ps2, pb2 = S[i]["ps2"], S[i]["pb2"]
                    if i % 2:
                        P.op("act", lambda e: e.copy(pT[i % 2][:, :], ps2[:, 0:384]), reads=[pb2], writes=[pT_b[i % 2]])
                    else:
                        P.op("dve", lambda e: e.tensor_copy(pT[i % 2][:, :], ps2[:, 0:384]), reads=[pb2], writes=[pT_b[i % 2]])

                def stE(i):
                    j, h = divmod(i, 8)
                    kv = h // 4
                    ob, obb = bank()
                    P.op("pe", [mm(ob[:, 0:128], vtB[:, j + c, kv * 128:(kv + 1) * 128],
                                   pT[i % 2][:, c * 128:(c + 1) * 128], c == 0, c == 2) for c in range(3)],
                         reads=[pT_b[i % 2]] + vtB_b[j:j + 3], writes=[obb])
                    P.op("dve", lambda e: e.tensor_copy(yT[:, h, j * 128:(j + 1) * 128], ob[:, 0:128]), reads=[obb], writes=[yT_b[j]])

                skewed(32, [stA, stB, stC, stD, stE], [0, 0, 2, 2, 3])
                zc = 0
                for half, img in enumerate((ZB0, ZB1)):
                    wt, wb_ = w_get(l * NIMG + img)
                    for bl in range(4):
                        b = half * 4 + bl
                        ps, pb = fm_block(wt, wb_, bl, 16, hT, hT_b)
                        zz = zc % 2; zc += 1
                        P.op("act", lambda e, ps=ps, zz=zz: e.activation(szt[zz][:, :], ps[:, :], AF.Silu), reads=[pb], writes=[szt_b[zz]])
                        P.op("dve", lambda e, b=b, zz=zz: e.tensor_tensor(yT[:, b, :], yT[:, b, :], szt[zz][:, :], ALU.mult),
                             reads=yT_b + [szt_b[zz]], writes=yT_b)
                if D0:
                    dump("m0", merged[:, :, :].rearrange("p a b -> p (a b)"), mrg_b, 8192)
                    dump("yb", yT[:, :, :].rearrange("p a b -> p (a b)"), yT_b, 4096)
                merge_term(P, l, 1, w_get, fm_block, yT, yT_b, hT, hT_b, merged, mrg_b, bank, (PB0, GB0, GB1, PB1, GB2, GB3), tB, tBh_b)

                wv = [w_get(l * NIMG + VC0), w_get(l * NIMG + VC1, hold=1)]
                for j in range(4):
                    for half in range(2):
                        ps, pb = tm_group(hT, hT_b, j * 128, wv[half][0], wv[half][1], 0, 512)
                        P.op("act", lambda e, ps=ps, j=j, half=half: e.copy(vtC[:, j, half * 512:(half + 1) * 512], ps[:, :]),
                             reads=[pb], writes=[uT_b])
                for d in range(2):
                    ps, pb = bank()
                    P.op("pe", [mm(ps[0:16, :], wlr[:, k, d * 16:(d + 1) * 16], hT[:, k, :], k == 0, k == 15) for k in range(16)],
                         reads=[lp_b, hT_b], writes=[pb])
                    P.op("dve", lambda e, ps=ps, d=d: e.tensor_copy(lrT[d][0:16, :], ps[0:16, :]), reads=[pb], writes=[lrT_b[d]])
                wq, wq_b = w_get(l * NIMG + QC)
                wk, wk_b = w_get(l * NIMG + KC, hold=1)
                qscale = 128.0 ** -0.5
                def G1(j):
                    tile = t_first + j
                    par = j % 2
                    cs = slice(j * 128, (j + 1) * 128)
                    P.dma("sp", Sb16[par][:, :], st_d[tile], reads=[st_b[tile]], writes=[Sb16_b[par]])
                    for d in range(2):
                        gate_L(d, lrT[d][:, cs], lrT_b[d], tB[:, d * 512:(d + 1) * 512], tBh_b[d])
                    psq, pbq = bank()
                    P.op("pe", [mm(psq[:, h * 128:(h + 1) * 128], wq[:, k, h * 128:(h + 1) * 128], hT[:, k, cs], k == 0, k == 15)
                                for h in range(4) for k in range(16)], reads=[wq_b, hT_b], writes=[pbq])
                    psk, pbk = bank()
                    P.op("pe", [mm(psk[:, h * 128:(h + 1) * 128], wk[:, k, h * 128:(h + 1) * 128], hT[:, k, cs], k == 0, k == 15)
                                for h in range(4) for k in range(16)], reads=[wk_b, hT_b], writes=[pbk])
                    for d in range(2):
                        L = tB[:, d * 512:(d + 1) * 512]
                        qd, qdb = qe[2 * par + d], qe_b[2 * par + d]
                        ad, adb = Am[2 * par + d], Am_b[2 * par + d]
                        ps, pb = bank()
                        P.op("pe", [mm(ps[:, h * 128:(h + 1) * 128], L[:, h * 128:(h + 1) * 128], M1[d]) for h in range(4)],
                             reads=[tBh_b[d], cf_b], writes=[pb])
                        P.op("act", lambda e, ps=ps, d=d: e.activation(gE[d][:, :], ps[:, :], AF.Exp), reads=[pb], writes=[gE_b[d]])
                        P.op("act", lambda e, ps=ps, d=d: e.activation(gEi[d][:, :], ps[:, :], AF.Exp, scale=-1.0), reads=[pb], writes=[gEi_b[d]])
                        if d == 0:
                            P.op("act", lambda e, ps=ps: e.activation(dec[:, 4 * par:4 * par + 4], ps[:, :].rearrange("p (h c) -> p h c", c=128)[:, :, 127], AF.Exp),
                                 reads=[pb], writes=[decm_b[par]])
                        P.op("dve", lambda e, d=d, qd=qd: e.scalar_tensor_tensor(qd[:, :], psq[:, :], qscale, gE[d][:, :], ALU.mult, ALU.mult),
                             reads=[pbq, gE_b[d]], writes=[qdb])
                        P.op("dve", lambda e, d=d: e.tensor_tensor(ke[d][:, :], psk[:, :], gEi[d][:, :], ALU.mult),
                             reads=[pbk, gEi_b[d]], writes=[ke_b[d]])
                        ps, pb = bank()
                        P.op("pe", [mm(ps[:, h * 128:(h + 1) * 128], ke[d][:, h * 128:(h + 1) * 128], qd[:, h * 128:(h + 1) * 128]) for h in range(4)],
                             reads=[ke_b[d], qdb], writes=[pb])
                        P.op("dve", lambda e, ps=ps, d=d, ad=ad: e.tensor_tensor(ad[:, :], ps[:, :], gmask[d], ALU.mult), reads=[pb, cbf_b], writes=[adb])
                    ps, pb = bank()
                    P.op("pe", mm(ps[:, :], M2[0], tB[:, 0:512]), reads=[tBh_b[0], cf_b], writes=[pb])
                    P.op("act", lambda e, ps=ps: e.activation(gEk[0][:, :], ps[:, :], AF.Exp), reads=[pb], writes=[gEk_b[0]])
                    ps, pb = tm_group(hT, hT_b, j * 128, wk, wk_b, 0, 512)
                    P.op("dve", lambda e, ps=ps: e.tensor_tensor(kd[par][:, :], ps[:, :], gEk[0][:, :], ALU.mult), reads=[pb, gEk_b[0]], writes=[kd_b[par]])

                def G2(j):
                    par = j % 2
                    obanks = [bank(), bank()]
                    for hp in range(2):
                        ob, obb = obanks[hp]
                        fns = []
                        for hh in range(2):
                            h = 2 * hp + hh
                            for vb in range(2):
                                oc = ob[:, (hh * 2 + vb) * 128:(hh * 2 + vb + 1) * 128]
                                vsl = slice(h * 256 + vb * 128, h * 256 + (vb + 1) * 128)
                                hs = slice(h * 128, (h + 1) * 128)
                                fns.append(mm(oc, vtC[:, j, vsl], Am[2 * par][:, hs], True, False))
                                fns.append(mm(oc, vtC[:, j, vsl], Am[2 * par + 1][:, hs], False, False))
                                fns.append(mm(oc, Sf16[:, vsl], qe[2 * par][:, hs], False, False))
                                fns.append(mm(oc, Sb16[par][:, vsl], qe[2 * par + 1][:, hs], False, True))
                        P.op("pe", fns, reads=[uT_b, Am_b[2 * par], Am_b[2 * par + 1], Sf16_b, Sb16_b[par], qe_b[2 * par], qe_b[2 * par + 1]], writes=[obb])
                    for hp in range(2):
                        ps, pb = bank()
                        P.op("pe", [mm(ps[:, hh * 256:(hh + 1) * 256], kd[par][:, (2 * hp + hh) * 128:(2 * hp + hh + 1) * 128],
                                       vtC[:, j, (2 * hp + hh) * 256:(2 * hp + hh + 1) * 256]) for hh in range(2)],
                             reads=[kd_b[par], uT_b], writes=[pb])
                        for hh in range(2):
                            h = 2 * hp + hh
                            P.op("dve", lambda e, ps=ps, h=h, hh=hh: e.scalar_tensor_tensor(
                                Sf[:, h * 256:(h + 1) * 256], Sf[:, h * 256:(h + 1) * 256], dec[:, 4 * par + h:4 * par + h + 1],
                                ps[:, hh * 256:(hh + 1) * 256], ALU.mult, ALU.add), reads=[pb, decm_b[par], Sf_b], writes=[Sf_b])
                    P.op("pool", lambda e: e.tensor_copy(Sf16[:, :], Sf[:, :]), reads=[Sf_b], writes=[Sf16_b])
                    for hp in range(2):
                        ob, obb = obanks[hp]
                        P.op("act", lambda e, ob=ob, hp=hp: e.activation(sq[:, hp * 512:(hp + 1) * 512], ob[:, :], AF.Square), reads=[obb], writes=[sq_b])
                    ps, pb = bank()
                    P.op("pe", [mm(ps[:, h * 128:(h + 1) * 128], ones_bf, sq[:, (h * 2 + vb) * 128:(h * 2 + vb + 1) * 128], vb == 0, vb == 1)
                                for h in range(4) for vb in range(2)], reads=[sq_b, cbf_b], writes=[pb])
                    P.op("act", lambda e, ps=ps: e.activation(rstd[:, :], ps[:, :], AF.Ln, bias=eps_c, scale=1.0 / 256), reads=[pb, cf_b], writes=[rstd_b])
                    P.op("act", lambda e: e.activation(rstd[:, :], rstd[:, :], AF.Exp, scale=-0.5), reads=[rstd_b], writes=[rstd_b])
                    for hp in range(2):
                        ob, obb = obanks[hp]
                        for vb in range(2):
                            P.op("dve", lambda e, ob=ob, hp=hp, vb=vb, j=j: e.scalar_tensor_tensor(
                                yT[:, 4 * hp:4 * hp + 4, j * 128:(j + 1) * 128].rearrange("p (h v) c -> p h v c", v=2)[:, :, vb, :],
                                ob[:, :].rearrange("p (h v c) -> p h v c", v=2, c=128)[:, :, vb, :], cng[:, vb:vb + 1],
                                rstd[:, hp * 256:(hp + 1) * 256].rearrange("p (h c) -> p h c", c=128), ALU.mult, ALU.mult),
                                reads=[obb, rstd_b, lp_b], writes=[yT_b[j]])

                skewed(4, [G1, G2], [0, 1])
                zc = 0
                for half, img in enumerate((ZC0, ZC1)):
                    wt, wb_ = w_get(l * NIMG + img)
                    for bl in range(4):
                        b = half * 4 + bl
                        ps, pb = fm_block(wt, wb_, bl, 16, hT, hT_b)
                        zz = zc % 2; zc += 1
                        P.op("act", lambda e, ps=ps, zz=zz: e.activation(szt[zz][:, :], ps[:, :], AF.Silu), reads=[pb], writes=[szt_b[zz]])
                        P.op("dve", lambda e, b=b, zz=zz: e.tensor_tensor(yT[:, b, :], yT[:, b, :], szt[zz][:, :], ALU.mult),
                             reads=yT_b + [szt_b[zz]], writes=yT_b)
                if D0:
                    dump("yc", yT[:, :, :].rearrange("p a b -> p (a b)"), yT_b, 4096)
                merge_term(P, l, 2, w_get, fm_block, yT, yT_b, hT, hT_b, merged, mrg_b, bank, (PC0, GC0, GC1, PC1, GC2, GC3), tB, tBh_b)

                if D0:
                    dump("m2", merged[:, :, :].rearrange("p a b -> p (a b)"), mrg_b, 8192)
                srcb = X_b[0] if l == 0 else X_b[1 + (l - 1) % 2]
                xc = [0]
                s5 = []
                for c in range(4):
                    def getw(c=c):
                        s5w[0] = w_get(l * NIMG + WO0 + c)
                    for j in range(4):
                        def grp(c=c, j=j, getw=getw):
                            if j == 0:
                                getw()
                            wo, wo_b = s5w[0]
                            tile = t_first + j
                            rows = slice(tile * 128, (tile + 1) * 128)
                            cols = slice(c * 512, (c + 1) * 512)
                            xx = xc[0] % 2; xc[0] += 1
                            P.dma("sp", xch[xx][:, :], cur[rows, cols], reads=[srcb[tile]], writes=[xch_b[xx]])
                            ps, pb = bank()
                            P.op("pe", [mm(ps[:, :], merged[:, k, j * 128:(j + 1) * 128], wo[:, k, :], k == 0, k == 15) for k in range(16)],
                                 reads=[wo_b] + mrg_b, writes=[pb])
                            P.op("dve", lambda e: e.tensor_tensor(xch[xx][:, :], ps[:, :], xch[xx][:, :], ALU.add),
                                 reads=[pb, xch_b[xx]], writes=[xch_b[xx]])
                            P.dma("pool", X[nxt_i][rows, cols], xch[xx][:, :], reads=[xch_b[xx]], writes=[X_b[nxt_i][tile]])
                        s5.append(grp)
                s5w = [None]
                nxt0 = stage0_items(st + 1) if has_next else []
                gi = 0
                if nxt0:
                    nxt0[0][0]()
                for ii, (ia, ib) in enumerate(nxt0):
                    for _ in range(3):
                        if gi < len(s5):
                            s5[gi](); gi += 1
                    ib()
                    if ii + 1 < len(nxt0):
                        nxt0[ii + 1][0]()
                while gi < len(s5):
                    s5[gi](); gi += 1
                pre0[0] = has_next

    fin = X[1 + (depth - 1) % 2]
    fin_b = X_b[1 + (depth - 1) % 2]
    P.dma("sp", tA[:, :], fg_d[0, 0:1024].partition_broadcast(128), writes=[tA_b])
    P.dma("sp", tB[:, :], fg_d[0, 1024:2048].partition_broadcast(128), writes=tBh_b)
    for tile in range(NT):
        p = tile % 2
        xt, xtb = xst[p], xst_b[p]
        rows = slice(tile * 128, (tile + 1) * 128)
        P.dma("sp", xt[:, :], fin[rows, :], reads=[fin_b[tile]], writes=[xtb])
        ss, ssb = statcol()
        P.op("act", lambda e, xt=xt, ss=ss: e.activation(hst[:, :], xt[:, :], AF.Square, accum_out=ss), reads=[xtb], writes=[hst_b, ssb])
        r, rb = rstd_from((ss, ssb), 1.0 / D, 1)
        P.op("dve", lambda e, xt=xt, r=r: e.scalar_tensor_tensor(xt[:, 0:1024], xt[:, 0:1024], r, tA[:, :], ALU.mult, ALU.mult),
             reads=[xtb, rb, tA_b], writes=[xtb])
        P.op("dve", lambda e, xt=xt, r=r: e.scalar_tensor_tensor(xt[:, 1024:2048], xt[:, 1024:2048], r, tB[:, :], ALU.mult, ALU.mult),
             reads=[xtb, rb] + tBh_b, writes=[xtb])
        P.dma("pool", y_out[rows, :], xt[:, :], reads=[xtb], writes=[Y_b[tile]])
    P.finish()
    P.run_block()
    es.close()
    return nc


def merge_term(P, l, which, w_get, fm_block, yT, yT_b, hT, hT_b, merged, mrg_b, bank, imgs, tB, tBh_b):
    PA, G0, G1, PB_, G2, G3 = imgs
    order = [(PA, (G0, G1)), (PB_, (G2, G3))]
    for half, (pimg, gimgs) in enumerate(order):
        wp3, wp_b = w_get(l * NIMG + pimg, 1024)
        for gi, gimg in enumerate(gimgs):
            wg3, wg_b = w_get(l * NIMG + gimg, hold=gi + 1)
            for bl in range(4):
                m = half * 8 + gi * 4 + bl
                psg, pgb = fm_block(wg3, wg_b, bl, 16, hT, hT_b)
                sg = tB[:, (m % 2) * 512:(m % 2 + 1) * 512]
                sgb = tBh_b[m % 2]
                P.op("act", lambda e, psg=psg, sg=sg: e.activation(sg, psg[:, :], AF.Sigmoid), reads=[pgb], writes=[sgb])
                psp, ppb = bank()
                P.op("pe", [mm(psp[:, :], wp3[:, k, (gi * 4 + bl) * 128:(gi * 4 + bl + 1) * 128], yT[:, k, :], k == 0, k == 7) for k in range(8)],
                     reads=[wp_b] + yT_b, writes=[ppb])
                if which == 0:
                    P.op("dve", lambda e, psp=psp, sg=sg, m=m: e.tensor_tensor(merged[:, m, :], psp[:, :], sg, ALU.mult),
                         reads=[ppb, sgb], writes=[mrg_b[m]])
                else:
                    P.op("dve", lambda e, psp=psp, sg=sg: e.tensor_tensor(sg, psp[:, :], sg, ALU.mult), reads=[ppb, sgb], writes=[sgb])
                    P.op("pool", lambda e, sg=sg, m=m: e.tensor_tensor(merged[:, m, :], merged[:, m, :], sg, ALU.add),
                         reads=[sgb, mrg_b[m]], writes=[mrg_b[m]])


def _img_in(w, c0, n=512):
    a = w[:, c0:c0 + n].reshape(16, 128, n).transpose(1, 0, 2)
    return a


def _pack_layer(w_in, w_pa, w_pb, w_pc, w_out):
    imgs = np.empty((NIMG, 128, 8192), np.float32)

    def put_in(i, c0):
        imgs[i] = _img_in(w_in, c0).reshape(128, 8192)

    def put_proj(i, wp, c0):
        imgs[i] = wp[:, c0:c0 + 1024].reshape(8, 128, 1024).transpose(1, 0, 2).reshape(128, 8192)

    put_in(VA0, OFF["va"]); put_in(VA1, OFF["va"] + 512)
    put_in(UA0, OFF["ua"]); put_in(UA1, OFF["ua"] + 512)
    put_in(ZA0, OFF["za"]); put_in(ZA1, OFF["za"] + 512)
    put_in(QB0, OFF["qb"]); put_in(QB1, OFF["qb"] + 512)
    put_in(KVB, OFF["kb"])
    put_in(ZB0, OFF["zb"]); put_in(ZB1, OFF["zb"] + 512)
    put_in(QC, OFF["qc"]); put_in(KC, OFF["kc"])
    put_in(VC0, OFF["vc"]); put_in(VC1, OFF["vc"] + 512)
    put_in(ZC0, OFF["zc"]); put_in(ZC1, OFF["zc"] + 512)
    for name, ids in (("ga", (GA0, GA1, GA2, GA3)), ("gb", (GB0, GB1, GB2, GB3)), ("gc", (GC0, GC1, GC2, GC3))):
        for i, img in enumerate(ids):
            put_in(img, OFF[name] + 512 * i)
    put_proj(PA0, w_pa, 0); put_proj(PA1, w_pa, 1024)
    put_proj(PB0, w_pb, 0); put_proj(PB1, w_pb, 1024)
    put_proj(PC0, w_pc, 0); put_proj(PC1, w_pc, 1024)
    for c in range(4):
        imgs[WO0 + c] = w_out[:, c * 512:(c + 1) * 512].reshape(16, 128, 512).transpose(1, 0, 2).reshape(128, 8192)
    return imgs


def _consts(smax):
    i = np.arange(128)[:, None]
    j = np.arange(384)[None, :]
    ok = (j >= i) & (j <= i + 256)
    m0 = np.where(ok, 0.0, NEG)
    m1 = np.where(ok & (j >= 128), 0.0, NEG)
    m2 = np.where(ok & (j < 256), 0.0, NEG)
    mm_, cc = np.arange(128)[:, None], np.arange(128)[None, :]
    gmf = np.tile((mm_ <= cc).astype(np.float32), (1, 4))
    gmb = np.tile((mm_ > cc).astype(np.float32), (1, 4))
    cbf = np.concatenate([np.eye(128), m0, m1, m2, gmf, gmb, np.ones((128, 128))], axis=1).astype(np.float32)
    s = -1.0 / 16.0
    M1f = (mm_ <= cc) * s
    M1b = (mm_ >= cc) * s
    M2f = (mm_ > cc) * s
    M2b = (mm_ < cc) * s
    cf = np.concatenate([M1f, M1b, M2f, M2b, np.full((128, 1), s), np.ones((128, 1)), np.full((128, 1), EPS),
                         np.zeros((128, 1))], axis=1).astype(np.float32)
    half = 64
    inv = (10000.0 ** (-np.arange(half, dtype=np.float32) * 2.0 / 128)).astype(np.float32)
    ang = np.arange(smax, dtype=np.float32)[None, :] * inv[:, None]
    cos = np.cos(ang).astype(np.float32)
    sin = np.sin(ang).astype(np.float32)
    ropeC = np.concatenate([cos, cos], axis=0)
    ropeS = np.concatenate([sin, -sin], axis=0)
    return cbf, cf, np.ascontiguousarray(ropeC), np.ascontiguousarray(ropeS)


def _run(xs_per_core, seqs, depth, norm_g, w_in, a_ln_g, a_ln_b, a_ws, a_bs, b_sink, c_wf, c_bf, c_wb, c_bb,
         c_norm_g, w_pa, w_pb, w_pc, w_out, final_g):
    smax = max(seqs)
    f = lambda a: np.ascontiguousarray(np.asarray(a, dtype=np.float32))
    w_in, w_pa, w_pb, w_pc, w_out = map(f, (w_in, w_pa, w_pb, w_pc, w_out))
    wsl = np.concatenate([_pack_layer(w_in[l], w_pa[l], w_pb[l], w_pc[l], w_out[l]) for l in range(depth)], axis=0)
    wlr = np.stack([_img_in(w_in[l], OFF["lrf"], 32).reshape(128, 512) for l in range(depth)])
    wg = np.stack([np.concatenate([f(w)[l], f(b)[l][None, :]], axis=0) for l in range(depth) for (w, b) in ((c_wf, c_bf), (c_wb, c_bb))])
    wsT = np.stack([f(a_ws)[l].transpose(2, 0, 1).reshape(128, 512) for l in range(depth)])
    cbf, cf, ropeC, ropeS = _consts(smax)
    common = dict(wsl=wsl, wlr=f(wlr), wg=f(wg), wsT=f(wsT), norm_gT=np.ascontiguousarray(f(norm_g)[:depth].reshape(depth, 16, 128).transpose(0, 2, 1)), a_ln_g=f(a_ln_g)[:depth],
                  a_ln_b=f(a_ln_b)[:depth], a_bs=f(a_bs)[:depth].reshape(depth, 512), b_sink=f(b_sink)[:depth],
                  c_norm_gT=np.ascontiguousarray(f(c_norm_g)[:depth].reshape(depth, 2, 128).transpose(0, 2, 1)), final_g=f(final_g).reshape(1, D), cbf=cbf, cf=cf, ropeC=ropeC, ropeS=ropeS)
    nc = build(seqs, depth, smax, dbg=_DBGFLAG[0])
    in_maps = [dict(common, x=f(x)) for x in xs_per_core]
    res = run_bass_kernel_spmd(nc, in_maps, core_ids=list(range(len(in_maps))))
    if _DBGFLAG[0]:
        _DBGOUT.append(res.results[0])
    return [r["y"] for r in res.results]


_DBGFLAG = [False]
_DBGOUT = []


def kernel(x_prompt, x_sample, norm_g, w_in, a_ln_g, a_ln_b, a_ws, a_bs, b_sink, c_wf, c_bf, c_wb, c_bb,
           c_norm_g, w_pa, w_pb, w_pc, w_out, final_g):
    x_prompt = np.asarray(x_prompt, dtype=np.float32)
    x_sample = np.asarray(x_sample, dtype=np.float32)
    SP, SS = x_prompt.shape[1], x_sample.shape[1]
    xs = [np.concatenate([x_sample[c], x_prompt[c % 4]], axis=0) for c in range(8)]
    ys = _run(xs, [SS, SP], 4, norm_g, w_in, a_ln_g, a_ln_b, a_ws, a_bs, b_sink, c_wf, c_bf, c_wb, c_bb,
              c_norm_g, w_pa, w_pb, w_pc, w_out, final_g)
    y_sample = np.stack([ys[c][:SS] for c in range(8)])
    y_prompt = np.stack([ys[c][SS:SS + SP] for c in range(4)])
    return (y_prompt, y_sample)
```

```python
import numpy as np
from contextlib import ExitStack
import concourse.bass as bass
import concourse.mybir as mybir
from concourse.bass_utils import run_bass_kernel_spmd

F32 = mybir.dt.float32
BF16 = mybir.dt.bfloat16
AF = mybir.ActivationFunctionType
ALU = mybir.AluOpType
AX = mybir.AxisListType

D = 2048
EPS = 1e-6
NEG = -30000.0
ENGS = ("pe", "act", "dve", "pool", "sp")

OFF = dict(ua=0, va=1024, za=2048, qb=3072, kb=4096, vb=4352, zb=4608, qc=5632, kc=6144,
           vc=6656, zc=7680, lrf=8704, lrb=8720, ga=8736, gb=10784, gc=12832)
NIMG = 39
(VA0, VA1, UA0, UA1, ZA0, ZA1, PA0, GA0, GA1, PA1, GA2, GA3,
 QB0, QB1, KVB, ZB0, ZB1, PB0, GB0, GB1, PB1, GB2, GB3,
 QC, KC, VC0, VC1, ZC0, ZC1, PC0, GC0, GC1, PC1, GC2, GC3,
 WO0, WO1, WO2, WO3) = range(NIMG)
MAIN_ORDER = [VA0, VA1, UA0, UA1, ZA0, ZA1, GA0, PA0, GA1, GA2, PA1, GA3,
              QB0, QB1, KVB, ZB0, ZB1, GB0, PB0, GB1, GB2, PB1, GB3,
              VC0, VC1, QC, KC, ZC0, ZC1, GC0, PC0, GC1, GC2, PC1, GC3,
              WO0, WO1, WO2, WO3]
PRE_ORDER = [KC, VC0, VC1]


class Buf:
    __slots__ = ("name", "writers", "readers")

    def __init__(self, name=""):
        self.name = name
        self.writers = {}
        self.readers = {}


class Prog:
    NDMA = 12

    def __init__(self, nc):
        self.nc = nc
        self.q = {e: [] for e in ENGS}
        self.esem = {e: nc.alloc_semaphore(name=f"s_{e}") for e in ENGS}
        self.ecnt = {e: 0 for e in ENGS}
        self.dsem = {e: [nc.alloc_semaphore(name=f"d_{e}{i}") for i in range(self.NDMA)]
                     for e in ("sp", "pool")}
        self.dcnt = {e: [0] * self.NDMA for e in self.dsem}
        self.dnext = {e: 0 for e in self.dsem}
        self.waited = {e: {} for e in ENGS}
        self.semobj = {}
        for e in ENGS:
            self.semobj[("e", e)] = self.esem[e]
        for e in self.dsem:
            for i, s in enumerate(self.dsem[e]):
                self.semobj[("d", e, i)] = s
        self.ninst = 0

    def _emit_waits(self, eng, evs):
        w = self.waited[eng]
        for key, val in evs.items():
            if eng == "pe" and key == ("e", "pe"):
                continue
            if w.get(key, 0) >= val:
                continue
            w[key] = val
            sem = self.semobj[key]
            self.q[eng].append(lambda e, sem=sem, val=val: e.wait_ge(sem, val))

    @staticmethod
    def _merge(d, key, val):
        if d.get(key, 0) < val:
            d[key] = val

    @staticmethod
    def _flat(bufs):
        out = []
        for b in bufs:
            if isinstance(b, (list, tuple)):
                out.extend(Prog._flat(b))
            else:
                out.append(b)
        return out

    def _deps(self, reads, writes):
        evs = {}
        for b in reads:
            for k, v in b.writers.items():
                self._merge(evs, k, v)
        for b in writes:
            for k, v in b.writers.items():
                self._merge(evs, k, v)
            for k, v in b.readers.items():
                self._merge(evs, k, v)
        return evs

    def _commit(self, ev, reads, writes):
        key, val = ev
        for b in writes:
            b.writers = {key: val}
            b.readers = {}
        for b in reads:
            self._merge(b.readers, key, val)

    def op(self, eng, fns, reads=(), writes=()):
        reads, writes = self._flat(reads), self._flat(writes)
        if callable(fns):
            fns = [fns]
        self._emit_waits(eng, self._deps(reads, writes))
        self.ecnt[eng] += 1
        val = self.ecnt[eng]
        sem = self.esem[eng]
        for f in fns[:-1]:
            self.q[eng].append(f)
        last = fns[-1]
        self.q[eng].append(lambda e, last=last, sem=sem: last(e).then_inc(sem, 1))
        self.ninst += len(fns)
        self._commit((("e", eng), val), reads, writes)

    def dma(self, qeng, out, in_, reads=(), writes=()):
        reads, writes = self._flat(reads), self._flat(writes)
        i = self.dnext[qeng]
        self.dnext[qeng] = (i + 1) % self.NDMA
        key = ("d", qeng, i)
        evs = self._deps(reads, writes)
        if self.dcnt[qeng][i] > 0:
            self._merge(evs, key, self.dcnt[qeng][i])
        self._emit_waits(qeng, evs)
        self.dcnt[qeng][i] += 16
        val = self.dcnt[qeng][i]
        sem = self.dsem[qeng][i]
        self.q[qeng].append(
            lambda e, out=out, in_=in_, sem=sem: e.dma_start(out=out, in_=in_).then_inc(sem, 16))
        self.ninst += 1
        self._commit((key, val), reads, writes)

    def finish(self):
        evs = {}
        for e in self.dsem:
            for i in range(self.NDMA):
                if self.dcnt[e][i] > 0:
                    self._merge(evs, ("d", e, i), self.dcnt[e][i])
        for e in ENGS:
            if e != "sp" and self.ecnt[e] > 0:
                self._merge(evs, ("e", e), self.ecnt[e])
        self._emit_waits("sp", evs)

    def run_block(self):
        q = self.q
        with self.nc.Block() as block:
            @block.tensor
            def _(e):
                for f in q["pe"]:
                    f(e)

            @block.scalar
            def _(e):
                for f in q["act"]:
                    f(e)

            @block.vector
            def _(e):
                for f in q["dve"]:
                    f(e)

            @block.gpsimd
            def _(e):
                for f in q["pool"]:
                    f(e)

            @block.sync
            def _(e):
                for f in q["sp"]:
                    f(e)


def skewed(n, stages, offsets):
    for t in range(n + max(offsets)):
        for fn, off in zip(stages, offsets):
            i = t - off
            if 0 <= i < n:
                fn(i)


def mm(out, lhsT, rhs, start=True, stop=True):
    return lambda e: e.matmul(out, lhsT, rhs, start=start, stop=stop)


DBG = []


def build(seqs, depth, smax, dbg=False):
    NTOK = sum(seqs)
    NT = NTOK // 128
    nc = bass.Bass("TRN2", target_bir_lowering=False)
    es = ExitStack()

    def din(name, shape, dt=F32):
        return nc.dram_tensor(name, list(shape), dt, kind="ExternalInput").ap()

    def dint(name, shape, dt):
        return nc.dram_tensor(name, list(shape), dt, kind="Internal").ap()

    x_in = din("x", [NTOK, D])
    wsl = din("wsl", [depth * NIMG, 128, 8192])
    wlr_d = din("wlr", [depth, 128, 512])
    wg_d = din("wg", [depth * 2, 17, 512])
    wsT_d = din("wsT", [depth, 128, 512])
    ng_d = din("norm_gT", [depth, 128, 16])
    lng_d = din("a_ln_g", [depth, 1024])
    lnb_d = din("a_ln_b", [depth, 1024])
    bs_d = din("a_bs", [depth, 512])
    sink_d = din("b_sink", [depth, 8])
    cng_d = din("c_norm_gT", [depth, 128, 2])
    fg_d = din("final_g", [1, D])
    cbf_d = din("cbf", [128, 2432])
    cf_d = din("cf", [128, 516])
    ropeC_d = din("ropeC", [128, smax])
    ropeS_d = din("ropeS", [128, smax])
    y_out = nc.dram_tensor("y", [NTOK, D], F32, kind="ExternalOutput").ap()
    wBl = [dint(f"wB{l}", [NIMG, 128, 8192], BF16) for l in range(depth)]
    wB = [wBl[i // NIMG][i % NIMG] for i in range(depth * NIMG)]
    xa = dint("xa", [NTOK, D], F32)
    xb = dint("xb", [NTOK, D], F32)
    st_d = dint("states", [NT, 128, 1024], BF16)

    P = Prog(nc)
    dbg_n = [0]

    def dump(label, ap, bufs, ncols):
        if not dbg:
            return
        o = nc.dram_tensor(f"dbg_{label}", [128, ncols], F32, kind="ExternalOutput").ap()
        P.dma("pool", o, ap, reads=bufs, writes=[Buf()])
        DBG.append(label)

    def sb(name, shape, dt):
        return es.enter_context(nc.sbuf_tensor("s_" + name, list(shape), dt))

    NRING = 3
    ring = [sb(f"ring{i}", [128, 8192], BF16) for i in range(NRING)]
    ring_b = [Buf(f"ring{i}") for i in range(NRING)]
    hT = sb("hT", [128, 16, 512], BF16); hT_b = [Buf(f"hT{i}") for i in range(4)]
    hTh = sb("hTh", [128, 16, 128], BF16); hTh_b = Buf("hTh")
    xst = [sb("xst0", [128, D], F32)] * 2
    xst_b = [Buf("xst0")] * 2
    hst = sb("hst", [128, D], BF16); hst_b = Buf("hst")
    stat = sb("stat", [128, 192], F32)
    stat_b = [Buf(f"stat{i}") for i in range(16)]
    merged = sb("merged", [128, 16, 512], BF16)
    mrg_b = [Buf(f"mrg{i}") for i in range(16)]
    yT = sb("yT", [128, 8, 512], BF16)
    yT_b = [Buf(f"yT{i}") for i in range(4)]
    cbf = sb("cbf", [128, 2432], BF16); cbf_b = Buf("cbf")
    ident = cbf[:, 0:128]
    masks = [cbf[:, 128 + 384 * i:128 + 384 * (i + 1)] for i in range(3)]
    gmask = [cbf[:, 1280:1792], cbf[:, 1792:2304]]
    ones_bf = cbf[:, 2304:2432]
    cf = sb("cf", [128, 516], F32); cf_b = Buf("cf")
    M1 = [cf[:, 0:128], cf[:, 128:256]]
    M2 = [cf[:, 256:384], cf[:, 384:512]]
    negc = cf[:, 512:513]
    one_c = cf[:, 513:514]
    eps_c = cf[:, 514:515]
    gT = sb("gT", [128, 16], F32); G_b = Buf("G")
    lnG = sb("lnG", [128, 1024], F32)
    lnB = sb("lnB", [128, 1024], F32)
    bs_bc = sb("bs_bc", [128, 512], F32)
    sink_bc = sb("sink_bc", [128, 8], F32)
    nsink = sb("nsink", [128, 8], F32)
    cng = sb("cng", [128, 2], F32)
    wlr = sb("wlr", [128, 16, 32], BF16)
    wg = [sb(f"wg{i}", [17, 512], F32) for i in range(2)]
    wsT = sb("wsT", [128, 4, 128], BF16)
    lp_b = Buf("layer_params")
    ropeC = sb("ropeC", [128, 640], F32)
    ropeS = sb("ropeS", [128, 640], F32)
    rope_b = Buf("rope")
    tA = sb("tA", [128, 1024], F32); tA_b = Buf("tA")
    tB = sb("tB", [128, 1024], F32); tB_b = Buf("tB"); tBh_b = [tB_b, Buf("tB1")]
    tC = [sb(f"tC{i}", [128, 1024], BF16) for i in range(2)]
    tC_b = [Buf(f"tC{i}") for i in range(2)]
    uT = sb("uT", [128, 8, 512], BF16); uT_b = Buf("uT")
    szt = [sb(f"szt{i}", [128, 512], BF16) for i in range(2)]; szt_b = [Buf(f"szt{i}") for i in range(2)]
    xch = [sb(f"xch{i}", [128, 512], F32) for i in range(2)]; xch_b = [Buf(f"xch{i}") for i in range(2)]
    krT = sb("krT", [128, 2, 768], BF16); krT_b = [Buf(f"krT{i}") for i in range(6)]
    vtB = sb("vtB", [128, 6, 256], BF16); vtB_b = [Buf(f"vtB{i}") for i in range(6)]
    pbuf = [sb(f"p{i}", [128, 384], BF16) for i in range(3)]; pbuf_b = [Buf(f"p{i}") for i in range(3)]
    pT = [sb(f"pT{i}", [128, 384], BF16) for i in range(2)]; pT_b = [Buf(f"pT{i}") for i in range(2)]
    Dg = [sb(f"Dg{i}", [128, 128], BF16) for i in range(3)]; Dg_b = [Buf(f"Dg{i}") for i in range(3)]
    vtC = uT[:, :, :].rearrange("p a b -> p (a b)").rearrange("p (j c) -> p j c", c=1024); vtC_b = [uT_b] * 4
    lrT = [sb(f"lrT{i}", [17, 512], F32) for i in range(2)]; lrT_b = [Buf(f"lrT{i}") for i in range(2)]
    gE = [sb("gE", [128, 512], BF16)] * 2; gE_b = [Buf("gE")] * 2
    gEi = [sb("gEi", [128, 512], BF16)] * 2; gEi_b = [Buf("gEi")] * 2
    gEk = [sb("gEk", [128, 512], BF16)] * 2; gEk_b = [Buf("gEk")] * 2
    qe = [sb(f"qe{i}", [128, 512], BF16) for i in range(4)]; qe_b = [Buf(f"qe{i}") for i in range(4)]
    ke = [sb("ke", [128, 512], BF16)] * 2; ke_b = [Buf("ke")] * 2
    kd = [sb(f"kd{i}", [128, 512], BF16) for i in range(2)]; kd_b = [Buf(f"kd{i}") for i in range(2)]
    Am = [sb(f"Am{i}", [128, 512], BF16) for i in range(4)]; Am_b = [Buf(f"Am{i}") for i in range(4)]
    sq = sb("sq", [128, 1024], BF16); sq_b = Buf("sq")
    rstd = sb("rstd", [128, 512], F32); rstd_b = Buf("rstd")
    dec = sb("dec", [128, 12], F32); dec_b = Buf("dec"); decm_b = [Buf("dec0"), Buf("dec1")]
    Sf = sb("Sf", [128, 1024], F32); Sf_b = Buf("Sf")
    Sf16 = sb("Sf16", [128, 1024], BF16); Sf16_b = Buf("Sf16")
    Sb = Sf; Sb_b = Sf_b
    Sb16 = [sb(f"Sb16_{i}", [128, 1024], BF16) for i in range(2)]
    Sb16_b = [Buf(f"Sb16_{i}") for i in range(2)]
    psum = [es.enter_context(nc.psum_tensor(f"ps{i}", [128, 512], F32)) for i in range(8)]
    ps_b = [Buf(f"ps{i}") for i in range(8)]
    pctr = [0]

    reserved = set()

    def bank(reserve=False):
        while pctr[0] % 8 in reserved:
            pctr[0] += 1
        i = pctr[0] % 8
        pctr[0] += 1
        if reserve:
            reserved.add(i)
        return psum[i], ps_b[i]

    sctr = [0]

    def statcol(n=1):
        assert n <= 12
        sl = sctr[0] % 16
        sctr[0] += 1
        return stat[:, sl * 12:sl * 12 + n], stat_b[sl]

    X = [x_in, xa, xb]
    X_b = [[Buf() for _ in range(NT)] for _ in range(3)]
    Y_b = [Buf() for _ in range(NT)]
    st_b = [Buf() for _ in range(NT)]
    wB_b = [Buf() for _ in range(depth * NIMG)]

    P.dma("pool", cbf[:, :], cbf_d, writes=[cbf_b])
    P.dma("sp", cf[:, :], cf_d, writes=[cf_b])
    conv_order = PRE_ORDER + [i for i in MAIN_ORDER if i not in PRE_ORDER]

    def convert(l, lo, hi):
        for i in conv_order[lo:hi]:
            P.dma("pool", wB[l * NIMG + i], wsl[l * NIMG + i], writes=[wB_b[l * NIMG + i]])

    convert(0, 0, NIMG)
    for t in lrT:
        P.op("pool", lambda e, t=t: e.memset(t[:, :], 1.0), writes=lrT_b)

    sched = []
    for l in range(depth):
        sched += [l * NIMG + i for i in PRE_ORDER]
        nst = NTOK // 512
        for _ in range(nst):
            sched += [l * NIMG + i for i in MAIN_ORDER]
    wstate = dict(issued=0, used=0)

    def w_prefetch(upto):
        while wstate["issued"] < min(upto, len(sched)):
            n = wstate["issued"]
            r = n % NRING
            P.dma("sp", ring[r][:, :], wB[sched[n]], reads=[wB_b[sched[n]]], writes=[ring_b[r]])
            wstate["issued"] += 1

    def w_get(img, ncol=512, hold=0):
        n = wstate["used"]
        assert sched[n] == img, (n, sched[n], img)
        w_prefetch(n + NRING - hold)
        wstate["used"] += 1
        r = n % NRING
        return ring[r][:, :].rearrange("p (k n) -> p k n", n=ncol), ring_b[r]

    def load_layer_params(l):
        w = [lp_b]
        P.dma("sp", gT[:, :], ng_d[l], writes=[G_b])
        P.dma("sp", lnG[:, :], lng_d[l, :].partition_broadcast(128), writes=w)
        P.dma("sp", lnB[:, :], lnb_d[l, :].partition_broadcast(128), writes=w)
        P.dma("sp", bs_bc[:, :], bs_d[l, :].partition_broadcast(128), writes=w)
        P.dma("sp", sink_bc[:, :], sink_d[l, :].partition_broadcast(128), writes=w)
        P.dma("sp", cng[:, :], cng_d[l], writes=w)
        for i in range(2):
            P.dma("sp", wg[i][:, :], wg_d[l * 2 + i], writes=w)
        P.dma("pool", wlr[:, :, :], wlr_d[l].rearrange("p (k c) -> p k c", c=32), writes=w)
        P.dma("pool", wsT[:, :, :], wsT_d[l].rearrange("p (g i) -> p g i", i=128), writes=w)
        P.op("pool", lambda e: e.tensor_scalar(nsink[:, :], sink_bc[:, :], -1.0, None, ALU.mult), reads=w, writes=w)

    def rstd_from(ssum, scale, n):
        r, rb = statcol(n)
        P.op("act", lambda e: e.activation(r, ssum[0], AF.Ln, bias=eps_c, scale=scale), reads=[ssum[1], cf_b], writes=[rb])
        P.op("act", lambda e: e.activation(r, r, AF.Exp, scale=-0.5), reads=[rb], writes=[rb])
        return r, rb

    def make_hT_A(l, tile):
        xt, xtb = xst[0], xst_b[0]
        src = X[0] if l == 0 else X[1 + (l - 1) % 2]
        srcb = X_b[0] if l == 0 else X_b[1 + (l - 1) % 2]
        P.dma("sp", xt[:, :], src[tile * 128:(tile + 1) * 128, :], reads=[srcb[tile]], writes=[xtb])
        ss, ssb = statcol()
        P.op("act", lambda e: e.activation(hst[:, :], xt[:, :], AF.Square, accum_out=ss), reads=[xtb], writes=[hst_b, ssb])
        r, rb = rstd_from((ss, ssb), 1.0 / D, 1)
        P.op("dve", lambda e: e.tensor_scalar(hst[:, :], xt[:, :], r, None, ALU.mult), reads=[xtb, rb], writes=[hst_b])

    def make_hT_B(dst, dst_b, c0):
        for q in range(4):
            ps, pb = bank()
            P.op("pe", [mm(ps[:, kk * 128:(kk + 1) * 128], hst[:, (4 * q + kk) * 128:(4 * q + kk + 1) * 128], ident)
                        for kk in range(4)], reads=[hst_b, cbf_b], writes=[pb])
            for kk in range(4):
                k = 4 * q + kk
                d2 = dst[:, k, c0:c0 + 128]
                s2 = ps[:, kk * 128:(kk + 1) * 128]
                if kk % 2 == 0:
                    P.op("act", lambda e, d2=d2, s2=s2, k=k: e.activation(d2, s2, AF.Copy, scale=gT[:, k:k + 1]), reads=[pb, G_b], writes=[dst_b])
                else:
                    P.op("dve", lambda e, d2=d2, s2=s2, k=k: e.tensor_scalar(d2, s2, gT[:, k:k + 1], None, ALU.mult), reads=[pb, G_b], writes=[dst_b])

    def make_hT(l, tile, dst, dst_b, c0):
        make_hT_A(l, tile)
        make_hT_B(dst, dst_b, c0)

    def fm_block(wt, wb_, blk, K, rhs, rhs_b, ncols=512, rhs_c0=0):
        ps, pb = bank()
        P.op("pe", [mm(ps[:, 0:ncols], wt[:, k, blk * 128:(blk + 1) * 128], rhs[:, k, rhs_c0:rhs_c0 + ncols], k == 0, k == K - 1)
                    for k in range(K)], reads=[wb_, rhs_b], writes=[pb])
        return ps, pb

    def tm_group(lhs, lhs_b, c0, wt, wb_, wc0, ncols, K=16, reserve=False):
        ps, pb = bank(reserve=reserve)
        P.op("pe", [mm(ps[:, 0:ncols], lhs[:, k, c0:c0 + 128], wt[:, k, wc0:wc0 + ncols], k == 0, k == K - 1)
                    for k in range(K)], reads=[wb_, lhs_b], writes=[pb])
        return ps, pb

    def gate_L(d, lr_ap, lr_buf, Lout, Lout_b):
        ps, pb = bank()
        P.op("pe", mm(ps[:, :], lr_ap, wg[d][:, :]), reads=[lr_buf, lp_b], writes=[pb])
        P.op("act", lambda e: e.activation(Lout, ps[:, :], AF.Exp, scale=-1.0), reads=[pb], writes=[Lout_b])
        P.op("act", lambda e: e.activation(Lout, Lout, AF.Ln, bias=one_c, scale=1.0), reads=[Lout_b, cf_b], writes=[Lout_b])

    seq_tiles = []
    t0 = 0
    for s in seqs:
        seq_tiles.append((t0, s // 128))
        t0 += s // 128

    for l in range(depth):
        load_layer_params(l)
        cur = X[0] if l == 0 else X[1 + (l - 1) % 2]
        nxt_i = 1 + l % 2
        last = (l == depth - 1)
        wkc, wkc_b = w_get(l * NIMG + KC)
        wv0, wv0_b = w_get(l * NIMG + VC0, hold=1)
        wv1, wv1_b = w_get(l * NIMG + VC1, hold=2)
        for (tb, ntile) in seq_tiles:
            P.op("pool", lambda e: e.memset(Sb[:, :], 0.0), writes=[Sb_b])
            cur16 = 0
            P.op("pool", lambda e, c=cur16: e.memset(Sb16[c][:, :], 0.0), writes=[Sb16_b[cur16]])
            order = list(reversed(range(ntile)))
            PS = [dict() for _ in range(ntile)]
            c16 = [0]

            def PP0(i):
                make_hT(l, tb + order[i], hT, hT_b[i % 4], (i % 4) * 128)

            def PP1(i):
                par = i % 2
                hb_, c0 = hT_b[i % 4], (i % 4) * 128
                PS[i]["k"] = tm_group(hT, hb_, c0, wkc, wkc_b, 0, 512, reserve=True)
                vt = tC[par]; vt_b = tC_b[par]
                for half, (wv, wvb) in enumerate(((wv0, wv0_b), (wv1, wv1_b))):
                    ps, pb = tm_group(hT, hb_, c0, wv, wvb, 0, 512)
                    P.op("act", lambda e, ps=ps, half=half, vt=vt: e.copy(vt[:, half * 512:(half + 1) * 512], ps[:, :]), reads=[pb], writes=[vt_b])
                ps, pb = bank()
                P.op("pe", [mm(ps[0:16, 0:128], wlr[:, k, 16:32], hT[:, k, c0:c0 + 128], k == 0, k == 15) for k in range(16)],
                     reads=[lp_b, hb_], writes=[pb])
                P.op("dve", lambda e, ps=ps: e.tensor_copy(lrT[par][0:16, 0:128], ps[0:16, 0:128]), reads=[pb], writes=[lrT_b[par]])
                gate_L(1, lrT[par][:, 0:128], lrT_b[par], tB[:, par * 512:(par + 1) * 512], tBh_b[par])

            def PP2(i):
                par = i % 2
                tile = tb + order[i]
                Lb = tB[:, par * 512:(par + 1) * 512]
                vt = tC[par]; vt_b = tC_b[par]
                ps, pb = bank()
                P.op("pe", mm(ps[:, :], M2[1], Lb), reads=[cf_b, tBh_b[par]], writes=[pb])
                P.op("act", lambda e, ps=ps: e.activation(gEk[1][:, :], ps[:, :], AF.Exp), reads=[pb], writes=[gEk_b[1]])
                ps, pb = bank()
                P.op("pe", [mm(ps[:, h:h + 1], Lb[:, h * 128:(h + 1) * 128], negc) for h in range(4)], reads=[cf_b, tBh_b[par]], writes=[pb])
                P.op("act", lambda e, ps=ps: e.activation(dec[:, 8:12], ps[:, 0:4], AF.Exp), reads=[pb], writes=[dec_b])
                psk, pbk = PS[i]["k"]
                P.op("dve", lambda e: e.tensor_tensor(kd[1][:, :], psk[:, :], gEk[1][:, :], ALU.mult), reads=[pbk, gEk_b[1]], writes=[kd_b[1]])
                reserved.discard(ps_b.index(pbk))
                cur16 = c16[0]
                P.dma("pool", st_d[tile], Sb16[cur16][:, :], reads=[Sb16_b[cur16]], writes=[st_b[tile]])
                for hp in range(2):
                    ps, pb = bank()
                    P.op("pe", [mm(ps[:, hh * 256:(hh + 1) * 256], kd[1][:, (2 * hp + hh) * 128:(2 * hp + hh + 1) * 128],
                                   vt[:, (2 * hp + hh) * 256:(2 * hp + hh + 1) * 256]) for hh in range(2)],
                         reads=[kd_b[1], vt_b], writes=[pb])
                    for hh in range(2):
                        h = 2 * hp + hh
                        P.op("dve", lambda e, ps=ps, h=h, hh=hh: e.scalar_tensor_tensor(
                            Sb[:, h * 256:(h + 1) * 256], Sb[:, h * 256:(h + 1) * 256], dec[:, 8 + h:9 + h],
                            ps[:, hh * 256:(hh + 1) * 256], ALU.mult, ALU.add), reads=[pb, dec_b, Sb_b], writes=[Sb_b])
                c16[0] ^= 1
                nx = c16[0]
                P.op("pool", lambda e: e.tensor_copy(Sb16[nx][:, :], Sb[:, :]), reads=[Sb_b], writes=[Sb16_b[nx]])

            skewed(ntile, [PP0, PP1, PP2], [0, 1, 2])

        stc = [0]
        pre0 = [False]
        for (tb, ntile) in seq_tiles:
            nst = ntile // 4
            P.op("pool", lambda e: e.memset(Sf[:, :], 0.0), writes=[Sf_b])
            P.op("pool", lambda e: e.memset(Sf16[:, :], 0.0), writes=[Sf16_b])
            for st in range(nst):
                t_first = tb + st * 4
                pos0 = st * 512
                has_next = st < nst - 1
                has_prev = st > 0
                if l + 1 < depth:
                    per = -(-NIMG // (NTOK // 512))
                    convert(l + 1, stc[0] * per, (stc[0] + 1) * per)
                    stc[0] += 1
                def stage0_items(st_, l=l, tb=tb, nst=nst):
                    tf = tb + st_ * 4
                    hn = st_ < nst - 1
                    items = [((lambda j=j: make_hT_A(l, tf + j)), (lambda j=j: make_hT_B(hT, hT_b[j], j * 128))) for j in range(4)]
                    if hn:
                        items.append(((lambda: make_hT_A(l, tf + 4)), (lambda: make_hT_B(hTh, hTh_b, 0))))

                    def ropes():
                        nr = 640 if hn else 512
                        P.dma("sp", ropeC[:, 0:nr], ropeC_d[:, st_ * 512:st_ * 512 + nr], writes=[rope_b])
                        P.dma("sp", ropeS[:, 0:nr], ropeS_d[:, st_ * 512:st_ * 512 + nr], writes=[rope_b])
                    items.append((ropes, lambda: None))
                    return items

                if not pre0[0]:
                    for ia, ib in stage0_items(st):
                        ia(); ib()
                pre0[0] = False
                D0 = (l == 0 and st == 0 and tb == 0)
                if D0:
                    dump("hT", hT[:, :, :].rearrange("p a b -> p (a b)"), hT_b, 8192)
                wva = [w_get(l * NIMG + VA0), w_get(l * NIMG + VA1, hold=1)]
                S1 = [None] * 4

                def A1(j):
                    pss = [tm_group(hT, hT_b, j * 128, wva[h][0], wva[h][1], 0, 512) for h in range(2)]
                    bst, bstb = statcol(12)
                    for h in range(2):
                        P.op("dve", lambda e, h=h, ps=pss[h][0], bst=bst: e.bn_stats(bst[:, h * 6:(h + 1) * 6], ps[:, :]),
                             reads=[pss[h][1]], writes=[bstb])
                    mv, mvb = statcol(2)
                    P.op("dve", lambda e, bst=bst, mv=mv: e.bn_aggr(mv, bst), reads=[bstb], writes=[mvb])
                    r, rb = rstd_from((mv[:, 1:2], mvb), 1.0, 1)
                    nm, nmb = statcol()
                    P.op("dve", lambda e, nm=nm, mv=mv, r=r: e.scalar_tensor_tensor(nm, mv[:, 0:1], -1.0, r, ALU.mult, ALU.mult),
                         reads=[mvb, rb], writes=[nmb])
                    for h in range(2):
                        P.op("act", lambda e, h=h, ps=pss[h][0], r=r, nm=nm: e.activation(
                            tA[:, h * 512:(h + 1) * 512], ps[:, :], AF.Identity, bias=nm, scale=r),
                            reads=[pss[h][1], rb, nmb], writes=[tA_b])
                    P.op("dve", lambda e: e.tensor_tensor(tA[:, :], tA[:, :], lnG[:, :], ALU.mult), reads=[tA_b, lp_b], writes=[tA_b])
                    vn = tC[j % 2]; vnb = tC_b[j % 2]
                    P.op("dve", lambda e, vn=vn: e.tensor_tensor(vn[:, :], tA[:, :], lnB[:, :], ALU.add), reads=[tA_b, lp_b], writes=[vnb])
                    S1[j] = (vn, vnb)

                def A2(j):
                    vn, vnb = S1[j]
                    for hb in range(2):
                        ps, pb = bank()
                        P.op("pe", [mm(ps[:, c * 128:(c + 1) * 128], vn[:, (4 * hb + c) * 128:(4 * hb + c + 1) * 128],
                                       wsT[:, (4 * hb + c) // 2, :]) for c in range(4)], reads=[vnb, lp_b], writes=[pb])
                        for gg in range(2):
                            g = hb * 2 + gg
                            P.op("dve", lambda e, ps=ps, gg=gg, g=g, hb=hb: e.tensor_tensor(
                                yT[:, 4 * hb + 2 * gg:4 * hb + 2 * gg + 2, j * 128:(j + 1) * 128],
                                ps[:, gg * 256:(gg + 1) * 256].rearrange("p (a b) -> p a b", b=128),
                                bs_bc[:, g * 128:(g + 1) * 128].unsqueeze(1).to_broadcast([128, 2, 128]), ALU.add),
                                reads=[pb, lp_b], writes=[yT_b[j]])

                skewed(4, [A1, A2], [0, 1])
                for half, img in enumerate((UA0, UA1)):
                    wt, wb_ = w_get(l * NIMG + img)
                    for bl in range(4):
                        ps, pb = fm_block(wt, wb_, bl, 16, hT, hT_b)
                        P.op("act", lambda e, ps=ps, b=half * 4 + bl: e.copy(uT[:, b, :], ps[:, :]), reads=[pb], writes=[uT_b])
                zc = 0
                for half, img in enumerate((ZA0, ZA1)):
                    wt, wb_ = w_get(l * NIMG + img)
                    for bl in range(4):
                        b = half * 4 + bl
                        ps, pb = fm_block(wt, wb_, bl, 16, hT, hT_b)
                        zz = zc % 2; zc += 1
                        P.op("act", lambda e, ps=ps, zz=zz: e.activation(szt[zz][:, :], ps[:, :], AF.Silu), reads=[pb], writes=[szt_b[zz]])
                        P.op("dve", lambda e, b=b, zz=zz: e.tensor_tensor(uT[:, b, :], uT[:, b, :], szt[zz][:, :], ALU.mult),
                             reads=[uT_b, szt_b[zz]], writes=[uT_b])
                        P.op("dve", lambda e, b=b: e.tensor_tensor(yT[:, b, :], yT[:, b, :], uT[:, b, :], ALU.mult),
                             reads=[uT_b] + yT_b, writes=yT_b)
                if D0:
                    dump("ya", yT[:, :, :].rearrange("p a b -> p (a b)"), yT_b, 4096)
                merge_term(P, l, 0, w_get, fm_block, yT, yT_b, hT, hT_b, merged, mrg_b, bank, (PA0, GA0, GA1, PA1, GA2, GA3), tB, tBh_b)

                if has_prev:
                    P.op("pool", lambda e: e.tensor_copy(krT[:, :, 0:128], krT[:, :, 512:640]), reads=[krT_b[4]], writes=[krT_b[0]])
                    P.op("pool", lambda e: e.tensor_copy(vtB[:, 0, :], vtB[:, 4, :]), reads=[vtB_b[4]], writes=[vtB_b[0]])
                else:
                    P.op("pool", lambda e: e.memset(krT[:, :, 0:128], 0.0), writes=[krT_b[0]])
                    P.op("pool", lambda e: e.memset(vtB[:, 0, :], 0.0), writes=[vtB_b[0]])
                if not has_next:
                    P.op("pool", lambda e: e.memset(krT[:, :, 640:768], 0.0), writes=[krT_b[5]])
                    P.op("pool", lambda e: e.memset(vtB[:, 5, :], 0.0), writes=[vtB_b[5]])

                rctr = [0]

                def rope_evac(ps, pb, ncols, rc0, dst, dst_bufs):
                    if rctr[0] % 2 == 0:
                        T, Tb = tA, [tA_b]
                    else:
                        T, Tb = tB, tBh_b
                    rctr[0] += 1
                    P.op("dve", lambda e: e.tensor_tensor(T[:, 0:ncols], ps[:, 0:ncols], ropeC[:, rc0:rc0 + ncols], ALU.mult),
                         reads=[pb, rope_b], writes=Tb)
                    P.op("dve", lambda e: e.tensor_tensor(T[0:64, 512:512 + ncols], ps[64:128, 0:ncols], ropeS[64:128, rc0:rc0 + ncols], ALU.mult),
                         reads=[pb, rope_b], writes=Tb)
                    P.op("dve", lambda e: e.tensor_tensor(T[64:128, 512:512 + ncols], ps[0:64, 0:ncols], ropeS[0:64, rc0:rc0 + ncols], ALU.mult),
                         reads=[pb, rope_b], writes=Tb)
                    P.op("dve", lambda e: e.tensor_tensor(dst, T[:, 0:ncols], T[:, 512:512 + ncols], ALU.add), reads=Tb, writes=dst_bufs)

                for half, img in enumerate((QB0, QB1)):
                    wt, wb_ = w_get(l * NIMG + img)
                    for bl in range(4):
                        ps, pb = fm_block(wt, wb_, bl, 16, hT, hT_b)
                        rope_evac(ps, pb, 512, 0, uT[:, half * 4 + bl, :], [uT_b])
                wt, wb_ = w_get(l * NIMG + KVB)
                for kv in range(2):
                    ps, pb = fm_block(wt, wb_, kv, 16, hT, hT_b)
                    rope_evac(ps, pb, 512, 0, krT[:, kv, 128:640], krT_b[1:5])
                    if has_next:
                        ps, pb = fm_block(wt, wb_, kv, 16, hTh, hTh_b, ncols=128)
                        rope_evac(ps, pb, 128, 512, krT[:, kv, 640:768], [krT_b[5]])
                for j in range(5 if has_next else 4):
                    src, srcb, c0 = (hT, hT_b, j * 128) if j < 4 else (hTh, hTh_b, 0)
                    ps, pb = tm_group(src, srcb, c0, wt, wb_, 256, 256)
                    P.op("act", lambda e, ps=ps, j=j: e.copy(vtB[:, j + 1, :], ps[:, 0:256]), reads=[pb], writes=[vtB_b[j + 1]])
                scale = 128.0 ** -0.5
                S = [dict() for _ in range(32)]

                def stA(i):
                    j, h = divmod(i, 8)
                    kv = h // 4
                    tl = st * 4 + j
                    mk = masks[1] if tl == 0 else (masks[2] if tl == ntile - 1 else masks[0])
                    ps, pb = bank()
                    S[i]["ps"], S[i]["pb"] = ps, pb
                    P.op("pe", [mm(ps[:, 0:384], ident, mk, True, False),
                                mm(ps[:, 0:384], uT[:, h, j * 128:(j + 1) * 128], krT[:, kv, j * 128:j * 128 + 384], False, True)],
                         reads=[cbf_b, uT_b] + krT_b[j:j + 3], writes=[pb])

                def stB(i):
                    j, h = divmod(i, 8)
                    ps, pb = S[i]["ps"], S[i]["pb"]
                    pp = i % 3
                    mx, mxb = statcol(4)
                    P.op("dve", lambda e: e.reduce_max(mx[:, 0:1], ps[:, 0:384], AX.X), reads=[pb], writes=[mxb])
                    P.op("dve", lambda e: e.tensor_scalar(mx[:, 1:2], mx[:, 0:1], -scale, nsink[:, h:h + 1], ALU.mult, ALU.min),
                         reads=[mxb, lp_b], writes=[mxb])
                    P.op("act", lambda e: e.activation(pbuf[pp][:, :], ps[:, 0:384], AF.Exp, bias=mx[:, 1:2], scale=scale,
                                                       accum_out=mx[:, 2:3]), reads=[pb, mxb], writes=[pbuf_b[pp], mxb])
                    P.op("act", lambda e: e.activation(mx[:, 3:4], mx[:, 1:2], AF.Exp, bias=sink_bc[:, h:h + 1], scale=1.0),
                         reads=[mxb, lp_b], writes=[mxb])
                    P.op("dve", lambda e: e.tensor_tensor(mx[:, 2:3], mx[:, 2:3], mx[:, 3:4], ALU.add), reads=[mxb], writes=[mxb])
                    P.op("dve", lambda e: e.reciprocal(mx[:, 0:1], mx[:, 2:3]), reads=[mxb], writes=[mxb])
                    if i % 2:
                        P.op("pool", lambda e: e.tensor_scalar(Dg[pp][:, :], ident, mx[:, 0:1], None, ALU.mult),
                             reads=[mxb, cbf_b], writes=[Dg_b[pp]])
                    else:
                        P.op("act", lambda e: e.activation(Dg[pp][:, :], ident, AF.Copy, scale=mx[:, 0:1]),
                             reads=[mxb, cbf_b], writes=[Dg_b[pp]])

                def stC(i):
                    pp = i % 3
                    ps2, pb2 = bank()
                    S[i]["ps2"], S[i]["pb2"] = ps2, pb2
                    P.op("pe", [mm(ps2[:, c * 128:(c + 1) * 128], pbuf[pp][:, c * 128:(c + 1) * 128], Dg[pp][:, :]) for c in range(3)],
                         reads=[pbuf_b[pp], Dg_b[pp]], writes=[pb2])

                def stD(i):
                    ps2, pb2 = S[i]["ps2"], S[i]["pb2"]
                    if i % 2:
                        P.op("act", lambda e: e.copy(pT[i % 2][:, :], ps2[:, 0:384]), reads=[pb2], writes=[pT_b[i % 2]])
                    else:
                        P.op("dve", lambda e: e.tensor_copy(pT[i % 2][:, :], ps2[:, 0:384]), reads=[pb2], writes=[pT_b[i % 2]])

                def stE(i):
                    j, h = divmod(i, 8)
                    kv = h // 4
                    ob, obb = bank()
                    P.op("pe", [mm(ob[:, 0:128], vtB[:, j + c, kv * 128:(kv + 1) * 128],
                                   pT[i % 2][:, c * 128:(c + 1) * 128], c == 0, c == 2) for c in range(3)],
                         reads=[pT_b[i % 2]] + vtB_b[j:j + 3], writes=[obb])
                    P.op("dve", lambda e: e.tensor_copy(yT[:, h, j * 128:(j + 1) * 128], ob[:, 0:128]), reads=[obb], writes=[yT_b[j]])

                skewed(32, [stA, stB, stC, stD, stE], [0, 0, 2, 2, 3])
                zc = 0
                for half, img in enumerate((ZB0, ZB1)):
                    wt, wb_ = w_get(l * NIMG + img)
                    for bl in range(4):
                        b = half * 4 + bl
                        ps, pb = fm_block(wt, wb_, bl, 16, hT, hT_b)
                        zz = zc % 2; zc += 1
                        P.op("act", lambda e, ps=ps, zz=zz: e.activation(szt[zz][:, :], ps[:, :], AF.Silu), reads=[pb], writes=[szt_b[zz]])
                        P.op("dve", lambda e, b=b, zz=zz: e.tensor_tensor(yT[:, b, :], yT[:, b, :], szt[zz][:, :], ALU.mult),
                             reads=yT_b + [szt_b[zz]], writes=yT_b)
                if D0:
                    dump("m0", merged[:, :, :].rearrange("p a b -> p (a b)"), mrg_b, 8192)
                    dump("yb", yT[:, :, :].rearrange("p a b -> p (a b)"), yT_b, 4096)
                merge_term(P, l, 1, w_get, fm_block, yT, yT_b, hT, hT_b, merged, mrg_b, bank, (PB0, GB0, GB1, PB1, GB2, GB3), tB, tBh_b)

                for half in range(2):
                    wvh = w_get(l * NIMG + VC0 + half)
                    for j in range(4):
                        ps, pb = tm_group(hT, hT_b, j * 128, wvh[0], wvh[1], 0, 512)
                        P.op("act", lambda e, ps=ps, j=j, half=half: e.copy(vtC[:, j, half * 512:(half + 1) * 512], ps[:, :]),
                             reads=[pb], writes=[uT_b])
                for d in range(2):
                    ps, pb = bank()
                    P.op("pe", [mm(ps[0:16, :], wlr[:, k, d * 16:(d + 1) * 16], hT[:, k, :], k == 0, k == 15) for k in range(16)],
                         reads=[lp_b, hT_b], writes=[pb])
                    P.op("dve", lambda e, ps=ps, d=d: e.tensor_copy(lrT[d][0:16, :], ps[0:16, :]), reads=[pb], writes=[lrT_b[d]])
                wq, wq_b = w_get(l * NIMG + QC)
                wk, wk_b = w_get(l * NIMG + KC, hold=1)
                qscale = 128.0 ** -0.5
                def G1(j):
                    tile = t_first + j
                    par = j % 2
                    cs = slice(j * 128, (j + 1) * 128)
                    P.dma("sp", Sb16[par][:, :], st_d[tile], reads=[st_b[tile]], writes=[Sb16_b[par]])
                    for d in range(2):
                        gate_L(d, lrT[d][:, cs], lrT_b[d], tB[:, d * 512:(d + 1) * 512], tBh_b[d])
                    psq, pbq = bank()
                    P.op("pe", [mm(psq[:, h * 128:(h + 1) * 128], wq[:, k, h * 128:(h + 1) * 128], hT[:, k, cs], k == 0, k == 15)
                                for h in range(4) for k in range(16)], reads=[wq_b, hT_b], writes=[pbq])
                    psk, pbk = bank()
                    P.op("pe", [mm(psk[:, h * 128:(h + 1) * 128], wk[:, k, h * 128:(h + 1) * 128], hT[:, k, cs], k == 0, k == 15)
                                for h in range(4) for k in range(16)], reads=[wk_b, hT_b], writes=[pbk])
                    for d in range(2):
                        L = tB[:, d * 512:(d + 1) * 512]
                        qd, qdb = qe[2 * par + d], qe_b[2 * par + d]
                        ad, adb = Am[2 * par + d], Am_b[2 * par + d]
                        ps, pb = bank()
                        P.op("pe", [mm(ps[:, h * 128:(h + 1) * 128], L[:, h * 128:(h + 1) * 128], M1[d]) for h in range(4)],
                             reads=[tBh_b[d], cf_b], writes=[pb])
                        P.op("act", lambda e, ps=ps, d=d: e.activation(gE[d][:, :], ps[:, :], AF.Exp), reads=[pb], writes=[gE_b[d]])
                        P.op("act", lambda e, ps=ps, d=d: e.activation(gEi[d][:, :], ps[:, :], AF.Exp, scale=-1.0), reads=[pb], writes=[gEi_b[d]])
                        if d == 0:
                            P.op("act", lambda e, ps=ps: e.activation(dec[:, 4 * par:4 * par + 4], ps[:, :].rearrange("p (h c) -> p h c", c=128)[:, :, 127], AF.Exp),
                                 reads=[pb], writes=[decm_b[par]])
                        P.op("dve", lambda e, d=d, qd=qd: e.scalar_tensor_tensor(qd[:, :], psq[:, :], qscale, gE[d][:, :], ALU.mult, ALU.mult),
                             reads=[pbq, gE_b[d]], writes=[qdb])
                        P.op("dve", lambda e, d=d: e.tensor_tensor(ke[d][:, :], psk[:, :], gEi[d][:, :], ALU.mult),
                             reads=[pbk, gEi_b[d]], writes=[ke_b[d]])
                        ps, pb = bank()
                        P.op("pe", [mm(ps[:, h * 128:(h + 1) * 128], ke[d][:, h * 128:(h + 1) * 128], qd[:, h * 128:(h + 1) * 128]) for h in range(4)],
                             reads=[ke_b[d], qdb], writes=[pb])
                        P.op("dve", lambda e, ps=ps, d=d, ad=ad: e.tensor_tensor(ad[:, :], ps[:, :], gmask[d], ALU.mult), reads=[pb, cbf_b], writes=[adb])
                    ps, pb = bank()
                    P.op("pe", mm(ps[:, :], M2[0], tB[:, 0:512]), reads=[tBh_b[0], cf_b], writes=[pb])
                    P.op("act", lambda e, ps=ps: e.activation(gEk[0][:, :], ps[:, :], AF.Exp), reads=[pb], writes=[gEk_b[0]])
                    ps, pb = tm_group(hT, hT_b, j * 128, wk, wk_b, 0, 512)
                    P.op("dve", lambda e, ps=ps: e.tensor_tensor(kd[par][:, :], ps[:, :], gEk[0][:, :], ALU.mult), reads=[pb, gEk_b[0]], writes=[kd_b[par]])

                def G2(j):
                    par = j % 2
                    obanks = [bank(), bank()]
                    for hp in range(2):
                        ob, obb = obanks[hp]
                        fns = []
                        for hh in range(2):
                            h = 2 * hp + hh
                            for vb in range(2):
                                oc = ob[:, (hh * 2 + vb) * 128:(hh * 2 + vb + 1) * 128]
                                vsl = slice(h * 256 + vb * 128, h * 256 + (vb + 1) * 128)
                                hs = slice(h * 128, (h + 1) * 128)
                                fns.append(mm(oc, vtC[:, j, vsl], Am[2 * par][:, hs], True, False))
                                fns.append(mm(oc, vtC[:, j, vsl], Am[2 * par + 1][:, hs], False, False))
                                fns.append(mm(oc, Sf16[:, vsl], qe[2 * par][:, hs], False, False))
                                fns.append(mm(oc, Sb16[par][:, vsl], qe[2 * par + 1][:, hs], False, True))
                        P.op("pe", fns, reads=[uT_b, Am_b[2 * par], Am_b[2 * par + 1], Sf16_b, Sb16_b[par], qe_b[2 * par], qe_b[2 * par + 1]], writes=[obb])
                    for hp in range(2):
                        ps, pb = bank()
                        P.op("pe", [mm(ps[:, hh * 256:(hh + 1) * 256], kd[par][:, (2 * hp + hh) * 128:(2 * hp + hh + 1) * 128],
                                       vtC[:, j, (2 * hp + hh) * 256:(2 * hp + hh + 1) * 256]) for hh in range(2)],
                             reads=[kd_b[par], uT_b], writes=[pb])
                        for hh in range(2):
                            h = 2 * hp + hh
                            P.op("dve", lambda e, ps=ps, h=h, hh=hh: e.scalar_tensor_tensor(
                                Sf[:, h * 256:(h + 1) * 256], Sf[:, h * 256:(h + 1) * 256], dec[:, 4 * par + h:4 * par + h + 1],
                                ps[:, hh * 256:(hh + 1) * 256], ALU.mult, ALU.add), reads=[pb, decm_b[par], Sf_b], writes=[Sf_b])
                    P.op("pool", lambda e: e.tensor_copy(Sf16[:, :], Sf[:, :]), reads=[Sf_b], writes=[Sf16_b])
                    for hp in range(2):
                        ob, obb = obanks[hp]
                        P.op("act", lambda e, ob=ob, hp=hp: e.activation(sq[:, hp * 512:(hp + 1) * 512], ob[:, :], AF.Square), reads=[obb], writes=[sq_b])
                    ps, pb = bank()
                    P.op("pe", [mm(ps[:, h * 128:(h + 1) * 128], ones_bf, sq[:, (h * 2 + vb) * 128:(h * 2 + vb + 1) * 128], vb == 0, vb == 1)
                                for h in range(4) for vb in range(2)], reads=[sq_b, cbf_b], writes=[pb])
                    P.op("act", lambda e, ps=ps: e.activation(rstd[:, :], ps[:, :], AF.Ln, bias=eps_c, scale=1.0 / 256), reads=[pb, cf_b], writes=[rstd_b])
                    P.op("act", lambda e: e.activation(rstd[:, :], rstd[:, :], AF.Exp, scale=-0.5), reads=[rstd_b], writes=[rstd_b])
                    for hp in range(2):
                        ob, obb = obanks[hp]
                        for vb in range(2):
                            P.op("dve", lambda e, ob=ob, hp=hp, vb=vb, j=j: e.scalar_tensor_tensor(
                                yT[:, 4 * hp:4 * hp + 4, j * 128:(j + 1) * 128].rearrange("p (h v) c -> p h v c", v=2)[:, :, vb, :],
                                ob[:, :].rearrange("p (h v c) -> p h v c", v=2, c=128)[:, :, vb, :], cng[:, vb:vb + 1],
                                rstd[:, hp * 256:(hp + 1) * 256].rearrange("p (h c) -> p h c", c=128), ALU.mult, ALU.mult),
                                reads=[obb, rstd_b, lp_b], writes=[yT_b[j]])

                skewed(4, [G1, G2], [0, 1])
                zc = 0
                for half, img in enumerate((ZC0, ZC1)):
                    wt, wb_ = w_get(l * NIMG + img)
                    for bl in range(4):
                        b = half * 4 + bl
                        ps, pb = fm_block(wt, wb_, bl, 16, hT, hT_b)
                        zz = zc % 2; zc += 1
                        P.op("act", lambda e, ps=ps, zz=zz: e.activation(szt[zz][:, :], ps[:, :], AF.Silu), reads=[pb], writes=[szt_b[zz]])
                        P.op("dve", lambda e, b=b, zz=zz: e.tensor_tensor(yT[:, b, :], yT[:, b, :], szt[zz][:, :], ALU.mult),
                             reads=yT_b + [szt_b[zz]], writes=yT_b)
                if D0:
                    dump("yc", yT[:, :, :].rearrange("p a b -> p (a b)"), yT_b, 4096)
                merge_term(P, l, 2, w_get, fm_block, yT, yT_b, hT, hT_b, merged, mrg_b, bank, (PC0, GC0, GC1, PC1, GC2, GC3), tB, tBh_b)

                if D0:
                    dump("m2", merged[:, :, :].rearrange("p a b -> p (a b)"), mrg_b, 8192)
                srcb = X_b[0] if l == 0 else X_b[1 + (l - 1) % 2]
                xc = [0]
                s5 = []
                for c in range(4):
                    def getw(c=c):
                        s5w[0] = w_get(l * NIMG + WO0 + c)
                    for j in range(4):
                        def grp(c=c, j=j, getw=getw):
                            if j == 0:
                                getw()
                            wo, wo_b = s5w[0]
                            tile = t_first + j
                            rows = slice(tile * 128, (tile + 1) * 128)
                            cols = slice(c * 512, (c + 1) * 512)
                            xx = xc[0] % 2; xc[0] += 1
                            P.dma("sp", xch[xx][:, :], cur[rows, cols], reads=[srcb[tile]], writes=[xch_b[xx]])
                            ps, pb = bank()
                            P.op("pe", [mm(ps[:, :], merged[:, k, j * 128:(j + 1) * 128], wo[:, k, :], k == 0, k == 15) for k in range(16)],
                                 reads=[wo_b] + mrg_b, writes=[pb])
                            P.op("dve", lambda e: e.tensor_tensor(xch[xx][:, :], ps[:, :], xch[xx][:, :], ALU.add),
                                 reads=[pb, xch_b[xx]], writes=[xch_b[xx]])
                            P.dma("pool", X[nxt_i][rows, cols], xch[xx][:, :], reads=[xch_b[xx]], writes=[X_b[nxt_i][tile]])
                        s5.append(grp)
                s5w = [None]
                nxt0 = stage0_items(st + 1) if has_next else []
                gi = 0
                if nxt0:
                    nxt0[0][0]()
                for ii, (ia, ib) in enumerate(nxt0):
                    for _ in range(3):
                        if gi < len(s5):
                            s5[gi](); gi += 1
                    ib()
                    if ii + 1 < len(nxt0):
                        nxt0[ii + 1][0]()
                while gi < len(s5):
                    s5[gi](); gi += 1
                pre0[0] = has_next

    fin = X[1 + (depth - 1) % 2]
    fin_b = X_b[1 + (depth - 1) % 2]
    P.dma("sp", tA[:, :], fg_d[0, 0:1024].partition_broadcast(128), writes=[tA_b])
    P.dma("sp", tB[:, :], fg_d[0, 1024:2048].partition_broadcast(128), writes=tBh_b)
    for tile in range(NT):
        p = tile % 2
        xt, xtb = xst[p], xst_b[p]
        rows = slice(tile * 128, (tile + 1) * 128)
        P.dma("sp", xt[:, :], fin[rows, :], reads=[fin_b[tile]], writes=[xtb])
        ss, ssb = statcol()
        P.op("act", lambda e, xt=xt, ss=ss: e.activation(hst[:, :], xt[:, :], AF.Square, accum_out=ss), reads=[xtb], writes=[hst_b, ssb])
        r, rb = rstd_from((ss, ssb), 1.0 / D, 1)
        P.op("dve", lambda e, xt=xt, r=r: e.scalar_tensor_tensor(xt[:, 0:1024], xt[:, 0:1024], r, tA[:, :], ALU.mult, ALU.mult),
             reads=[xtb, rb, tA_b], writes=[xtb])
        P.op("dve", lambda e, xt=xt, r=r: e.scalar_tensor_tensor(xt[:, 1024:2048], xt[:, 1024:2048], r, tB[:, :], ALU.mult, ALU.mult),
             reads=[xtb, rb] + tBh_b, writes=[xtb])
        P.dma("pool", y_out[rows, :], xt[:, :], reads=[xtb], writes=[Y_b[tile]])
    P.finish()
    P.run_block()
    es.close()
    return nc


def merge_term(P, l, which, w_get, fm_block, yT, yT_b, hT, hT_b, merged, mrg_b, bank, imgs, tB, tBh_b):
    PA, G0, G1, PB_, G2, G3 = imgs
    order = [(PA, (G0, G1)), (PB_, (G2, G3))]
    for half, (pimg, gimgs) in enumerate(order):
        for gi, gimg in enumerate(gimgs):
            if gi == 0:
                wg3, wg_b = w_get(l * NIMG + gimg)
                wp3, wp_b = w_get(l * NIMG + pimg, 1024, hold=1)
            else:
                wg3, wg_b = w_get(l * NIMG + gimg, hold=1)
            for bl in range(4):
                m = half * 8 + gi * 4 + bl
                psg, pgb = fm_block(wg3, wg_b, bl, 16, hT, hT_b)
                sg = tB[:, (m % 2) * 512:(m % 2 + 1) * 512]
                sgb = tBh_b[m % 2]
                P.op("act", lambda e, psg=psg, sg=sg: e.activation(sg, psg[:, :], AF.Sigmoid), reads=[pgb], writes=[sgb])
                psp, ppb = bank()
                P.op("pe", [mm(psp[:, :], wp3[:, k, (gi * 4 + bl) * 128:(gi * 4 + bl + 1) * 128], yT[:, k, :], k == 0, k == 7) for k in range(8)],
                     reads=[wp_b] + yT_b, writes=[ppb])
                if which == 0:
                    P.op("dve", lambda e, psp=psp, sg=sg, m=m: e.tensor_tensor(merged[:, m, :], psp[:, :], sg, ALU.mult),
                         reads=[ppb, sgb], writes=[mrg_b[m]])
                else:
                    P.op("dve", lambda e, psp=psp, sg=sg: e.tensor_tensor(sg, psp[:, :], sg, ALU.mult), reads=[ppb, sgb], writes=[sgb])
                    P.op("pool", lambda e, sg=sg, m=m: e.tensor_tensor(merged[:, m, :], merged[:, m, :], sg, ALU.add),
                         reads=[sgb, mrg_b[m]], writes=[mrg_b[m]])


def _img_in(w, c0, n=512):
    a = w[:, c0:c0 + n].reshape(16, 128, n).transpose(1, 0, 2)
    return a


def _pack_layer(w_in, w_pa, w_pb, w_pc, w_out):
    imgs = np.empty((NIMG, 128, 8192), np.float32)

    def put_in(i, c0):
        imgs[i] = _img_in(w_in, c0).reshape(128, 8192)

    def put_proj(i, wp, c0):
        imgs[i] = wp[:, c0:c0 + 1024].reshape(8, 128, 1024).transpose(1, 0, 2).reshape(128, 8192)

    put_in(VA0, OFF["va"]); put_in(VA1, OFF["va"] + 512)
    put_in(UA0, OFF["ua"]); put_in(UA1, OFF["ua"] + 512)
    put_in(ZA0, OFF["za"]); put_in(ZA1, OFF["za"] + 512)
    put_in(QB0, OFF["qb"]); put_in(QB1, OFF["qb"] + 512)
    put_in(KVB, OFF["kb"])
    put_in(ZB0, OFF["zb"]); put_in(ZB1, OFF["zb"] + 512)
    put_in(QC, OFF["qc"]); put_in(KC, OFF["kc"])
    put_in(VC0, OFF["vc"]); put_in(VC1, OFF["vc"] + 512)
    put_in(ZC0, OFF["zc"]); put_in(ZC1, OFF["zc"] + 512)
    for name, ids in (("ga", (GA0, GA1, GA2, GA3)), ("gb", (GB0, GB1, GB2, GB3)), ("gc", (GC0, GC1, GC2, GC3))):
        for i, img in enumerate(ids):
            put_in(img, OFF[name] + 512 * i)
    put_proj(PA0, w_pa, 0); put_proj(PA1, w_pa, 1024)
    put_proj(PB0, w_pb, 0); put_proj(PB1, w_pb, 1024)
    put_proj(PC0, w_pc, 0); put_proj(PC1, w_pc, 1024)
    for c in range(4):
        imgs[WO0 + c] = w_out[:, c * 512:(c + 1) * 512].reshape(16, 128, 512).transpose(1, 0, 2).reshape(128, 8192)
    return imgs


def _consts(smax):
    i = np.arange(128)[:, None]
    j = np.arange(384)[None, :]
    ok = (j >= i) & (j <= i + 256)
    m0 = np.where(ok, 0.0, NEG)
    m1 = np.where(ok & (j >= 128), 0.0, NEG)
    m2 = np.where(ok & (j < 256), 0.0, NEG)
    mm_, cc = np.arange(128)[:, None], np.arange(128)[None, :]
    gmf = np.tile((mm_ <= cc).astype(np.float32), (1, 4))
    gmb = np.tile((mm_ > cc).astype(np.float32), (1, 4))
    cbf = np.concatenate([np.eye(128), m0, m1, m2, gmf, gmb, np.ones((128, 128))], axis=1).astype(np.float32)
    s = -1.0 / 16.0
    M1f = (mm_ <= cc) * s
    M1b = (mm_ >= cc) * s
    M2f = (mm_ > cc) * s
    M2b = (mm_ < cc) * s
    cf = np.concatenate([M1f, M1b, M2f, M2b, np.full((128, 1), s), np.ones((128, 1)), np.full((128, 1), EPS),
                         np.zeros((128, 1))], axis=1).astype(np.float32)
    half = 64
    inv = (10000.0 ** (-np.arange(half, dtype=np.float32) * 2.0 / 128)).astype(np.float32)
    ang = np.arange(smax, dtype=np.float32)[None, :] * inv[:, None]
    cos = np.cos(ang).astype(np.float32)
    sin = np.sin(ang).astype(np.float32)
    ropeC = np.concatenate([cos, cos], axis=0)
    ropeS = np.concatenate([sin, -sin], axis=0)
    return cbf, cf, np.ascontiguousarray(ropeC), np.ascontiguousarray(ropeS)


def _run(xs_per_core, seqs, depth, norm_g, w_in, a_ln_g, a_ln_b, a_ws, a_bs, b_sink, c_wf, c_bf, c_wb, c_bb,
         c_norm_g, w_pa, w_pb, w_pc, w_out, final_g):
    smax = max(seqs)
    f = lambda a: np.ascontiguousarray(np.asarray(a, dtype=np.float32))
    w_in, w_pa, w_pb, w_pc, w_out = map(f, (w_in, w_pa, w_pb, w_pc, w_out))
    wsl = np.concatenate([_pack_layer(w_in[l], w_pa[l], w_pb[l], w_pc[l], w_out[l]) for l in range(depth)], axis=0)
    wlr = np.stack([_img_in(w_in[l], OFF["lrf"], 32).reshape(128, 512) for l in range(depth)])
    wg = np.stack([np.concatenate([f(w)[l], f(b)[l][None, :]], axis=0) for l in range(depth) for (w, b) in ((c_wf, c_bf), (c_wb, c_bb))])
    wsT = np.stack([f(a_ws)[l].transpose(2, 0, 1).reshape(128, 512) for l in range(depth)])
    cbf, cf, ropeC, ropeS = _consts(smax)
    common = dict(wsl=wsl, wlr=f(wlr), wg=f(wg), wsT=f(wsT), norm_gT=np.ascontiguousarray(f(norm_g)[:depth].reshape(depth, 16, 128).transpose(0, 2, 1)), a_ln_g=f(a_ln_g)[:depth],
                  a_ln_b=f(a_ln_b)[:depth], a_bs=f(a_bs)[:depth].reshape(depth, 512), b_sink=f(b_sink)[:depth],
                  c_norm_gT=np.ascontiguousarray(f(c_norm_g)[:depth].reshape(depth, 2, 128).transpose(0, 2, 1)), final_g=f(final_g).reshape(1, D), cbf=cbf, cf=cf, ropeC=ropeC, ropeS=ropeS)
    nc = build(seqs, depth, smax, dbg=_DBGFLAG[0])
    in_maps = [dict(common, x=f(x)) for x in xs_per_core]
    res = run_bass_kernel_spmd(nc, in_maps, core_ids=list(range(len(in_maps))))
    if _DBGFLAG[0]:
        _DBGOUT.append(res.results[0])
    return [r["y"] for r in res.results]


_DBGFLAG = [False]
_DBGOUT = []


def kernel(x_prompt, x_sample, norm_g, w_in, a_ln_g, a_ln_b, a_ws, a_bs, b_sink, c_wf, c_bf, c_wb, c_bb,
           c_norm_g, w_pa, w_pb, w_pc, w_out, final_g):
    x_prompt = np.asarray(x_prompt, dtype=np.float32)
    x_sample = np.asarray(x_sample, dtype=np.float32)
    SP, SS = x_prompt.shape[1], x_sample.shape[1]
    xs = [np.concatenate([x_sample[c], x_prompt[c % 4]], axis=0) for c in range(8)]
    ys = _run(xs, [SS, SP], 4, norm_g, w_in, a_ln_g, a_ln_b, a_ws, a_bs, b_sink, c_wf, c_bf, c_wb, c_bb,
              c_norm_g, w_pa, w_pb, w_pc, w_out, final_g)
    y_sample = np.stack([ys[c][:SS] for c in range(8)])
    y_prompt = np.stack([ys[c][SS:SS + SP] for c in range(4)])
    return (y_prompt, y_sample)
```

```python
import numpy as np
from contextlib import ExitStack
import concourse.bass as bass
import concourse.mybir as mybir
from concourse.bass_utils import run_bass_kernel_spmd

F32 = mybir.dt.float32
BF16 = mybir.dt.bfloat16
AF = mybir.ActivationFunctionType
ALU = mybir.AluOpType
AX = mybir.AxisListType

D = 2048
EPS = 1e-6
NEG = -30000.0
ENGS = ("pe", "act", "dve", "pool", "sp")

OFF = dict(ua=0, va=1024, za=2048, qb=3072, kb=4096, vb=4352, zb=4608, qc=5632, kc=6144,
           vc=6656, zc=7680, lrf=8704, lrb=8720, ga=8736, gb=10784, gc=12832)
NIMG = 39
(VA0, VA1, UA0, UA1, ZA0, ZA1, PA0, GA0, GA1, PA1, GA2, GA3,
 QB0, QB1, KVB, ZB0, ZB1, PB0, GB0, GB1, PB1, GB2, GB3,
 QC, KC, VC0, VC1, ZC0, ZC1, PC0, GC0, GC1, PC1, GC2, GC3,
 WO0, WO1, WO2, WO3) = range(NIMG)
MAIN_ORDER = [VA0, VA1, UA0, UA1, ZA0, ZA1, GA0, PA0, GA1, GA2, PA1, GA3,
              QB0, QB1, KVB, ZB0, ZB1, GB0, PB0, GB1, GB2, PB1, GB3,
              VC0, VC1, QC, KC, ZC0, ZC1, GC0, PC0, GC1, GC2, PC1, GC3,
              WO0, WO1, WO2, WO3]
PRE_ORDER = [KC, VC0, VC1]


class Buf:
    __slots__ = ("name", "writers", "readers")

    def __init__(self, name=""):
        self.name = name
        self.writers = {}
        self.readers = {}


class Prog:
    NDMA = 12

    def __init__(self, nc):
        self.nc = nc
        self.q = {e: [] for e in ENGS}
        self.esem = {e: nc.alloc_semaphore(name=f"s_{e}") for e in ENGS}
        self.ecnt = {e: 0 for e in ENGS}
        self.dsem = {e: [nc.alloc_semaphore(name=f"d_{e}{i}") for i in range(self.NDMA)]
                     for e in ("sp", "pool")}
        self.dcnt = {e: [0] * self.NDMA for e in self.dsem}
        self.dnext = {e: 0 for e in self.dsem}
        self.waited = {e: {} for e in ENGS}
        self.semobj = {}
        for e in ENGS:
            self.semobj[("e", e)] = self.esem[e]
        for e in self.dsem:
            for i, s in enumerate(self.dsem[e]):
                self.semobj[("d", e, i)] = s
        self.ninst = 0

    def _emit_waits(self, eng, evs):
        w = self.waited[eng]
        for key, val in evs.items():
            if eng == "pe" and key == ("e", "pe"):
                continue
            if w.get(key, 0) >= val:
                continue
            w[key] = val
            sem = self.semobj[key]
            self.q[eng].append(lambda e, sem=sem, val=val: e.wait_ge(sem, val))

    @staticmethod
    def _merge(d, key, val):
        if d.get(key, 0) < val:
            d[key] = val

    @staticmethod
    def _flat(bufs):
        out = []
        for b in bufs:
            if isinstance(b, (list, tuple)):
                out.extend(Prog._flat(b))
            else:
                out.append(b)
        return out

    def _deps(self, reads, writes):
        evs = {}
        for b in reads:
            for k, v in b.writers.items():
                self._merge(evs, k, v)
        for b in writes:
            for k, v in b.writers.items():
                self._merge(evs, k, v)
            for k, v in b.readers.items():
                self._merge(evs, k, v)
        return evs

    def _commit(self, ev, reads, writes):
        key, val = ev
        for b in writes:
            b.writers = {key: val}
            b.readers = {}
        for b in reads:
            self._merge(b.readers, key, val)

    def op(self, eng, fns, reads=(), writes=()):
        reads, writes = self._flat(reads), self._flat(writes)
        if callable(fns):
            fns = [fns]
        self._emit_waits(eng, self._deps(reads, writes))
        self.ecnt[eng] += 1
        val = self.ecnt[eng]
        sem = self.esem[eng]
        for f in fns[:-1]:
            self.q[eng].append(f)
        last = fns[-1]
        self.q[eng].append(lambda e, last=last, sem=sem: last(e).then_inc(sem, 1))
        self.ninst += len(fns)
        self._commit((("e", eng), val), reads, writes)

    def dma(self, qeng, out, in_, reads=(), writes=()):
        reads, writes = self._flat(reads), self._flat(writes)
        i = self.dnext[qeng]
        self.dnext[qeng] = (i + 1) % self.NDMA
        key = ("d", qeng, i)
        evs = self._deps(reads, writes)
        if self.dcnt[qeng][i] > 0:
            self._merge(evs, key, self.dcnt[qeng][i])
        self._emit_waits(qeng, evs)
        self.dcnt[qeng][i] += 16
        val = self.dcnt[qeng][i]
        sem = self.dsem[qeng][i]
        self.q[qeng].append(
            lambda e, out=out, in_=in_, sem=sem: e.dma_start(out=out, in_=in_).then_inc(sem, 16))
        self.ninst += 1
        self._commit((key, val), reads, writes)

    def finish(self):
        evs = {}
        for e in self.dsem:
            for i in range(self.NDMA):
                if self.dcnt[e][i] > 0:
                    self._merge(evs, ("d", e, i), self.dcnt[e][i])
        for e in ENGS:
            if e != "sp" and self.ecnt[e] > 0:
                self._merge(evs, ("e", e), self.ecnt[e])
        self._emit_waits("sp", evs)

    def run_block(self):
        q = self.q
        with self.nc.Block() as block:
            @block.tensor
            def _(e):
                for f in q["pe"]:
                    f(e)

            @block.scalar
            def _(e):
                for f in q["act"]:
                    f(e)

            @block.vector
            def _(e):
                for f in q["dve"]:
                    f(e)

            @block.gpsimd
            def _(e):
                for f in q["pool"]:
                    f(e)

            @block.sync
            def _(e):
                for f in q["sp"]:
                    f(e)


def skewed(n, stages, offsets):
    for t in range(n + max(offsets)):
        for fn, off in zip(stages, offsets):
            i = t - off
            if 0 <= i < n:
                fn(i)


def mm(out, lhsT, rhs, start=True, stop=True):
    return lambda e: e.matmul(out, lhsT, rhs, start=start, stop=stop)


DBG = []


def build(seqs, depth, smax, dbg=False):
    NTOK = sum(seqs)
    NT = NTOK // 128
    nc = bass.Bass("TRN2", target_bir_lowering=False)
    es = ExitStack()

    def din(name, shape, dt=F32):
        return nc.dram_tensor(name, list(shape), dt, kind="ExternalInput").ap()

    def dint(name, shape, dt):
        return nc.dram_tensor(name, list(shape), dt, kind="Internal").ap()

    x_in = din("x", [NTOK, D])
    wsl = din("wsl", [depth * NIMG, 128, 8192])
    wlr_d = din("wlr", [depth, 128, 512])
    wg_d = din("wg", [depth * 2, 17, 512])
    wsT_d = din("wsT", [depth, 128, 512])
    ng_d = din("norm_gT", [depth, 128, 16])
    lng_d = din("a_ln_g", [depth, 1024])
    lnb_d = din("a_ln_b", [depth, 1024])
    bs_d = din("a_bs", [depth, 512])
    sink_d = din("b_sink", [depth, 8])
    cng_d = din("c_norm_gT", [depth, 128, 2])
    fg_d = din("final_g", [1, D])
    cbf_d = din("cbf", [128, 2432])
    cf_d = din("cf", [128, 516])
    ropeC_d = din("ropeC", [128, smax])
    ropeS_d = din("ropeS", [128, smax])
    y_out = nc.dram_tensor("y", [NTOK, D], F32, kind="ExternalOutput").ap()
    wBl = [dint(f"wB{l}", [NIMG, 128, 8192], BF16) for l in range(depth)]
    wB = [wBl[i // NIMG][i % NIMG] for i in range(depth * NIMG)]
    xa = dint("xa", [NTOK, D], F32)
    xb = dint("xb", [NTOK, D], F32)
    st_d = dint("states", [NT, 128, 1024], BF16)

    P = Prog(nc)
    dbg_n = [0]

    def dump(label, ap, bufs, ncols):
        if not dbg:
            return
        o = nc.dram_tensor(f"dbg_{label}", [128, ncols], F32, kind="ExternalOutput").ap()
        P.dma("pool", o, ap, reads=bufs, writes=[Buf()])
        DBG.append(label)

    def sb(name, shape, dt):
        return es.enter_context(nc.sbuf_tensor("s_" + name, list(shape), dt))

    NRING = 3
    ring = [sb(f"ring{i}", [128, 8192], BF16) for i in range(NRING)]
    ring_b = [Buf(f"ring{i}") for i in range(NRING)]
    hT = sb("hT", [128, 16, 512], BF16); hT_b = [Buf(f"hT{i}") for i in range(4)]
    hTh = sb("hTh", [128, 16, 128], BF16); hTh_b = Buf("hTh")
    xst = [sb("xst0", [128, D], F32)] * 2
    xst_b = [Buf("xst0")] * 2
    hst = sb("hst", [128, D], BF16); hst_b = Buf("hst")
    stat = sb("stat", [128, 192], F32)
    stat_b = [Buf(f"stat{i}") for i in range(16)]
    merged = sb("merged", [128, 16, 512], BF16)
    mrg_b = [Buf(f"mrg{i}") for i in range(16)]
    yT = sb("yT", [128, 8, 512], BF16)
    yT_b = [Buf(f"yT{i}") for i in range(4)]
    cbf = sb("cbf", [128, 2432], BF16); cbf_b = Buf("cbf")
    ident = cbf[:, 0:128]
    masks = [cbf[:, 128 + 384 * i:128 + 384 * (i + 1)] for i in range(3)]
    gmask = [cbf[:, 1280:1792], cbf[:, 1792:2304]]
    ones_bf = cbf[:, 2304:2432]
    cf = sb("cf", [128, 516], F32); cf_b = Buf("cf")
    M1 = [cf[:, 0:128], cf[:, 128:256]]
    M2 = [cf[:, 256:384], cf[:, 384:512]]
    negc = cf[:, 512:513]
    one_c = cf[:, 513:514]
    eps_c = cf[:, 514:515]
    gT = sb("gT", [128, 16], F32); G_b = Buf("G")
    lnG = sb("lnG", [128, 1024], F32)
    lnB = sb("lnB", [128, 1024], F32)
    bs_bc = sb("bs_bc", [128, 512], F32)
    sink_bc = sb("sink_bc", [128, 8], F32)
    nsink = sb("nsink", [128, 8], F32)
    cng = sb("cng", [128, 2], F32)
    wlr = sb("wlr", [128, 16, 32], BF16)
    wg = [sb(f"wg{i}", [17, 512], F32) for i in range(2)]
    wsT = sb("wsT", [128, 4, 128], BF16)
    lp_b = Buf("layer_params")
    ropeC = sb("ropeC", [128, 640], F32)
    ropeS = sb("ropeS", [128, 640], F32)
    rope_b = Buf("rope")
    tA = sb("tA", [128, 1024], F32); tA_b = Buf("tA")
    tB = sb("tB", [128, 1024], F32); tB_b = Buf("tB"); tBh_b = [tB_b, Buf("tB1")]
    tC = [sb(f"tC{i}", [128, 1024], BF16) for i in range(2)]
    tC_b = [Buf(f"tC{i}") for i in range(2)]
    uT = sb("uT", [128, 8, 512], BF16); uT_b = Buf("uT")
    szt = [sb(f"szt{i}", [128, 512], BF16) for i in range(2)]; szt_b = [Buf(f"szt{i}") for i in range(2)]
    xch = [sb(f"xch{i}", [128, 512], F32) for i in range(2)]; xch_b = [Buf(f"xch{i}") for i in range(2)]
    krT = sb("krT", [128, 2, 768], BF16); krT_b = [Buf(f"krT{i}") for i in range(6)]
    vtB = sb("vtB", [128, 6, 256], BF16); vtB_b = [Buf(f"vtB{i}") for i in range(6)]
    pbuf = [sb(f"p{i}", [128, 384], BF16) for i in range(4)]; pbuf_b = [Buf(f"p{i}") for i in range(4)]
    pT = [sb(f"pT{i}", [128, 384], BF16) for i in range(2)]; pT_b = [Buf(f"pT{i}") for i in range(2)]
    Dg = [sb(f"Dg{i}", [128, 128], BF16) for i in range(4)]; Dg_b = [Buf(f"Dg{i}") for i in range(4)]
    vtC = uT[:, :, :].rearrange("p a b -> p (a b)").rearrange("p (j c) -> p j c", c=1024); vtC_b = [uT_b] * 4
    lrT = [sb(f"lrT{i}", [17, 512], F32) for i in range(2)]; lrT_b = [Buf(f"lrT{i}") for i in range(2)]
    gE = [sb("gE", [128, 512], BF16)] * 2; gE_b = [Buf("gE")] * 2
    gEi = [sb("gEi", [128, 512], BF16)] * 2; gEi_b = [Buf("gEi")] * 2
    gEk = [sb("gEk", [128, 512], BF16)] * 2; gEk_b = [Buf("gEk")] * 2
    qe = [sb(f"qe{i}", [128, 512], BF16) for i in range(4)]; qe_b = [Buf(f"qe{i}") for i in range(4)]
    ke = [sb("ke", [128, 512], BF16)] * 2; ke_b = [Buf("ke")] * 2
    kd = [sb(f"kd{i}", [128, 512], BF16) for i in range(2)]; kd_b = [Buf(f"kd{i}") for i in range(2)]
    Am = [sb(f"Am{i}", [128, 512], BF16) for i in range(4)]; Am_b = [Buf(f"Am{i}") for i in range(4)]
    sq = sb("sq", [128, 1024], BF16); sq_b = Buf("sq")
    rstd = sb("rstd", [128, 512], F32); rstd_b = Buf("rstd")
    dec = sb("dec", [128, 12], F32); dec_b = Buf("dec"); decm_b = [Buf("dec0"), Buf("dec1")]
    Sf = sb("Sf", [128, 1024], F32); Sf_b = Buf("Sf")
    Sf16 = sb("Sf16", [128, 1024], BF16); Sf16_b = Buf("Sf16")
    Sb = Sf; Sb_b = Sf_b
    Sb16 = [sb(f"Sb16_{i}", [128, 1024], BF16) for i in range(2)]
    Sb16_b = [Buf(f"Sb16_{i}") for i in range(2)]
    psum = [es.enter_context(nc.psum_tensor(f"ps{i}", [128, 512], F32)) for i in range(8)]
    ps_b = [Buf(f"ps{i}") for i in range(8)]
    pctr = [0]

    reserved = set()

    def bank(reserve=False):
        while pctr[0] % 8 in reserved:
            pctr[0] += 1
        i = pctr[0] % 8
        pctr[0] += 1
        if reserve:
            reserved.add(i)
        return psum[i], ps_b[i]

    sctr = [0]

    def statcol(n=1):
        assert n <= 12
        sl = sctr[0] % 16
        sctr[0] += 1
        return stat[:, sl * 12:sl * 12 + n], stat_b[sl]

    X = [x_in, xa, xb]
    X_b = [[Buf() for _ in range(NT)] for _ in range(3)]
    Y_b = [Buf() for _ in range(NT)]
    st_b = [Buf() for _ in range(NT)]
    wB_b = [Buf() for _ in range(depth * NIMG)]

    P.dma("pool", cbf[:, :], cbf_d, writes=[cbf_b])
    P.dma("sp", cf[:, :], cf_d, writes=[cf_b])
    conv_order = PRE_ORDER + [i for i in MAIN_ORDER if i not in PRE_ORDER]

    def convert(l, lo, hi):
        for i in conv_order[lo:hi]:
            P.dma("pool", wB[l * NIMG + i], wsl[l * NIMG + i], writes=[wB_b[l * NIMG + i]])

    convert(0, 0, NIMG)
    for t in lrT:
        P.op("pool", lambda e, t=t: e.memset(t[:, :], 1.0), writes=lrT_b)

    sched = []
    for l in range(depth):
        sched += [l * NIMG + i for i in PRE_ORDER]
        nst = NTOK // 512
        for _ in range(nst):
            sched += [l * NIMG + i for i in MAIN_ORDER]
    wstate = dict(issued=0, used=0)

    def w_prefetch(upto):
        while wstate["issued"] < min(upto, len(sched)):
            n = wstate["issued"]
            r = n % NRING
            P.dma("sp", ring[r][:, :], wB[sched[n]], reads=[wB_b[sched[n]]], writes=[ring_b[r]])
            wstate["issued"] += 1

    def w_get(img, ncol=512, hold=0):
        n = wstate["used"]
        assert sched[n] == img, (n, sched[n], img)
        w_prefetch(n + NRING - hold)
        wstate["used"] += 1
        r = n % NRING
        return ring[r][:, :].rearrange("p (k n) -> p k n", n=ncol), ring_b[r]

    def load_layer_params(l):
        w = [lp_b]
        P.dma("sp", gT[:, :], ng_d[l], writes=[G_b])
        P.dma("sp", lnG[:, :], lng_d[l, :].partition_broadcast(128), writes=w)
        P.dma("sp", lnB[:, :], lnb_d[l, :].partition_broadcast(128), writes=w)
        P.dma("sp", bs_bc[:, :], bs_d[l, :].partition_broadcast(128), writes=w)
        P.dma("sp", sink_bc[:, :], sink_d[l, :].partition_broadcast(128), writes=w)
        P.dma("sp", cng[:, :], cng_d[l], writes=w)
        for i in range(2):
            P.dma("sp", wg[i][:, :], wg_d[l * 2 + i], writes=w)
        P.dma("pool", wlr[:, :, :], wlr_d[l].rearrange("p (k c) -> p k c", c=32), writes=w)
        P.dma("pool", wsT[:, :, :], wsT_d[l].rearrange("p (g i) -> p g i", i=128), writes=w)
        P.op("pool", lambda e: e.tensor_scalar(nsink[:, :], sink_bc[:, :], -1.0, None, ALU.mult), reads=w, writes=w)

    def rstd_from(ssum, scale, n):
        r, rb = statcol(n)
        P.op("act", lambda e: e.activation(r, ssum[0], AF.Ln, bias=eps_c, scale=scale), reads=[ssum[1], cf_b], writes=[rb])
        P.op("act", lambda e: e.activation(r, r, AF.Exp, scale=-0.5), reads=[rb], writes=[rb])
        return r, rb

    def make_hT_A(l, tile):
        xt, xtb = xst[0], xst_b[0]
        src = X[0] if l == 0 else X[1 + (l - 1) % 2]
        srcb = X_b[0] if l == 0 else X_b[1 + (l - 1) % 2]
        P.dma("sp", xt[:, :], src[tile * 128:(tile + 1) * 128, :], reads=[srcb[tile]], writes=[xtb])
        ss, ssb = statcol()
        P.op("act", lambda e: e.activation(hst[:, :], xt[:, :], AF.Square, accum_out=ss), reads=[xtb], writes=[hst_b, ssb])
        r, rb = rstd_from((ss, ssb), 1.0 / D, 1)
        P.op("dve", lambda e: e.tensor_scalar(hst[:, :], xt[:, :], r, None, ALU.mult), reads=[xtb, rb], writes=[hst_b])

    def make_hT_B(dst, dst_b, c0):
        for q in range(4):
            ps, pb = bank()
            P.op("pe", [mm(ps[:, kk * 128:(kk + 1) * 128], hst[:, (4 * q + kk) * 128:(4 * q + kk + 1) * 128], ident)
                        for kk in range(4)], reads=[hst_b, cbf_b], writes=[pb])
            for kk in range(4):
                k = 4 * q + kk
                d2 = dst[:, k, c0:c0 + 128]
                s2 = ps[:, kk * 128:(kk + 1) * 128]
                if kk % 2 == 0:
                    P.op("act", lambda e, d2=d2, s2=s2, k=k: e.activation(d2, s2, AF.Copy, scale=gT[:, k:k + 1]), reads=[pb, G_b], writes=[dst_b])
                else:
                    P.op("dve", lambda e, d2=d2, s2=s2, k=k: e.tensor_scalar(d2, s2, gT[:, k:k + 1], None, ALU.mult), reads=[pb, G_b], writes=[dst_b])

    def make_hT(l, tile, dst, dst_b, c0):
        make_hT_A(l, tile)
        make_hT_B(dst, dst_b, c0)

    def fm_block(wt, wb_, blk, K, rhs, rhs_b, ncols=512, rhs_c0=0):
        ps, pb = bank()
        P.op("pe", [mm(ps[:, 0:ncols], wt[:, k, blk * 128:(blk + 1) * 128], rhs[:, k, rhs_c0:rhs_c0 + ncols], k == 0, k == K - 1)
                    for k in range(K)], reads=[wb_, rhs_b], writes=[pb])
        return ps, pb

    def tm_group(lhs, lhs_b, c0, wt, wb_, wc0, ncols, K=16, reserve=False):
        ps, pb = bank(reserve=reserve)
        P.op("pe", [mm(ps[:, 0:ncols], lhs[:, k, c0:c0 + 128], wt[:, k, wc0:wc0 + ncols], k == 0, k == K - 1)
                    for k in range(K)], reads=[wb_, lhs_b], writes=[pb])
        return ps, pb

    def gate_L(d, lr_ap, lr_buf, Lout, Lout_b):
        ps, pb = bank()
        P.op("pe", mm(ps[:, :], lr_ap, wg[d][:, :]), reads=[lr_buf, lp_b], writes=[pb])
        P.op("act", lambda e: e.activation(Lout, ps[:, :], AF.Exp, scale=-1.0), reads=[pb], writes=[Lout_b])
        P.op("act", lambda e: e.activation(Lout, Lout, AF.Ln, bias=one_c, scale=1.0), reads=[Lout_b, cf_b], writes=[Lout_b])

    seq_tiles = []
    t0 = 0
    for s in seqs:
        seq_tiles.append((t0, s // 128))
        t0 += s // 128

    for l in range(depth):
        load_layer_params(l)
        cur = X[0] if l == 0 else X[1 + (l - 1) % 2]
        nxt_i = 1 + l % 2
        last = (l == depth - 1)
        wkc, wkc_b = w_get(l * NIMG + KC)
        wv0, wv0_b = w_get(l * NIMG + VC0, hold=1)
        wv1, wv1_b = w_get(l * NIMG + VC1, hold=2)
        for (tb, ntile) in seq_tiles:
            P.op("pool", lambda e: e.memset(Sb[:, :], 0.0), writes=[Sb_b])
            cur16 = 0
            P.op("pool", lambda e, c=cur16: e.memset(Sb16[c][:, :], 0.0), writes=[Sb16_b[cur16]])
            order = list(reversed(range(ntile)))
            PS = [dict() for _ in range(ntile)]
            c16 = [0]

            def PP0(i):
                make_hT(l, tb + order[i], hT, hT_b[i % 4], (i % 4) * 128)

            def PP1(i):
                par = i % 2
                hb_, c0 = hT_b[i % 4], (i % 4) * 128
                PS[i]["k"] = tm_group(hT, hb_, c0, wkc, wkc_b, 0, 512, reserve=True)
                vt = tC[par]; vt_b = tC_b[par]
                for half, (wv, wvb) in enumerate(((wv0, wv0_b), (wv1, wv1_b))):
                    ps, pb = tm_group(hT, hb_, c0, wv, wvb, 0, 512)
                    P.op("act", lambda e, ps=ps, half=half, vt=vt: e.copy(vt[:, half * 512:(half + 1) * 512], ps[:, :]), reads=[pb], writes=[vt_b])
                ps, pb = bank()
                P.op("pe", [mm(ps[0:16, 0:128], wlr[:, k, 16:32], hT[:, k, c0:c0 + 128], k == 0, k == 15) for k in range(16)],
                     reads=[lp_b, hb_], writes=[pb])
                P.op("dve", lambda e, ps=ps: e.tensor_copy(lrT[par][0:16, 0:128], ps[0:16, 0:128]), reads=[pb], writes=[lrT_b[par]])
                gate_L(1, lrT[par][:, 0:128], lrT_b[par], tB[:, par * 512:(par + 1) * 512], tBh_b[par])

            def PP2(i):
                par = i % 2
                tile = tb + order[i]
                Lb = tB[:, par * 512:(par + 1) * 512]
                vt = tC[par]; vt_b = tC_b[par]
                ps, pb = bank()
                P.op("pe", mm(ps[:, :], M2[1], Lb), reads=[cf_b, tBh_b[par]], writes=[pb])
                P.op("act", lambda e, ps=ps: e.activation(gEk[1][:, :], ps[:, :], AF.Exp), reads=[pb], writes=[gEk_b[1]])
                ps, pb = bank()
                P.op("pe", [mm(ps[:, h:h + 1], Lb[:, h * 128:(h + 1) * 128], negc) for h in range(4)], reads=[cf_b, tBh_b[par]], writes=[pb])
                P.op("act", lambda e, ps=ps: e.activation(dec[:, 8:12], ps[:, 0:4], AF.Exp), reads=[pb], writes=[dec_b])
                psk, pbk = PS[i]["k"]
                P.op("dve", lambda e: e.tensor_tensor(kd[1][:, :], psk[:, :], gEk[1][:, :], ALU.mult), reads=[pbk, gEk_b[1]], writes=[kd_b[1]])
                reserved.discard(ps_b.index(pbk))
                cur16 = c16[0]
                P.dma("pool", st_d[tile], Sb16[cur16][:, :], reads=[Sb16_b[cur16]], writes=[st_b[tile]])
                for hp in range(2):
                    ps, pb = bank()
                    P.op("pe", [mm(ps[:, hh * 256:(hh + 1) * 256], kd[1][:, (2 * hp + hh) * 128:(2 * hp + hh + 1) * 128],
                                   vt[:, (2 * hp + hh) * 256:(2 * hp + hh + 1) * 256]) for hh in range(2)],
                         reads=[kd_b[1], vt_b], writes=[pb])
                    for hh in range(2):
                        h = 2 * hp + hh
                        P.op("dve", lambda e, ps=ps, h=h, hh=hh: e.scalar_tensor_tensor(
                            Sb[:, h * 256:(h + 1) * 256], Sb[:, h * 256:(h + 1) * 256], dec[:, 8 + h:9 + h],
                            ps[:, hh * 256:(hh + 1) * 256], ALU.mult, ALU.add), reads=[pb, dec_b, Sb_b], writes=[Sb_b])
                c16[0] ^= 1
                nx = c16[0]
                P.op("pool", lambda e: e.tensor_copy(Sb16[nx][:, :], Sb[:, :]), reads=[Sb_b], writes=[Sb16_b[nx]])

            skewed(ntile, [PP0, PP1, PP2], [0, 1, 2])

        stc = [0]
        pre0 = [False]
        for (tb, ntile) in seq_tiles:
            nst = ntile // 4
            P.op("pool", lambda e: e.memset(Sf[:, :], 0.0), writes=[Sf_b])
            P.op("pool", lambda e: e.memset(Sf16[:, :], 0.0), writes=[Sf16_b])
            for st in range(nst):
                t_first = tb + st * 4
                pos0 = st * 512
                has_next = st < nst - 1
                has_prev = st > 0
                if l + 1 < depth:
                    per = -(-NIMG // (NTOK // 512))
                    convert(l + 1, stc[0] * per, (stc[0] + 1) * per)
                    stc[0] += 1
                def stage0_items(st_, l=l, tb=tb, nst=nst):
                    tf = tb + st_ * 4
                    hn = st_ < nst - 1
                    items = [((lambda j=j: make_hT_A(l, tf + j)), (lambda j=j: make_hT_B(hT, hT_b[j], j * 128))) for j in range(4)]
                    if hn:
                        items.append(((lambda: make_hT_A(l, tf + 4)), (lambda: make_hT_B(hTh, hTh_b, 0))))

                    def ropes():
                        nr = 640 if hn else 512
                        P.dma("sp", ropeC[:, 0:nr], ropeC_d[:, st_ * 512:st_ * 512 + nr], writes=[rope_b])
                        P.dma("sp", ropeS[:, 0:nr], ropeS_d[:, st_ * 512:st_ * 512 + nr], writes=[rope_b])
                    items.append((ropes, lambda: None))
                    return items

                if not pre0[0]:
                    for ia, ib in stage0_items(st):
                        ia(); ib()
                pre0[0] = False
                D0 = (l == 0 and st == 0 and tb == 0)
                if D0:
                    dump("hT", hT[:, :, :].rearrange("p a b -> p (a b)"), hT_b, 8192)
                wva = [w_get(l * NIMG + VA0), w_get(l * NIMG + VA1, hold=1)]
                S1 = [None] * 4

                def A1(j):
                    pss = [tm_group(hT, hT_b, j * 128, wva[h][0], wva[h][1], 0, 512) for h in range(2)]
                    bst, bstb = statcol(12)
                    for h in range(2):
                        P.op("dve", lambda e, h=h, ps=pss[h][0], bst=bst: e.bn_stats(bst[:, h * 6:(h + 1) * 6], ps[:, :]),
                             reads=[pss[h][1]], writes=[bstb])
                    mv, mvb = statcol(2)
                    P.op("dve", lambda e, bst=bst, mv=mv: e.bn_aggr(mv, bst), reads=[bstb], writes=[mvb])
                    r, rb = rstd_from((mv[:, 1:2], mvb), 1.0, 1)
                    nm, nmb = statcol()
                    P.op("dve", lambda e, nm=nm, mv=mv, r=r: e.scalar_tensor_tensor(nm, mv[:, 0:1], -1.0, r, ALU.mult, ALU.mult),
                         reads=[mvb, rb], writes=[nmb])
                    for h in range(2):
                        P.op("act", lambda e, h=h, ps=pss[h][0], r=r, nm=nm: e.activation(
                            tA[:, h * 512:(h + 1) * 512], ps[:, :], AF.Identity, bias=nm, scale=r),
                            reads=[pss[h][1], rb, nmb], writes=[tA_b])
                    P.op("dve", lambda e: e.tensor_tensor(tA[:, :], tA[:, :], lnG[:, :], ALU.mult), reads=[tA_b, lp_b], writes=[tA_b])
                    vn = tC[j % 2]; vnb = tC_b[j % 2]
                    P.op("dve", lambda e, vn=vn: e.tensor_tensor(vn[:, :], tA[:, :], lnB[:, :], ALU.add), reads=[tA_b, lp_b], writes=[vnb])
                    S1[j] = (vn, vnb)

                def A2(j):
                    vn, vnb = S1[j]
                    for hb in range(2):
                        ps, pb = bank()
                        P.op("pe", [mm(ps[:, c * 128:(c + 1) * 128], vn[:, (4 * hb + c) * 128:(4 * hb + c + 1) * 128],
                                       wsT[:, (4 * hb + c) // 2, :]) for c in range(4)], reads=[vnb, lp_b], writes=[pb])
                        for gg in range(2):
                            g = hb * 2 + gg
                            P.op("dve", lambda e, ps=ps, gg=gg, g=g, hb=hb: e.tensor_tensor(
                                yT[:, 4 * hb + 2 * gg:4 * hb + 2 * gg + 2, j * 128:(j + 1) * 128],
                                ps[:, gg * 256:(gg + 1) * 256].rearrange("p (a b) -> p a b", b=128),
                                bs_bc[:, g * 128:(g + 1) * 128].unsqueeze(1).to_broadcast([128, 2, 128]), ALU.add),
                                reads=[pb, lp_b], writes=[yT_b[j]])

                skewed(4, [A1, A2], [0, 1])
                for half, img in enumerate((UA0, UA1)):
                    wt, wb_ = w_get(l * NIMG + img)
                    for bl in range(4):
                        ps, pb = fm_block(wt, wb_, bl, 16, hT, hT_b)
                        P.op("act", lambda e, ps=ps, b=half * 4 + bl: e.copy(uT[:, b, :], ps[:, :]), reads=[pb], writes=[uT_b])
                zc = 0
                for half, img in enumerate((ZA0, ZA1)):
                    wt, wb_ = w_get(l * NIMG + img)
                    for bl in range(4):
                        b = half * 4 + bl
                        ps, pb = fm_block(wt, wb_, bl, 16, hT, hT_b)
                        zz = zc % 2; zc += 1
                        P.op("act", lambda e, ps=ps, zz=zz: e.activation(szt[zz][:, :], ps[:, :], AF.Silu), reads=[pb], writes=[szt_b[zz]])
                        P.op("dve", lambda e, b=b, zz=zz: e.tensor_tensor(uT[:, b, :], uT[:, b, :], szt[zz][:, :], ALU.mult),
                             reads=[uT_b, szt_b[zz]], writes=[uT_b])
                        P.op("dve", lambda e, b=b: e.tensor_tensor(yT[:, b, :], yT[:, b, :], uT[:, b, :], ALU.mult),
                             reads=[uT_b] + yT_b, writes=yT_b)
                if D0:
                    dump("ya", yT[:, :, :].rearrange("p a b -> p (a b)"), yT_b, 4096)
                merge_term(P, l, 0, w_get, fm_block, yT, yT_b, hT, hT_b, merged, mrg_b, bank, (PA0, GA0, GA1, PA1, GA2, GA3), tB, tBh_b)

                if has_prev:
                    P.op("pool", lambda e: e.tensor_copy(krT[:, :, 0:128], krT[:, :, 512:640]), reads=[krT_b[4]], writes=[krT_b[0]])
                    P.op("pool", lambda e: e.tensor_copy(vtB[:, 0, :], vtB[:, 4, :]), reads=[vtB_b[4]], writes=[vtB_b[0]])
                else:
                    P.op("pool", lambda e: e.memset(krT[:, :, 0:128], 0.0), writes=[krT_b[0]])
                    P.op("pool", lambda e: e.memset(vtB[:, 0, :], 0.0), writes=[vtB_b[0]])
                if not has_next:
                    P.op("pool", lambda e: e.memset(krT[:, :, 640:768], 0.0), writes=[krT_b[5]])
                    P.op("pool", lambda e: e.memset(vtB[:, 5, :], 0.0), writes=[vtB_b[5]])

                rctr = [0]

                def rope_evac(ps, pb, ncols, rc0, dst, dst_bufs):
                    if rctr[0] % 2 == 0:
                        T, Tb = tA, [tA_b]
                    else:
                        T, Tb = tB, tBh_b
                    rctr[0] += 1
                    P.op("dve", lambda e: e.tensor_tensor(T[:, 0:ncols], ps[:, 0:ncols], ropeC[:, rc0:rc0 + ncols], ALU.mult),
                         reads=[pb, rope_b], writes=Tb)
                    P.op("dve", lambda e: e.tensor_tensor(T[0:64, 512:512 + ncols], ps[64:128, 0:ncols], ropeS[64:128, rc0:rc0 + ncols], ALU.mult),
                         reads=[pb, rope_b], writes=Tb)
                    P.op("dve", lambda e: e.tensor_tensor(T[64:128, 512:512 + ncols], ps[0:64, 0:ncols], ropeS[0:64, rc0:rc0 + ncols], ALU.mult),
                         reads=[pb, rope_b], writes=Tb)
                    P.op("dve", lambda e: e.tensor_tensor(dst, T[:, 0:ncols], T[:, 512:512 + ncols], ALU.add), reads=Tb, writes=dst_bufs)

                for half, img in enumerate((QB0, QB1)):
                    wt, wb_ = w_get(l * NIMG + img)
                    for bl in range(4):
                        ps, pb = fm_block(wt, wb_, bl, 16, hT, hT_b)
                        rope_evac(ps, pb, 512, 0, uT[:, half * 4 + bl, :], [uT_b])
                wt, wb_ = w_get(l * NIMG + KVB)
                for kv in range(2):
                    ps, pb = fm_block(wt, wb_, kv, 16, hT, hT_b)
                    rope_evac(ps, pb, 512, 0, krT[:, kv, 128:640], krT_b[1:5])
                    if has_next:
                        ps, pb = fm_block(wt, wb_, kv, 16, hTh, hTh_b, ncols=128)
                        rope_evac(ps, pb, 128, 512, krT[:, kv, 640:768], [krT_b[5]])
                for j in range(5 if has_next else 4):
                    src, srcb, c0 = (hT, hT_b, j * 128) if j < 4 else (hTh, hTh_b, 0)
                    ps, pb = tm_group(src, srcb, c0, wt, wb_, 256, 256)
                    P.op("act", lambda e, ps=ps, j=j: e.copy(vtB[:, j + 1, :], ps[:, 0:256]), reads=[pb], writes=[vtB_b[j + 1]])
                scale = 128.0 ** -0.5
                S = [dict() for _ in range(32)]

                def stA(i):
                    j, h = divmod(i, 8)
                    kv = h // 4
                    tl = st * 4 + j
                    mk = masks[1] if tl == 0 else (masks[2] if tl == ntile - 1 else masks[0])
                    ps, pb = bank()
                    S[i]["ps"], S[i]["pb"] = ps, pb
                    P.op("pe", [mm(ps[:, 0:384], ident, mk, True, False),
                                mm(ps[:, 0:384], uT[:, h, j * 128:(j + 1) * 128], krT[:, kv, j * 128:j * 128 + 384], False, True)],
                         reads=[cbf_b, uT_b] + krT_b[j:j + 3], writes=[pb])

                def stB(i):
                    j, h = divmod(i, 8)
                    ps, pb = S[i]["ps"], S[i]["pb"]
                    pp = i % 4
                    mx, mxb = statcol(4)
                    P.op("dve", lambda e: e.reduce_max(mx[:, 0:1], ps[:, 0:384], AX.X), reads=[pb], writes=[mxb])
                    P.op("dve", lambda e: e.tensor_scalar(mx[:, 1:2], mx[:, 0:1], -scale, nsink[:, h:h + 1], ALU.mult, ALU.min),
                         reads=[mxb, lp_b], writes=[mxb])
                    P.op("act", lambda e: e.activation(pbuf[pp][:, :], ps[:, 0:384], AF.Exp, bias=mx[:, 1:2], scale=scale,
                                                       accum_out=mx[:, 2:3]), reads=[pb, mxb], writes=[pbuf_b[pp], mxb])
                    P.op("act", lambda e: e.activation(mx[:, 3:4], mx[:, 1:2], AF.Exp, bias=sink_bc[:, h:h + 1], scale=1.0),
                         reads=[mxb, lp_b], writes=[mxb])
                    P.op("dve", lambda e: e.tensor_tensor(mx[:, 2:3], mx[:, 2:3], mx[:, 3:4], ALU.add), reads=[mxb], writes=[mxb])
                    P.op("dve", lambda e: e.reciprocal(mx[:, 0:1], mx[:, 2:3]), reads=[mxb], writes=[mxb])
                    if i % 2:
                        P.op("pool", lambda e: e.tensor_scalar(Dg[pp][:, :], ident, mx[:, 0:1], None, ALU.mult),
                             reads=[mxb, cbf_b], writes=[Dg_b[pp]])
                    else:
                        P.op("act", lambda e: e.activation(Dg[pp][:, :], ident, AF.Copy, scale=mx[:, 0:1]),
                             reads=[mxb, cbf_b], writes=[Dg_b[pp]])

                def stC(i):
                    pp = i % 4
                    ps2, pb2 = bank()
                    S[i]["ps2"], S[i]["pb2"] = ps2, pb2
                    P.op("pe", [mm(ps2[:, c * 128:(c + 1) * 128], pbuf[pp][:, c * 128:(c + 1) * 128], Dg[pp][:, :]) for c in range(3)],
                         reads=[pbuf_b[pp], Dg_b[pp]], writes=[pb2])

                def stD(i):
                    ps2, pb2 = S[i]["ps2"], S[i]["pb2"]
                    if i % 2:
                        P.op("act", lambda e: e.copy(pT[i % 2][:, :], ps2[:, 0:384]), reads=[pb2], writes=[pT_b[i % 2]])
                    else:
                        P.op("dve", lambda e: e.tensor_copy(pT[i % 2][:, :], ps2[:, 0:384]), reads=[pb2], writes=[pT_b[i % 2]])

                def stE(i):
                    j, h = divmod(i, 8)
                    kv = h // 4
                    ob, obb = bank()
                    P.op("pe", [mm(ob[:, 0:128], vtB[:, j + c, kv * 128:(kv + 1) * 128],
                                   pT[i % 2][:, c * 128:(c + 1) * 128], c == 0, c == 2) for c in range(3)],
                         reads=[pT_b[i % 2]] + vtB_b[j:j + 3], writes=[obb])
                    P.op("dve", lambda e: e.tensor_copy(yT[:, h, j * 128:(j + 1) * 128], ob[:, 0:128]), reads=[obb], writes=[yT_b[j]])

                skewed(32, [stA, stB, stC, stD, stE], [0, 0, 3, 3, 4])
                zc = 0
                for half, img in enumerate((ZB0, ZB1)):
                    wt, wb_ = w_get(l * NIMG + img)
                    for bl in range(4):
                        b = half * 4 + bl
                        ps, pb = fm_block(wt, wb_, bl, 16, hT, hT_b)
                        zz = zc % 2; zc += 1
                        P.op("act", lambda e, ps=ps, zz=zz: e.activation(szt[zz][:, :], ps[:, :], AF.Silu), reads=[pb], writes=[szt_b[zz]])
                        P.op("dve", lambda e, b=b, zz=zz: e.tensor_tensor(yT[:, b, :], yT[:, b, :], szt[zz][:, :], ALU.mult),
                             reads=yT_b + [szt_b[zz]], writes=yT_b)
                if D0:
                    dump("m0", merged[:, :, :].rearrange("p a b -> p (a b)"), mrg_b, 8192)
                    dump("yb", yT[:, :, :].rearrange("p a b -> p (a b)"), yT_b, 4096)
                merge_term(P, l, 1, w_get, fm_block, yT, yT_b, hT, hT_b, merged, mrg_b, bank, (PB0, GB0, GB1, PB1, GB2, GB3), tB, tBh_b)

                for half in range(2):
                    wvh = w_get(l * NIMG + VC0 + half)
                    for j in range(4):
                        ps, pb = tm_group(hT, hT_b, j * 128, wvh[0], wvh[1], 0, 512)
                        P.op("act", lambda e, ps=ps, j=j, half=half: e.copy(vtC[:, j, half * 512:(half + 1) * 512], ps[:, :]),
                             reads=[pb], writes=[uT_b])
                for d in range(2):
                    ps, pb = bank()
                    P.op("pe", [mm(ps[0:16, :], wlr[:, k, d * 16:(d + 1) * 16], hT[:, k, :], k == 0, k == 15) for k in range(16)],
                         reads=[lp_b, hT_b], writes=[pb])
                    P.op("dve", lambda e, ps=ps, d=d: e.tensor_copy(lrT[d][0:16, :], ps[0:16, :]), reads=[pb], writes=[lrT_b[d]])
                wq, wq_b = w_get(l * NIMG + QC)
                wk, wk_b = w_get(l * NIMG + KC, hold=1)
                qscale = 128.0 ** -0.5
                def G1(j):
                    tile = t_first + j
                    par = j % 2
                    cs = slice(j * 128, (j + 1) * 128)
                    P.dma("sp", Sb16[par][:, :], st_d[tile], reads=[st_b[tile]], writes=[Sb16_b[par]])
                    for d in range(2):
                        gate_L(d, lrT[d][:, cs], lrT_b[d], tB[:, d * 512:(d + 1) * 512], tBh_b[d])
                    psq, pbq = bank()
                    P.op("pe", [mm(psq[:, h * 128:(h + 1) * 128], wq[:, k, h * 128:(h + 1) * 128], hT[:, k, cs], k == 0, k == 15)
                                for h in range(4) for k in range(16)], reads=[wq_b, hT_b], writes=[pbq])
                    psk, pbk = bank()
                    P.op("pe", [mm(psk[:, h * 128:(h + 1) * 128], wk[:, k, h * 128:(h + 1) * 128], hT[:, k, cs], k == 0, k == 15)
                                for h in range(4) for k in range(16)], reads=[wk_b, hT_b], writes=[pbk])
                    for d in range(2):
                        L = tB[:, d * 512:(d + 1) * 512]
                        qd, qdb = qe[2 * par + d], qe_b[2 * par + d]
                        ad, adb = Am[2 * par + d], Am_b[2 * par + d]
                        ps, pb = bank()
                        P.op("pe", [mm(ps[:, h * 128:(h + 1) * 128], L[:, h * 128:(h + 1) * 128], M1[d]) for h in range(4)],
                             reads=[tBh_b[d], cf_b], writes=[pb])
                        P.op("act", lambda e, ps=ps, d=d: e.activation(gE[d][:, :], ps[:, :], AF.Exp), reads=[pb], writes=[gE_b[d]])
                        P.op("act", lambda e, ps=ps, d=d: e.activation(gEi[d][:, :], ps[:, :], AF.Exp, scale=-1.0), reads=[pb], writes=[gEi_b[d]])
                        if d == 0:
                            P.op("act", lambda e, ps=ps: e.activation(dec[:, 4 * par:4 * par + 4], ps[:, :].rearrange("p (h c) -> p h c", c=128)[:, :, 127], AF.Exp),
                                 reads=[pb], writes=[decm_b[par]])
                        P.op("dve", lambda e, d=d, qd=qd: e.scalar_tensor_tensor(qd[:, :], psq[:, :], qscale, gE[d][:, :], ALU.mult, ALU.mult),
                             reads=[pbq, gE_b[d]], writes=[qdb])
                        P.op("dve", lambda e, d=d: e.tensor_tensor(ke[d][:, :], psk[:, :], gEi[d][:, :], ALU.mult),
                             reads=[pbk, gEi_b[d]], writes=[ke_b[d]])
                        ps, pb = bank()
                        P.op("pe", [mm(ps[:, h * 128:(h + 1) * 128], ke[d][:, h * 128:(h + 1) * 128], qd[:, h * 128:(h + 1) * 128]) for h in range(4)],
                             reads=[ke_b[d], qdb], writes=[pb])
                        P.op("dve", lambda e, ps=ps, d=d, ad=ad: e.tensor_tensor(ad[:, :], ps[:, :], gmask[d], ALU.mult), reads=[pb, cbf_b], writes=[adb])
                    ps, pb = bank()
                    P.op("pe", mm(ps[:, :], M2[0], tB[:, 0:512]), reads=[tBh_b[0], cf_b], writes=[pb])
                    P.op("act", lambda e, ps=ps: e.activation(gEk[0][:, :], ps[:, :], AF.Exp), reads=[pb], writes=[gEk_b[0]])
                    ps, pb = tm_group(hT, hT_b, j * 128, wk, wk_b, 0, 512)
                    P.op("dve", lambda e, ps=ps: e.tensor_tensor(kd[par][:, :], ps[:, :], gEk[0][:, :], ALU.mult), reads=[pb, gEk_b[0]], writes=[kd_b[par]])

                def G2(j):
                    par = j % 2
                    obanks = [bank(), bank()]
                    for hp in range(2):
                        ob, obb = obanks[hp]
                        fns = []
                        for hh in range(2):
                            h = 2 * hp + hh
                            for vb in range(2):
                                oc = ob[:, (hh * 2 + vb) * 128:(hh * 2 + vb + 1) * 128]
                                vsl = slice(h * 256 + vb * 128, h * 256 + (vb + 1) * 128)
                                hs = slice(h * 128, (h + 1) * 128)
                                fns.append(mm(oc, vtC[:, j, vsl], Am[2 * par][:, hs], True, False))
                                fns.append(mm(oc, vtC[:, j, vsl], Am[2 * par + 1][:, hs], False, False))
                                fns.append(mm(oc, Sf16[:, vsl], qe[2 * par][:, hs], False, False))
                                fns.append(mm(oc, Sb16[par][:, vsl], qe[2 * par + 1][:, hs], False, True))
                        P.op("pe", fns, reads=[uT_b, Am_b[2 * par], Am_b[2 * par + 1], Sf16_b, Sb16_b[par], qe_b[2 * par], qe_b[2 * par + 1]], writes=[obb])
                    for hp in range(2):
                        ps, pb = bank()
                        P.op("pe", [mm(ps[:, hh * 256:(hh + 1) * 256], kd[par][:, (2 * hp + hh) * 128:(2 * hp + hh + 1) * 128],
                                       vtC[:, j, (2 * hp + hh) * 256:(2 * hp + hh + 1) * 256]) for hh in range(2)],
                             reads=[kd_b[par], uT_b], writes=[pb])
                        for hh in range(2):
                            h = 2 * hp + hh
                            P.op("dve", lambda e, ps=ps, h=h, hh=hh: e.scalar_tensor_tensor(
                                Sf[:, h * 256:(h + 1) * 256], Sf[:, h * 256:(h + 1) * 256], dec[:, 4 * par + h:4 * par + h + 1],
                                ps[:, hh * 256:(hh + 1) * 256], ALU.mult, ALU.add), reads=[pb, decm_b[par], Sf_b], writes=[Sf_b])
                    P.op("pool", lambda e: e.tensor_copy(Sf16[:, :], Sf[:, :]), reads=[Sf_b], writes=[Sf16_b])
                    for hp in range(2):
                        ob, obb = obanks[hp]
                        P.op("act", lambda e, ob=ob, hp=hp: e.activation(sq[:, hp * 512:(hp + 1) * 512], ob[:, :], AF.Square), reads=[obb], writes=[sq_b])
                    ps, pb = bank()
                    P.op("pe", [mm(ps[:, h * 128:(h + 1) * 128], ones_bf, sq[:, (h * 2 + vb) * 128:(h * 2 + vb + 1) * 128], vb == 0, vb == 1)
                                for h in range(4) for vb in range(2)], reads=[sq_b, cbf_b], writes=[pb])
                    P.op("act", lambda e, ps=ps: e.activation(rstd[:, :], ps[:, :], AF.Ln, bias=eps_c, scale=1.0 / 256), reads=[pb, cf_b], writes=[rstd_b])
                    P.op("act", lambda e: e.activation(rstd[:, :], rstd[:, :], AF.Exp, scale=-0.5), reads=[rstd_b], writes=[rstd_b])
                    for hp in range(2):
                        ob, obb = obanks[hp]
                        for vb in range(2):
                            P.op("dve", lambda e, ob=ob, hp=hp, vb=vb, j=j: e.scalar_tensor_tensor(
                                yT[:, 4 * hp:4 * hp + 4, j * 128:(j + 1) * 128].rearrange("p (h v) c -> p h v c", v=2)[:, :, vb, :],
                                ob[:, :].rearrange("p (h v c) -> p h v c", v=2, c=128)[:, :, vb, :], cng[:, vb:vb + 1],
                                rstd[:, hp * 256:(hp + 1) * 256].rearrange("p (h c) -> p h c", c=128), ALU.mult, ALU.mult),
                                reads=[obb, rstd_b, lp_b], writes=[yT_b[j]])

                skewed(4, [G1, G2], [0, 1])
                zc = 0
                for half, img in enumerate((ZC0, ZC1)):
                    wt, wb_ = w_get(l * NIMG + img)
                    for bl in range(4):
                        b = half * 4 + bl
                        ps, pb = fm_block(wt, wb_, bl, 16, hT, hT_b)
                        zz = zc % 2; zc += 1
                        P.op("act", lambda e, ps=ps, zz=zz: e.activation(szt[zz][:, :], ps[:, :], AF.Silu), reads=[pb], writes=[szt_b[zz]])
                        P.op("dve", lambda e, b=b, zz=zz: e.tensor_tensor(yT[:, b, :], yT[:, b, :], szt[zz][:, :], ALU.mult),
                             reads=yT_b + [szt_b[zz]], writes=yT_b)
                if D0:
                    dump("yc", yT[:, :, :].rearrange("p a b -> p (a b)"), yT_b, 4096)
                merge_term(P, l, 2, w_get, fm_block, yT, yT_b, hT, hT_b, merged, mrg_b, bank, (PC0, GC0, GC1, PC1, GC2, GC3), tB, tBh_b)

                if D0:
                    dump("m2", merged[:, :, :].rearrange("p a b -> p (a b)"), mrg_b, 8192)
                srcb = X_b[0] if l == 0 else X_b[1 + (l - 1) % 2]
                xc = [0]
                s5 = []
                for c in range(4):
                    def getw(c=c):
                        s5w[0] = w_get(l * NIMG + WO0 + c)
                    for j in range(4):
                        def grp(c=c, j=j, getw=getw):
                            if j == 0:
                                getw()
                            wo, wo_b = s5w[0]
                            tile = t_first + j
                            rows = slice(tile * 128, (tile + 1) * 128)
                            cols = slice(c * 512, (c + 1) * 512)
                            xx = xc[0] % 2; xc[0] += 1
                            P.dma("sp", xch[xx][:, :], cur[rows, cols], reads=[srcb[tile]], writes=[xch_b[xx]])
                            ps, pb = bank()
                            P.op("pe", [mm(ps[:, :], merged[:, k, j * 128:(j + 1) * 128], wo[:, k, :], k == 0, k == 15) for k in range(16)],
                                 reads=[wo_b] + mrg_b, writes=[pb])
                            P.op("dve", lambda e: e.tensor_tensor(xch[xx][:, :], ps[:, :], xch[xx][:, :], ALU.add),
                                 reads=[pb, xch_b[xx]], writes=[xch_b[xx]])
                            P.dma("pool", X[nxt_i][rows, cols], xch[xx][:, :], reads=[xch_b[xx]], writes=[X_b[nxt_i][tile]])
                        s5.append(grp)
                s5w = [None]
                if has_next:
                    nxt0 = stage0_items(st + 1)
                    have_nxt = True
                else:
                    si = [i for i, (tb_, nt_) in enumerate(seq_tiles) if tb_ == tb][0]
                    if si + 1 < len(seq_tiles):
                        ntb, nnt = seq_tiles[si + 1]
                        nxt0 = stage0_items(0, tb=ntb, nst=nnt // 4)
                        have_nxt = True
                    else:
                        nxt0 = []
                        have_nxt = False
                gi = 0
                if nxt0:
                    nxt0[0][0]()
                for ii, (ia, ib) in enumerate(nxt0):
                    for _ in range(3):
                        if gi < len(s5):
                            s5[gi](); gi += 1
                    ib()
                    if ii + 1 < len(nxt0):
                        nxt0[ii + 1][0]()
                while gi < len(s5):
                    s5[gi](); gi += 1
                pre0[0] = have_nxt

    fin = X[1 + (depth - 1) % 2]
    fin_b = X_b[1 + (depth - 1) % 2]
    P.dma("sp", tA[:, :], fg_d[0, 0:1024].partition_broadcast(128), writes=[tA_b])
    P.dma("sp", tB[:, :], fg_d[0, 1024:2048].partition_broadcast(128), writes=tBh_b)
    for tile in range(NT):
        p = tile % 2
        xt, xtb = xst[p], xst_b[p]
        rows = slice(tile * 128, (tile + 1) * 128)
        P.dma("sp", xt[:, :], fin[rows, :], reads=[fin_b[tile]], writes=[xtb])
        ss, ssb = statcol()
        P.op("act", lambda e, xt=xt, ss=ss: e.activation(hst[:, :], xt[:, :], AF.Square, accum_out=ss), reads=[xtb], writes=[hst_b, ssb])
        r, rb = rstd_from((ss, ssb), 1.0 / D, 1)
        P.op("dve", lambda e, xt=xt, r=r: e.scalar_tensor_tensor(xt[:, 0:1024], xt[:, 0:1024], r, tA[:, :], ALU.mult, ALU.mult),
             reads=[xtb, rb, tA_b], writes=[xtb])
        P.op("dve", lambda e, xt=xt, r=r: e.scalar_tensor_tensor(xt[:, 1024:2048], xt[:, 1024:2048], r, tB[:, :], ALU.mult, ALU.mult),
             reads=[xtb, rb] + tBh_b, writes=[xtb])
        P.dma("pool", y_out[rows, :], xt[:, :], reads=[xtb], writes=[Y_b[tile]])
    P.finish()
    P.run_block()
    es.close()
    return nc


def merge_term(P, l, which, w_get, fm_block, yT, yT_b, hT, hT_b, merged, mrg_b, bank, imgs, tB, tBh_b):
    PA, G0, G1, PB_, G2, G3 = imgs
    order = [(PA, (G0, G1)), (PB_, (G2, G3))]
    for half, (pimg, gimgs) in enumerate(order):
        for gi, gimg in enumerate(gimgs):
            if gi == 0:
                wg3, wg_b = w_get(l * NIMG + gimg)
                wp3, wp_b = w_get(l * NIMG + pimg, 1024, hold=1)
            else:
                wg3, wg_b = w_get(l * NIMG + gimg, hold=1)
            for bl in range(4):
                m = half * 8 + gi * 4 + bl
                psg, pgb = fm_block(wg3, wg_b, bl, 16, hT, hT_b)
                sg = tB[:, (m % 2) * 512:(m % 2 + 1) * 512]
                sgb = tBh_b[m % 2]
                P.op("act", lambda e, psg=psg, sg=sg: e.activation(sg, psg[:, :], AF.Sigmoid), reads=[pgb], writes=[sgb])
                psp, ppb = bank()
                P.op("pe", [mm(psp[:, :], wp3[:, k, (gi * 4 + bl) * 128:(gi * 4 + bl + 1) * 128], yT[:, k, :], k == 0, k == 7) for k in range(8)],
                     reads=[wp_b] + yT_b, writes=[ppb])
                if which == 0:
                    P.op("dve", lambda e, psp=psp, sg=sg, m=m: e.tensor_tensor(merged[:, m, :], psp[:, :], sg, ALU.mult),
                         reads=[ppb, sgb], writes=[mrg_b[m]])
                else:
                    P.op("dve", lambda e, psp=psp, sg=sg: e.tensor_tensor(sg, psp[:, :], sg, ALU.mult), reads=[ppb, sgb], writes=[sgb])
                    P.op("pool", lambda e, sg=sg, m=m: e.tensor_tensor(merged[:, m, :], merged[:, m, :], sg, ALU.add),
                         reads=[sgb, mrg_b[m]], writes=[mrg_b[m]])


def _img_in(w, c0, n=512):
    a = w[:, c0:c0 + n].reshape(16, 128, n).transpose(1, 0, 2)
    return a


def _pack_layer(w_in, w_pa, w_pb, w_pc, w_out):
    imgs = np.empty((NIMG, 128, 8192), np.float32)

    def put_in(i, c0):
        imgs[i] = _img_in(w_in, c0).reshape(128, 8192)

    def put_proj(i, wp, c0):
        imgs[i] = wp[:, c0:c0 + 1024].reshape(8, 128, 1024).transpose(1, 0, 2).reshape(128, 8192)

    put_in(VA0, OFF["va"]); put_in(VA1, OFF["va"] + 512)
    put_in(UA0, OFF["ua"]); put_in(UA1, OFF["ua"] + 512)
    put_in(ZA0, OFF["za"]); put_in(ZA1, OFF["za"] + 512)
    put_in(QB0, OFF["qb"]); put_in(QB1, OFF["qb"] + 512)
    put_in(KVB, OFF["kb"])
    put_in(ZB0, OFF["zb"]); put_in(ZB1, OFF["zb"] + 512)
    put_in(QC, OFF["qc"]); put_in(KC, OFF["kc"])
    put_in(VC0, OFF["vc"]); put_in(VC1, OFF["vc"] + 512)
    put_in(ZC0, OFF["zc"]); put_in(ZC1, OFF["zc"] + 512)
    for name, ids in (("ga", (GA0, GA1, GA2, GA3)), ("gb", (GB0, GB1, GB2, GB3)), ("gc", (GC0, GC1, GC2, GC3))):
        for i, img in enumerate(ids):
            put_in(img, OFF[name] + 512 * i)
    put_proj(PA0, w_pa, 0); put_proj(PA1, w_pa, 1024)
    put_proj(PB0, w_pb, 0); put_proj(PB1, w_pb, 1024)
    put_proj(PC0, w_pc, 0); put_proj(PC1, w_pc, 1024)
    for c in range(4):
        imgs[WO0 + c] = w_out[:, c * 512:(c + 1) * 512].reshape(16, 128, 512).transpose(1, 0, 2).reshape(128, 8192)
    return imgs


def _consts(smax):
    i = np.arange(128)[:, None]
    j = np.arange(384)[None, :]
    ok = (j >= i) & (j <= i + 256)
    m0 = np.where(ok, 0.0, NEG)
    m1 = np.where(ok & (j >= 128), 0.0, NEG)
    m2 = np.where(ok & (j < 256), 0.0, NEG)
    mm_, cc = np.arange(128)[:, None], np.arange(128)[None, :]
    gmf = np.tile((mm_ <= cc).astype(np.float32), (1, 4))
    gmb = np.tile((mm_ > cc).astype(np.float32), (1, 4))
    cbf = np.concatenate([np.eye(128), m0, m1, m2, gmf, gmb, np.ones((128, 128))], axis=1).astype(np.float32)
    s = -1.0 / 16.0
    M1f = (mm_ <= cc) * s
    M1b = (mm_ >= cc) * s
    M2f = (mm_ > cc) * s
    M2b = (mm_ < cc) * s
    cf = np.concatenate([M1f, M1b, M2f, M2b, np.full((128, 1), s), np.ones((128, 1)), np.full((128, 1), EPS),
                         np.zeros((128, 1))], axis=1).astype(np.float32)
    half = 64
    inv = (10000.0 ** (-np.arange(half, dtype=np.float32) * 2.0 / 128)).astype(np.float32)
    ang = np.arange(smax, dtype=np.float32)[None, :] * inv[:, None]
    cos = np.cos(ang).astype(np.float32)
    sin = np.sin(ang).astype(np.float32)
    ropeC = np.concatenate([cos, cos], axis=0)
    ropeS = np.concatenate([sin, -sin], axis=0)
    return cbf, cf, np.ascontiguousarray(ropeC), np.ascontiguousarray(ropeS)


def _run(xs_per_core, seqs, depth, norm_g, w_in, a_ln_g, a_ln_b, a_ws, a_bs, b_sink, c_wf, c_bf, c_wb, c_bb,
         c_norm_g, w_pa, w_pb, w_pc, w_out, final_g):
    smax = max(seqs)
    f = lambda a: np.ascontiguousarray(np.asarray(a, dtype=np.float32))
    w_in, w_pa, w_pb, w_pc, w_out = map(f, (w_in, w_pa, w_pb, w_pc, w_out))
    wsl = np.concatenate([_pack_layer(w_in[l], w_pa[l], w_pb[l], w_pc[l], w_out[l]) for l in range(depth)], axis=0)
    wlr = np.stack([_img_in(w_in[l], OFF["lrf"], 32).reshape(128, 512) for l in range(depth)])
    wg = np.stack([np.concatenate([f(w)[l], f(b)[l][None, :]], axis=0) for l in range(depth) for (w, b) in ((c_wf, c_bf), (c_wb, c_bb))])
    wsT = np.stack([f(a_ws)[l].transpose(2, 0, 1).reshape(128, 512) for l in range(depth)])
    cbf, cf, ropeC, ropeS = _consts(smax)
    common = dict(wsl=wsl, wlr=f(wlr), wg=f(wg), wsT=f(wsT), norm_gT=np.ascontiguousarray(f(norm_g)[:depth].reshape(depth, 16, 128).transpose(0, 2, 1)), a_ln_g=f(a_ln_g)[:depth],
                  a_ln_b=f(a_ln_b)[:depth], a_bs=f(a_bs)[:depth].reshape(depth, 512), b_sink=f(b_sink)[:depth],
                  c_norm_gT=np.ascontiguousarray(f(c_norm_g)[:depth].reshape(depth, 2, 128).transpose(0, 2, 1)), final_g=f(final_g).reshape(1, D), cbf=cbf, cf=cf, ropeC=ropeC, ropeS=ropeS)
    nc = build(seqs, depth, smax, dbg=_DBGFLAG[0])
    in_maps = [dict(common, x=f(x)) for x in xs_per_core]
    res = run_bass_kernel_spmd(nc, in_maps, core_ids=list(range(len(in_maps))))
    if _DBGFLAG[0]:
        _DBGOUT.append(res.results[0])
    return [r["y"] for r in res.results]


_DBGFLAG = [False]
_DBGOUT = []


def kernel(x_prompt, x_sample, norm_g, w_in, a_ln_g, a_ln_b, a_ws, a_bs, b_sink, c_wf, c_bf, c_wb, c_bb,
           c_norm_g, w_pa, w_pb, w_pc, w_out, final_g):
    x_prompt = np.asarray(x_prompt, dtype=np.float32)
    x_sample = np.asarray(x_sample, dtype=np.float32)
    SP, SS = x_prompt.shape[1], x_sample.shape[1]
    xs = [np.concatenate([x_sample[c], x_prompt[c % 4]], axis=0) for c in range(8)]
    ys = _run(xs, [SS, SP], 4, norm_g, w_in, a_ln_g, a_ln_b, a_ws, a_bs, b_sink, c_wf, c_bf, c_wb, c_bb,
              c_norm_g, w_pa, w_pb, w_pc, w_out, final_g)
    y_sample = np.stack([ys[c][:SS] for c in range(8)])
    y_prompt = np.stack([ys[c][SS:SS + SP] for c in range(4)])
    return (y_prompt, y_sample)
```
